# Optimizing a Trainium2 kernel written in Bass

```python
import math
import jax
import jax.numpy as jnp
from jax import lax
import numpy as np

D_MODEL = 1024
BATCH = 8
SEQ = 2048
DEPTH = 4

EXPAND = 2
D_INNER = EXPAND * D_MODEL
N_MIXERS = 4
D_BRANCH = D_INNER // N_MIXERS
CHUNK = 128
CONV_K = 4
NORM_EPS = 1e-6

SSD_HEAD_DIM = 64
SSD_HEADS = D_BRANCH // SSD_HEAD_DIM
SSD_GROUPS = 2
SSD_STATE = 128
SSD_CONV_DIM = D_BRANCH + 2 * SSD_GROUPS * SSD_STATE

ML_HEADS = 4
ML_HEAD_DIM = D_BRANCH // ML_HEADS

S5_GROUP = 16
S5_GROUPS = D_BRANCH // S5_GROUP
S5_STATE = 64

RET_HEADS = 4
RET_QK_DIM = 64
RET_V_DIM = D_BRANCH // RET_HEADS
RET_DECAY_BASE = 5.0
ROPE_BASE = 10000.0

IN_SPLITS = (
    D_BRANCH, SSD_CONV_DIM, SSD_HEADS,
    D_BRANCH, D_BRANCH, D_BRANCH, D_BRANCH, D_BRANCH, ML_HEADS, ML_HEADS,
    D_BRANCH, D_BRANCH,
    D_BRANCH, RET_HEADS * RET_QK_DIM, RET_HEADS * RET_QK_DIM, D_BRANCH,
)
D_IN_PROJ = sum(IN_SPLITS)

F32 = jnp.float32

kernel_name = 'hybrid_ssd_mlstm_s5_retention_trunk'


def rmsnorm(x, w):
    xf = x.astype(F32)
    y = xf * lax.rsqrt(jnp.mean(xf * xf, axis=-1, keepdims=True) + NORM_EPS)
    return y * w.astype(F32)


def head_rmsnorm(y, w):
    y = y * lax.rsqrt(jnp.mean(y * y, axis=-1, keepdims=True) + NORM_EPS)
    return y.reshape(y.shape[0], y.shape[1], -1) * w.astype(F32)


def causal_conv(x, w, b):
    out = lax.conv_general_dilated(
        x, w[:, None, :].astype(x.dtype), window_strides=(1,),
        padding=[(CONV_K - 1, 0)], dimension_numbers=('NWC', 'WIO', 'NWC'),
        feature_group_count=x.shape[-1])
    return out + b.astype(out.dtype)


def rotary(x, positions):
    half = x.shape[-1] // 2
    inv_freq = jnp.exp(-math.log(ROPE_BASE) * jnp.arange(half, dtype=F32) / half)
    ang = positions.astype(F32)[..., None] * inv_freq
    cos = jnp.cos(ang)[:, :, None, :]
    sin = jnp.sin(ang)[:, :, None, :]
    x1, x2 = x[..., :half], x[..., half:]
    return jnp.concatenate([x1 * cos - x2 * sin, x2 * cos + x1 * sin], axis=-1)


def segsum(a):
    t = a.shape[-1]
    cs = jnp.cumsum(a, axis=-1)
    diff = cs[..., :, None] - cs[..., None, :]
    mask = jnp.tril(jnp.ones((t, t), dtype=bool))
    return jnp.where(mask, diff, -jnp.inf)


def ssd_chunked(x, dt, a, bm, cm):
    bsz, s, nh, p = x.shape
    g, n = bm.shape[2], bm.shape[3]
    e = nh // g
    nc = s // CHUNK
    xd = (x * dt[..., None]).reshape(bsz, nc, CHUNK, g, e, p)
    da = (dt * a).reshape(bsz, nc, CHUNK, g, e).transpose(0, 3, 4, 1, 2)
    bc = bm.reshape(bsz, nc, CHUNK, g, n)
    cc = cm.reshape(bsz, nc, CHUNK, g, n)
    a_cum = jnp.cumsum(da, axis=-1)
    decay_in = jnp.exp(segsum(da))
    cb = jnp.einsum('bclgn,bcsgn->bcgls', cc, bc)
    y_diag = jnp.einsum('bcgls,bgecls,bcsgep->bclgep', cb, decay_in, xd)
    decay_to_end = jnp.exp(a_cum[..., -1:] - a_cum)
    states = jnp.einsum('bclgn,bgecl,bclgep->bcgepn', bc, decay_to_end, xd)
    states = jnp.concatenate([jnp.zeros_like(states[:, :1]), states], axis=1)
    chunk_tot = jnp.pad(a_cum[..., -1], ((0, 0), (0, 0), (0, 0), (1, 0)))
    decay_chunk = jnp.exp(segsum(chunk_tot))
    states = jnp.einsum('bgezc,bcgepn->bzgepn', decay_chunk, states)[:, :-1]
    y_off = jnp.einsum('bclgn,bcgepn,bgecl->bclgep', cc, states, jnp.exp(a_cum))
    return (y_diag + y_off).reshape(bsz, s, nh, p)


def ssd_branch(z, xbc, dt_raw, conv_w, conv_b, dt_bias, a_log, d_skip, norm_w):
    bsz, s, _ = xbc.shape
    xbc = jax.nn.silu(causal_conv(xbc, conv_w, conv_b)).astype(F32)
    xs, bm, cm = jnp.split(xbc, [D_BRANCH, D_BRANCH + SSD_GROUPS * SSD_STATE], axis=-1)
    xs = xs.reshape(bsz, s, SSD_HEADS, SSD_HEAD_DIM)
    bm = bm.reshape(bsz, s, SSD_GROUPS, SSD_STATE)
    cm = cm.reshape(bsz, s, SSD_GROUPS, SSD_STATE)
    dt = jax.nn.softplus(dt_raw.astype(F32) + dt_bias.astype(F32))
    a = -jnp.exp(a_log.astype(F32))
    y = ssd_chunked(xs, dt, a, bm, cm) + d_skip.astype(F32)[:, None] * xs
    y = y.reshape(bsz, s, D_BRANCH) * jax.nn.silu(z.astype(F32))
    return rmsnorm(y, norm_w)


def mlstm_chunkwise(q, k, v, i_log, f_log):
    bsz, s, nh, dk = q.shape
    dv = v.shape[-1]
    nc = s // CHUNK
    qc = q.reshape(bsz, nc, CHUNK, nh, dk)
    kc = k.reshape(bsz, nc, CHUNK, nh, dk)
    vc = v.reshape(bsz, nc, CHUNK, nh, dv)
    ic = i_log.reshape(bsz, nc, CHUNK, nh).transpose(0, 3, 1, 2)
    fc = f_log.reshape(bsz, nc, CHUNK, nh).transpose(0, 3, 1, 2)
    bcum = jnp.cumsum(fc, axis=-1)
    b_last = bcum[..., -1]
    a = b_last[..., None] - bcum + ic
    m_loc = jnp.max(a, axis=-1)
    w = jnp.exp(a - m_loc[..., None])
    c_loc = jnp.einsum('bhcl,bclhk,bclhv->bchkv', w, kc, vc)
    n_loc = jnp.einsum('bhcl,bclhk->bchk', w, kc)

    def step(carry, inp):
        c_st, n_st, m_st = carry
        cl, nl, ml, bl = inp
        m_new = jnp.maximum(bl + m_st, ml)
        s_old = jnp.exp(bl + m_st - m_new)
        s_new = jnp.exp(ml - m_new)
        c_new = s_old[..., None, None] * c_st + s_new[..., None, None] * cl
        n_new = s_old[..., None] * n_st + s_new[..., None] * nl
        return (c_new, n_new, m_new), (c_st, n_st, m_st)

    init = (jnp.zeros((bsz, nh, dk, dv), F32), jnp.zeros((bsz, nh, dk), F32),
            jnp.zeros((bsz, nh), F32))
    xs = (jnp.moveaxis(c_loc, 1, 0), jnp.moveaxis(n_loc, 1, 0),
          jnp.moveaxis(m_loc, 2, 0), jnp.moveaxis(b_last, 2, 0))
    _, (c_prev, n_prev, m_prev) = lax.scan(step, init, xs)
    c_prev = jnp.moveaxis(c_prev, 0, 1)
    n_prev = jnp.moveaxis(n_prev, 0, 1)
    m_prev = jnp.moveaxis(m_prev, 0, 2)
    mask = jnp.tril(jnp.ones((CHUNK, CHUNK), dtype=bool))
    dmat = jnp.where(mask, bcum[..., :, None] - bcum[..., None, :] + ic[..., None, :], -jnp.inf)
    g_inter = bcum + m_prev[..., None]
    m_t = jnp.maximum(g_inter, jnp.max(dmat, axis=-1))
    sm = jnp.einsum('bcthk,bcshk->bhcts', qc, kc) * jnp.exp(dmat - m_t[..., None])
    inter = jnp.exp(g_inter - m_t)
    num = (jnp.einsum('bhcts,bcshv->bcthv', sm, vc)
           + jnp.einsum('bcthk,bchkv->bcthv', qc, c_prev) * inter.transpose(0, 2, 3, 1)[..., None])
    den = jnp.sum(sm, axis=-1) + inter * jnp.einsum('bcthk,bchk->bhct', qc, n_prev)
    denom = jnp.maximum(jnp.abs(den), jnp.exp(-m_t)).transpose(0, 2, 3, 1)
    return (num / denom[..., None]).reshape(bsz, s, nh, dv)


def mlstm_branch(z, q_raw, k_raw, v, o_pre, i_pre, f_pre, conv_w, conv_b, i_bias, f_bias, norm_w):
    bsz, s, _ = v.shape
    qk = jax.nn.silu(causal_conv(jnp.concatenate([q_raw, k_raw], axis=-1), conv_w, conv_b)).astype(F32)
    q, k = jnp.split(qk, 2, axis=-1)
    q = q.reshape(bsz, s, ML_HEADS, ML_HEAD_DIM)
    k = k.reshape(bsz, s, ML_HEADS, ML_HEAD_DIM) * ML_HEAD_DIM ** -0.5
    v = v.astype(F32).reshape(bsz, s, ML_HEADS, ML_HEAD_DIM)
    i_log = i_pre.astype(F32) + i_bias.astype(F32)
    f_log = jax.nn.log_sigmoid(f_pre.astype(F32) + f_bias.astype(F32))
    h = mlstm_chunkwise(q, k, v, i_log, f_log)
    h = jax.nn.sigmoid(o_pre.astype(F32)).reshape(bsz, s, ML_HEADS, ML_HEAD_DIM) * h
    return head_rmsnorm(h, norm_w) * jax.nn.silu(z.astype(F32))


def complex_affine_combine(e1, e2):
    a1r, a1i, b1r, b1i = e1
    a2r, a2i, b2r, b2i = e2
    return (a1r * a2r - a1i * a2i, a1r * a2i + a1i * a2r,
            a2r * b1r - a2i * b1i + b2r, a2r * b1i + a2i * b1r + b2i)


def s5_branch(z, u, lam_re, lam_im, b_re, b_im, c_re, c_im, d, log_step, w_glu, b_glu, norm_w):
    bsz, s, _ = u.shape
    u = u.astype(F32)
    step = jnp.exp(log_step.astype(F32))[:, None]
    lr = jnp.minimum(lam_re.astype(F32), -1e-4)
    li = lam_im.astype(F32)
    mag = jnp.exp(lr * step)
    ang = li * step
    ab_re = mag * jnp.cos(ang)
    ab_im = mag * jnp.sin(ang)
    den = lr * lr + li * li
    coef_re = ((ab_re - 1.0) * lr + ab_im * li) / den
    coef_im = (ab_im * lr - (ab_re - 1.0) * li) / den
    br = b_re.astype(F32)
    bi = b_im.astype(F32)
    bb_re = coef_re[..., None] * br - coef_im[..., None] * bi
    bb_im = coef_re[..., None] * bi + coef_im[..., None] * br
    ug = u.reshape(bsz, s, S5_GROUPS, S5_GROUP)
    bu_re = jnp.einsum('gpc,bsgc->bsgp', bb_re, ug)
    bu_im = jnp.einsum('gpc,bsgc->bsgp', bb_im, ug)
    a_re = jnp.broadcast_to(ab_re, bu_re.shape)
    a_im = jnp.broadcast_to(ab_im, bu_im.shape)
    _, _, x_re, x_im = lax.associative_scan(complex_affine_combine, (a_re, a_im, bu_re, bu_im), axis=1)
    y = (jnp.einsum('gcp,bsgp->bsgc', c_re.astype(F32), x_re)
         - jnp.einsum('gcp,bsgp->bsgc', c_im.astype(F32), x_im))
    y = y.reshape(bsz, s, D_BRANCH) + d.astype(F32) * u
    y = jax.nn.gelu(y)
    ga, gb = jnp.split(y @ w_glu.astype(F32) + b_glu.astype(F32), 2, axis=-1)
    y = ga * jax.nn.sigmoid(gb)
    return rmsnorm(y, norm_w) * jax.nn.silu(z.astype(F32))


def retention_chunkwise(q, k, v, log_gamma):
    bsz, s, nh, dk = q.shape
    dv = v.shape[-1]
    nc = s // CHUNK
    qc = q.reshape(bsz, nc, CHUNK, nh, dk)
    kc = k.reshape(bsz, nc, CHUNK, nh, dk)
    vc = v.reshape(bsz, nc, CHUNK, nh, dv)
    pos = jnp.arange(CHUNK, dtype=F32)
    rel = pos[:, None] - pos[None, :]
    decay_in = jnp.where(rel >= 0, jnp.exp(jnp.maximum(rel, 0.0)[None] * log_gamma[:, None, None]), 0.0)
    scores = jnp.einsum('bcthk,bcshk->bchts', qc, kc) * decay_in
    inner = jnp.einsum('bchts,bcshv->bcthv', scores, vc)
    to_end = jnp.exp((CHUNK - 1.0 - pos)[:, None] * log_gamma[None, :])
    s_loc = jnp.einsum('bcshk,sh,bcshv->bchkv', kc, to_end, vc)
    cidx = jnp.arange(nc, dtype=F32)
    cgap = cidx[:, None] - cidx[None, :] - 1.0
    decay_chunk = jnp.where(cgap >= 0, jnp.exp(CHUNK * jnp.maximum(cgap, 0.0)[None] * log_gamma[:, None, None]), 0.0)
    r_start = jnp.einsum('hzc,bchkv->bzhkv', decay_chunk, s_loc)
    from_start = jnp.exp((pos + 1.0)[:, None] * log_gamma[None, :])
    cross = jnp.einsum('bcthk,bchkv->bcthv', qc, r_start) * from_start[:, :, None]
    return (inner + cross).reshape(bsz, s, nh, dv)


def retention_branch(z, q, k, v, positions, norm_w):
    bsz, s, _ = v.shape
    q = rotary(q.astype(F32).reshape(bsz, s, RET_HEADS, RET_QK_DIM), positions)
    k = rotary(k.astype(F32).reshape(bsz, s, RET_HEADS, RET_QK_DIM), positions) * RET_QK_DIM ** -0.5
    v = v.astype(F32).reshape(bsz, s, RET_HEADS, RET_V_DIM)
    log_gamma = jnp.log1p(-jnp.exp2(-(RET_DECAY_BASE + jnp.arange(RET_HEADS, dtype=F32))))
    y = retention_chunkwise(q, k, v, log_gamma)
    return head_rmsnorm(y, norm_w) * jax.nn.silu(z.astype(F32))


def hybrid_layer(x, cond, positions, norm_w, w_ada, b_ada, w_in, w_out,
                 ssd_conv_w, ssd_conv_b, ssd_dt_bias, ssd_a_log, ssd_d, ssd_norm_w,
                 ml_conv_w, ml_conv_b, ml_i_bias, ml_f_bias, ml_norm_w,
                 s5_lambda_re, s5_lambda_im, s5_b_re, s5_b_im, s5_c_re, s5_c_im,
                 s5_d, s5_log_step, s5_w_glu, s5_b_glu, s5_norm_w, ret_norm_w):
    mod = cond @ w_ada.astype(F32) + b_ada.astype(F32)
    shift, scale, gate = jnp.split(mod, 3, axis=-1)
    hn = (rmsnorm(x, norm_w) * (1.0 + scale[:, None, :]) + shift[:, None, :]).astype(x.dtype)
    proj = hn @ w_in
    split_points = np.cumsum(IN_SPLITS)[:-1].tolist()
    (s_z, s_xbc, s_dt, m_z, m_q, m_k, m_v, m_o, m_i, m_f,
     c_z, c_u, r_z, r_q, r_k, r_v) = jnp.split(proj, split_points, axis=-1)
    y_ssd = ssd_branch(s_z, s_xbc, s_dt, ssd_conv_w, ssd_conv_b, ssd_dt_bias, ssd_a_log, ssd_d, ssd_norm_w)
    y_ml = mlstm_branch(m_z, m_q, m_k, m_v, m_o, m_i, m_f, ml_conv_w, ml_conv_b, ml_i_bias, ml_f_bias, ml_norm_w)
    y_s5 = s5_branch(c_z, c_u, s5_lambda_re, s5_lambda_im, s5_b_re, s5_b_im, s5_c_re, s5_c_im,
                     s5_d, s5_log_step, s5_w_glu, s5_b_glu, s5_norm_w)
    y_ret = retention_branch(r_z, r_q, r_k, r_v, positions, ret_norm_w)
    y = jnp.concatenate([y_ssd, y_ml, y_s5, y_ret], axis=-1).astype(x.dtype)
    out = y @ w_out
    return (x + gate[:, None, :] * out).astype(x.dtype)


def setup_inputs(seed: int = 0) -> dict:
    key = jax.random.key(seed)
    ks = iter(jax.random.split(key, 48))
    L = DEPTH

    def nrm(shape, scale):
        return scale * jax.random.normal(next(ks), shape, F32)

    def gain(shape):
        return 1.0 + nrm(shape, 0.02)

    def log_uniform(shape, lo, hi):
        return jax.random.uniform(next(ks), shape, F32, minval=math.log(lo), maxval=math.log(hi))

    x = nrm((BATCH, SEQ, D_MODEL), 1.0)
    c = nrm((BATCH, D_MODEL), 1.0)
    offset = jax.random.randint(next(ks), (BATCH, 1), 0, 4096, dtype=jnp.int32)
    positions = offset + jnp.arange(SEQ, dtype=jnp.int32)[None, :]

    dt0 = jnp.exp(log_uniform((L, SSD_HEADS), 1e-3, 1e-1))
    ssd_dt_bias = dt0 + jnp.log(-jnp.expm1(-dt0))
    ssd_a_log = jnp.log(jax.random.uniform(next(ks), (L, SSD_HEADS), F32, minval=1.0, maxval=16.0))

    s5_lambda_re = -0.5 + nrm((L, S5_GROUPS, S5_STATE), 0.01)
    s5_lambda_im = (jnp.pi * jnp.arange(S5_STATE, dtype=F32))[None, None, :] + nrm((L, S5_GROUPS, S5_STATE), 0.01)

    return {
        'x': x,
        'c': c,
        'positions': positions,
        'norm_w': gain((L, D_MODEL)),
        'w_ada': nrm((L, D_MODEL, 3 * D_MODEL), 0.3 * D_MODEL ** -0.5),
        'b_ada': nrm((L, 3 * D_MODEL), 0.02),
        'w_in': nrm((L, D_MODEL, D_IN_PROJ), D_MODEL ** -0.5),
        'w_out': nrm((L, D_INNER, D_MODEL), D_INNER ** -0.5),
        'ssd_conv_w': nrm((L, CONV_K, SSD_CONV_DIM), CONV_K ** -0.5),
        'ssd_conv_b': nrm((L, SSD_CONV_DIM), 0.02),
        'ssd_dt_bias': ssd_dt_bias,
        'ssd_a_log': ssd_a_log,
        'ssd_d': gain((L, SSD_HEADS)),
        'ssd_norm_w': gain((L, D_BRANCH)),
        'ml_conv_w': nrm((L, CONV_K, 2 * D_BRANCH), CONV_K ** -0.5),
        'ml_conv_b': nrm((L, 2 * D_BRANCH), 0.02),
        'ml_i_bias': nrm((L, ML_HEADS), 0.1),
        'ml_f_bias': jnp.linspace(3.0, 6.0, ML_HEADS, dtype=F32)[None, :] + nrm((L, ML_HEADS), 0.1),
        'ml_norm_w': gain((L, D_BRANCH)),
        's5_lambda_re': s5_lambda_re,
        's5_lambda_im': s5_lambda_im,
        's5_b_re': nrm((L, S5_GROUPS, S5_STATE, S5_GROUP), S5_GROUP ** -0.5),
        's5_b_im': nrm((L, S5_GROUPS, S5_STATE, S5_GROUP), S5_GROUP ** -0.5),
        's5_c_re': nrm((L, S5_GROUPS, S5_GROUP, S5_STATE), S5_STATE ** -0.5),
        's5_c_im': nrm((L, S5_GROUPS, S5_GROUP, S5_STATE), S5_STATE ** -0.5),
        's5_d': nrm((L, D_BRANCH), 0.5),
        's5_log_step': log_uniform((L, S5_GROUPS), 1e-3, 1e-1),
        's5_w_glu': nrm((L, D_BRANCH, 2 * D_BRANCH), D_BRANCH ** -0.5),
        's5_b_glu': nrm((L, 2 * D_BRANCH), 0.02),
        's5_norm_w': gain((L, D_BRANCH)),
        'ret_norm_w': gain((L, D_BRANCH)),
        'final_norm_w': gain((D_MODEL,)),
    }


def reference(x, c, positions, norm_w, w_ada, b_ada, w_in, w_out,
              ssd_conv_w, ssd_conv_b, ssd_dt_bias, ssd_a_log, ssd_d, ssd_norm_w,
              ml_conv_w, ml_conv_b, ml_i_bias, ml_f_bias, ml_norm_w,
              s5_lambda_re, s5_lambda_im, s5_b_re, s5_b_im, s5_c_re, s5_c_im,
              s5_d, s5_log_step, s5_w_glu, s5_b_glu, s5_norm_w, ret_norm_w, final_norm_w):
    cond = jax.nn.silu(c.astype(F32))
    h = x
    for l in range(DEPTH):
        h = hybrid_layer(h, cond, positions, norm_w[l], w_ada[l], b_ada[l], w_in[l], w_out[l],
                         ssd_conv_w[l], ssd_conv_b[l], ssd_dt_bias[l], ssd_a_log[l], ssd_d[l], ssd_norm_w[l],
                         ml_conv_w[l], ml_conv_b[l], ml_i_bias[l], ml_f_bias[l], ml_norm_w[l],
                         s5_lambda_re[l], s5_lambda_im[l], s5_b_re[l], s5_b_im[l], s5_c_re[l], s5_c_im[l],
                         s5_d[l], s5_log_step[l], s5_w_glu[l], s5_b_glu[l], s5_norm_w[l], ret_norm_w[l])
    return rmsnorm(h, final_norm_w).astype(x.dtype)
```

```python
import math
from contextlib import ExitStack
import numpy as np
import concourse.bass as bass
import concourse.mybir as mybir
from concourse.bass_utils import run_bass_kernel_spmd

F32 = mybir.dt.float32
BF16 = mybir.dt.bfloat16
I32 = mybir.dt.int32
AF = mybir.ActivationFunctionType
ALU = mybir.AluOpType
AX = mybir.AxisListType

NL = 4
D = 1024
S = 2048
T = 128
NCH = 16
EPS = 1e-6
SEG = {}
_o = 0
for _n, _w in [("ssd_z", 512), ("ssd_xbc", 1024),
               ("ml0", 1280), ("ml1", 1280),
               ("s5_z", 512), ("s5_u", 512),
               ("ret0", 1024), ("ret1", 1024),
               ("misc", 16)]:
    SEG[_n] = (_o, _w)
    _o += _w
NCOLS = _o
ROW = {}
_o = 0
for _n, _w in [("norm_w", 1024), ("b_ada", 3072), ("dt_bias", 8), ("a_log", 8), ("ssd_d", 8), ("ssd_nw", 512),
               ("ml_ib", 4), ("ml_fb", 4), ("ml_nw", 512), ("b_glu", 1024), ("s5_nw", 512), ("ret_nw", 512),
               ("lam_re", 2048), ("lam_im", 2048), ("lstep", 2048)]:
    ROW[_n] = (_o, _w)
    _o += _w
NROW = _o
ENGS = ["pe", "act", "dve", "pool", "sp"]


class Prog:
    def __init__(self, nc):
        self.nc = nc
        self.ops = []

    @staticmethod
    def _nz(names):
        out = []
        for x in names:
            if len(x) >= 2 and x[0] == "b" and x[1].isdigit() and (len(x) == 2 or not x[2].isalpha() or x[2] in "_abcdorvz"):
                x = x[:2]
            out.append(x)
        return tuple(out)

    def add(self, eng, fn, r=(), w=(), dma=False):
        self.ops.append((eng, fn, self._nz(r), self._nz(w), dma))

    def fence(self):
        self.ops.append(("FENCE", None, (), (), False))

    def finalize(self, es):
        nc = self.nc
        GEN = 30000
        NDS = 12
        cnt = {e: 0 for e in ENGS}
        dma_i = {e: 0 for e in ENGS}
        dma_cnt = {}
        memo = {e: {} for e in ENGS}
        lastw, readers = {}, {}
        last_tok = {}
        dma_toks = []
        pend = {e: [] for e in ENGS}
        plan = []
        semnames = set()
        for (eng, fn, r, w, dma) in self.ops:
            if eng == "FENCE":
                toks = list(last_tok.values()) + dma_toks
                dma_toks = []
                for e in ENGS:
                    pend[e] = list(toks)
                continue
            deps = set(pend[eng])
            pend[eng] = []
            for x in r:
                if x in lastw:
                    deps.add(lastw[x])
            for x in w:
                if x in lastw:
                    deps.add(lastw[x])
                for t in readers.get(x, ()):
                    deps.add(t)
            if dma:
                k = dma_i[eng] % NDS
                dma_i[eng] += 1
                sn = "d_%s_%d" % (eng, k)
                prev = dma_cnt.get(sn, 0)
                tok = (sn, prev + 16)
                dma_cnt[sn] = prev + 16
                if prev > 0:
                    deps.add((sn, prev))
                inc = 16
                dma_toks.append(tok)
            else:
                g = cnt[eng] // GEN
                sn = "e_%s_%d" % (eng, g)
                cnt[eng] += 1
                tok = (sn, cnt[eng] - g * GEN)
                inc = 1
                last_tok[eng] = tok
            semnames.add(sn)
            best = {}
            for (dsn, dv) in deps:
                if eng == "pe" and dsn.startswith("e_pe_"):
                    continue
                if memo[eng].get(dsn, 0) >= dv:
                    continue
                if best.get(dsn, 0) < dv:
                    best[dsn] = dv
            for dsn, dv in best.items():
                memo[eng][dsn] = dv
            plan.append((eng, fn, sorted(best.items()), sn, inc))
            for x in r:
                readers.setdefault(x, []).append(tok)
            for x in w:
                lastw[x] = tok
                readers[x] = []
        sems = {}
        for sn in sorted(semnames):
            sems[sn] = es.enter_context(nc.semaphore(sn))
        block = es.enter_context(nc.Block())

        def emit(engname, eobj):
            for (eng, fn, waits, sn, inc) in plan:
                if eng != engname:
                    continue
                for (dsn, dv) in waits:
                    eobj.wait_ge(sems[dsn], dv)
                ins = fn(eobj)
                ins.then_inc(sems[sn], inc)

        @block.tensor
        def _(e):
            emit("pe", e)

        @block.scalar
        def _(e):
            emit("act", e)

        @block.vector
        def _(e):
            emit("dve", e)

        @block.gpsimd
        def _(e):
            emit("pool", e)

        @block.sync
        def _(e):
            emit("sp", e)
        return len(plan)


class Arena:
    def __init__(self, ap, nbytes):
        self.ap = ap
        self.n = nbytes
        self.off = 0

    def reset(self, off=0):
        self.off = off

    def alloc(self, shape, dt):
        nel = 1
        for s_ in shape[1:]:
            nel *= s_
        nb = nel * (4 if dt in (F32, I32) else 2)
        nb = (nb + 31) // 32 * 32
        assert self.off + nb <= self.n, ("arena overflow", self.off, nb, self.n)
        v = self.ap[:, self.off // 4:(self.off + nb) // 4]
        self.off += nb
        if dt != F32:
            v = v.bitcast(dt)
        v = v[:, 0:nel]
        if len(shape) == 3:
            v = v.rearrange("p (a b) -> p a b", b=shape[2])
        elif len(shape) == 4:
            v = v.rearrange("p (a b c) -> p a b c", b=shape[2], c=shape[3])
        return v


def build(nl=NL, mix=(0, 1, 2, 3)):
    NLW = max(nl, 1)
    nc = bass.Bass("TRN2", target_bir_lowering=False)
    P = Prog(nc)
    es = ExitStack()

    def din(name, shape, dt=F32):
        return nc.dram_tensor(name, list(shape), dt, kind="ExternalInput").ap()
    x_d = din("x", [S, D])
    cT_d = din("cT", [128, 8])
    pos_d = din("posT", [128, NCH], I32)
    w_in_d = din("w_in", [NLW, D, NCOLS])
    w_out_d = din("w_out", [NLW, 2048, D])
    w_ada_d = din("w_ada", [NLW, D, 3072])
    w_glu_d = din("w_glu", [NLW, 512, 1024])
    rows_d = din("rows", [NLW, NROW])
    fnw_d = din("fnw", [1, D])
    convs_d = din("convs", [NLW, 128, 8 * 5])
    convm_d = din("convm", [NLW, 128, 8 * 5])
    s5col_d = din("s5col", [NLW, 128, 4 + 48])
    s5b_d = din("s5b", [NLW, 2, 4, 128, 512])
    s5c_d = din("s5c", [NLW, 2, 16, 128, 128])
    cst_d = din("cst", [128, 128 * 5 + 32 + 512 + 8])
    out_d = nc.dram_tensor("out", [S, D], F32, kind="ExternalOutput").ap()
    hnT_d = nc.dram_tensor("hnT_scr", [128, 8, S], BF16, kind="Internal").ap()

    def sb(name, shape, dt=F32):
        return es.enter_context(nc.sbuf_tensor(name, list(shape), dt))
    x_sb = sb("x_sb", [128, NCH, D])
    cst = sb("cst_sb", [128, 128 * 5 + 32 + 512 + 8])
    identb = sb("identb", [128, 128], BF16)
    rope = sb("rope", [128, 2, NCH, 32])
    mrow = sb("mrow", [128, 3, D])
    condbc = sb("condbc", [128, 8, 128], BF16)
    ARB = 88 * 1024
    arena_t = sb("arena", [128, ARB // 4])
    AR = Arena(arena_t, ARB)
    banks = [es.enter_context(nc.psum_tensor("bank%d" % i, [128, 512], F32)) for i in range(8)]

    identF = cst[:, 0:128]
    Uf = cst[:, 128:256]
    onesf = cst[:, 256:384]
    maskneg = cst[:, 384:512]
    invf = cst[:, 640:672]
    retDT = cst[:, 672:1184].rearrange("p (h t) -> p h t", t=128)
    ret_ea = cst[:, 1184:1188]
    ret_wend = cst[:, 1188:1192]
    RET_DEC = [float((1.0 - 2.0 ** -(5 + h)) ** 128) for h in range(4)]

    def dma(eng, out, in_, r, w):
        P.add(eng, lambda e: e.dma_start(out=out, in_=in_), r, w, dma=True)

    def mm(out, lhsT, rhs, start, stop, r, w):
        P.add("pe", lambda e: e.matmul(out, lhsT, rhs, start=start, stop=stop), r, w)

    def tr(out, in_, r, w):
        P.add("pe", lambda e: e.transpose(out, in_, identb[:, :]), r + ["identb"], w)

    def act(out, in_, func, r, w, bias=None, scale=None, accum=None, eng="act"):
        kw = {}
        if bias is not None:
            kw["bias"] = bias
        if scale is not None:
            kw["scale"] = scale
        if accum is not None:
            kw["accum_out"] = accum
        P.add(eng, lambda e: e.activation(out, in_, func, **kw), r, w)

    def tt(out, in0, in1, op, r, w, eng="dve"):
        P.add(eng, lambda e: e.tensor_tensor(out, in0, in1, op), r, w)

    def ts(out, in0, s1, s2, op0, op1, r, w, eng="dve"):
        if op1 is None:
            P.add(eng, lambda e: e.tensor_scalar(out, in0, s1, None, op0), r, w)
        else:
            P.add(eng, lambda e: e.tensor_scalar(out, in0, s1, s2, op0, op1), r, w)

    def stt(out, in0, scalar, in1, op0, op1, r, w):
        P.add("dve", lambda e: e.scalar_tensor_tensor(out, in0, scalar, in1, op0, op1), r, w)

    def cp(out, in_, r, w, eng="dve"):
        if eng == "act":
            P.add(eng, lambda e: e.activation(out, in_, AF.Copy), r, w)
        else:
            P.add(eng, lambda e: e.tensor_copy(out, in_), r, w)

    def memset(ap, val, w, eng="dve"):
        P.add(eng, lambda e: e.memset(ap, val), [], w)

    def recip(out, in_, r, w):
        P.add("dve", lambda e: e.reciprocal(out, in_), r, w)

    def red(out, in_, r, w):
        P.add("dve", lambda e: e.tensor_reduce(out, in_, AX.X, ALU.add), r, w)

    def rowload(dst, l, name, r0=0, n=None, wname=None, eng="sp"):
        o, wd = ROW[name]
        n = wd if n is None else n
        dma(eng, dst, rows_d[l, o + r0:o + r0 + n].partition_broadcast(128), [], [wname])

    def sincos(sin_out, cos_out, ang, tmpa, tmpi, nm, shape_is3=False):
        ts(tmpa, ang, 1.0 / (2 * math.pi), None, ALU.mult, None, [nm + "ang"], [nm + "ta"])
        cp(tmpi, tmpa, [nm + "ta"], [nm + "ti"])
        cp(tmpa, tmpi, [nm + "ti"], [nm + "ta"])
        stt(tmpa, tmpa, -2 * math.pi, ang, ALU.mult, ALU.add, [nm + "ta", nm + "ang"], [nm + "ta"])
        for (o_, sh, on) in ((sin_out, 0.0, nm + "sin"), (cos_out, math.pi / 2, nm + "cos")):
            ts(o_, tmpa, sh, None, ALU.add, None, [nm + "ta"], [on])
            ts(tmpi.bitcast(F32), o_, math.pi, 2 * math.pi, ALU.is_gt, ALU.mult, [on], [nm + "ti"])
            tt(o_, o_, tmpi.bitcast(F32), ALU.subtract, [on, nm + "ti"], [on])
            ts(tmpi.bitcast(F32), o_, -math.pi, 2 * math.pi, ALU.is_lt, ALU.mult, [on], [nm + "ti"])
            tt(o_, o_, tmpi.bitcast(F32), ALU.add, [on, nm + "ti"], [on])
            act(o_, o_, AF.Sin, [on], [on])

    dma("sp", cst[:, :], cst_d[:, :], [], ["cst"])
    for c in range(NCH):
        dma("sp", x_sb[:, c, :], x_d[c * 128:(c + 1) * 128, :], [], ["x%d" % c])
    cp(identb[:, :], identF, ["cst"], ["identb"])
    AR.reset()
    cTs = AR.alloc([128, 8], F32)
    posi = AR.alloc([128, NCH], I32)
    posf = AR.alloc([128, NCH], F32)
    ang = AR.alloc([128, NCH, 32], F32)
    tmpa = AR.alloc([128, NCH, 32], F32)
    tmpi = AR.alloc([128, NCH, 32], I32)
    condb = AR.alloc([128, 8], BF16)
    dma("sp", cTs, cT_d[:, :], [], ["cTs"])
    dma("sp", posi, pos_d[:, :], [], ["posi"])
    act(condb, cTs, AF.Silu, ["cTs"], ["condb"])
    cp(condbc[:, :, :], condb.unsqueeze(2).to_broadcast([128, 8, 128]), ["condb"], ["condbc"])
    cp(posf, posi, ["posi"], ["posf"])
    tt(ang, invf.unsqueeze(1).to_broadcast([128, NCH, 32]), posf.unsqueeze(2).to_broadcast([128, NCH, 32]),
       ALU.mult, ["cst", "posf"], ["ropeang"])
    sincos(rope[:, 1, :, :], rope[:, 0, :, :], ang, tmpa, tmpi, "rope")
    P.fence()

    def load_W(W, l, col0, ncols, wname):
        src = w_in_d[l].rearrange("(k p) n -> p k n", p=128)
        c = 0
        while c < ncols:
            n = min(512, ncols - c)
            dma("pool", W[:, :, c:c + n], src[:, :, col0 + c:col0 + c + n], [], [wname])
            c += n

    def load_hn(buf, c, nm):
        dma("sp", buf, hnT_d[:, :, c * 128:(c + 1) * 128], ["hnT%d" % c], [nm])

    def proj_TM(ps, hn, hnm, W, wname, col0, ncols, pname):
        for k in range(8):
            mm(ps, hn[:, k, :], W[:, k, col0:col0 + ncols], k == 0, k == 7, [hnm, wname], [pname])

    def proj_FM(ps, hn, hnm, W, wname, col0, pname):
        for k in range(8):
            mm(ps, W[:, k, col0:col0 + 128], hn[:, k, :], k == 0, k == 7, [hnm, wname], [pname])

    def rms_scale(ycur, yname, n, rs, junk, nm):
        act(junk[:, 0:n], ycur, AF.Square, [yname], [nm + "junk", nm + "rs"], accum=rs[:, 0:1])
        ts(rs[:, 1:2], rs[:, 0:1], 1.0 / n, EPS, ALU.mult, ALU.add, [nm + "rs"], [nm + "rs1"])
        act(rs[:, 1:2], rs[:, 1:2], AF.Sqrt, [nm + "rs1"], [nm + "rs1"])
        recip(rs[:, 2:3], rs[:, 1:2], [nm + "rs1"], [nm + "rs2"])
        return rs[:, 2:3], nm + "rs2"

    def finish(l, c, ybf, ybname, nk, wout, woname, yT, nm):
        tb = banks[6][:, 0:256].bitcast(BF16).rearrange("p (a b) -> p a b", b=128)
        for k in range(nk):
            tr(tb[:, k, :], ybf[:, k * 128:(k + 1) * 128], [ybname], ["b6"])
        cp(yT[:, 0:nk, :], tb[:, 0:nk, :], ["b6"], [nm + "yT"], eng="act")
        for hf in range(2):
            for k in range(nk):
                mm(banks[hf][:, :], yT[:, k, :], wout[:, k, hf * 512:(hf + 1) * 512], k == 0, k == nk - 1,
                   [nm + "yT", woname], ["b%d" % hf])
        for hf in range(2):
            xs = x_sb[:, c, hf * 512:(hf + 1) * 512]
            tmp = AR_tmp[0][:, hf * 512:(hf + 1) * 512]
            tt(tmp, banks[hf][:, :], mrow[:, 2, hf * 512:(hf + 1) * 512], ALU.mult, ["b%d" % hf, "mrow"], ["fin_tmp%d" % hf])
            tt(xs, xs, tmp, ALU.add, ["fin_tmp%d" % hf, "x%d" % c], ["x%d" % c], eng="pool")

    AR_tmp = [None]

    def attn_chunk(c, H, dv, qT, kT, kTM, v, DT, ea, wend, dec, St, Sb, yout, ops, nm, kscale=None):
        grp = max(1, 512 // dv)
        mT, kw = ops["mT"], ops["kw"]
        for h in range(H):
            (kta, ktn) = kTM(h)
            tt(kw[:, h, :], kta, wend[0][:, h:h + 1].to_broadcast([128, 128]), ALU.mult, ktn + wend[1], [nm + "kw%d" % h])
        for g0 in range(0, H, grp):
            hs = list(range(g0, min(H, g0 + grp)))
            for h in hs:
                j = h - g0
                sslot = h % 4
                scp = banks[2][:, sslot * 128:(sslot + 1) * 128]
                (qa, qn) = qT(h)
                (ka, kn) = kT(h)
                mm(scp, ka, qa, True, True, qn + kn, ["b2_%d" % sslot])
                (da, dn) = DT(h)
                ms = mT[:, h % 2, :]
                tt(ms, scp, da, ALU.mult, ["b2_%d" % sslot] + dn, [nm + "mT%d" % (h % 2)])
                (va, vn) = v(h)
                mm(banks[3][:, j * dv:(j + 1) * dv], ms, va, True, True, [nm + "mT%d" % (h % 2)] + vn, ["b3"])
                if c > 0:
                    mm(banks[4][:, j * dv:(j + 1) * dv], qa, Sb[:, h, :], True, True, qn + [nm + "Sb%d" % h], ["b4"])
                stslot = h % 2
                stp = banks[5][:, stslot * 256:stslot * 256 + dv]
                mm(stp, kw[:, h, :], va, True, True, [nm + "kw%d" % h] + vn, ["b5_%d" % stslot])
                if c == 0:
                    cp(St[:, h, :], stp, ["b5_%d" % stslot], [nm + "St%d" % h])
                else:
                    d_ = dec(h)
                    if isinstance(d_, float):
                        stt(St[:, h, :], St[:, h, :], d_, stp, ALU.mult, ALU.add,
                            ["b5_%d" % stslot, nm + "St%d" % h], [nm + "St%d" % h])
                    else:
                        stt(St[:, h, :], St[:, h, :], d_[0], stp, ALU.mult, ALU.add,
                            ["b5_%d" % stslot, nm + "St%d" % h] + d_[1], [nm + "St%d" % h])
                if c < NCH - 1:
                    cp(Sb[:, h, :], St[:, h, :], [nm + "St%d" % h], [nm + "Sb%d" % h], eng="act")
            n = len(hs)
            yv = yout[:, g0:g0 + n, :]
            b3v = banks[3][:, 0:n * dv].rearrange("p (h d) -> p h d", d=dv)
            if c > 0:
                b4v = banks[4][:, 0:n * dv].rearrange("p (h d) -> p h d", d=dv)
                tt(yv, b4v, ea[0][:, g0:g0 + n].unsqueeze(2).to_broadcast([128, n, dv]), ALU.mult,
                   ["b4"] + ea[1], [nm + "yout"])
                tt(yv, yv, b3v, ALU.add, ["b3", nm + "yout"], [nm + "yout"])
            else:
                cp(yv, b3v, ["b3"], [nm + "yout"])

    for l in range(nl):
        AR.reset()
        Wa = [AR.alloc([128, 8, 512], BF16) for _ in range(2)]
        brow = [AR.alloc([128, 512], F32) for _ in range(2)]
        nwrow = AR.alloc([128, D], F32)
        src = w_ada_d[l].rearrange("(k p) n -> p k n", p=128)
        rowload(nwrow, l, "norm_w", wname="nwrow")
        for blk in range(6):
            s_ = blk % 2
            dma("pool", Wa[s_], src[:, :, blk * 512:(blk + 1) * 512], [], ["Wa%d" % s_])
            rowload(brow[s_], l, "b_ada", blk * 512, 512, "brow%d" % s_)
            pb = banks[blk % 2]
            for k in range(8):
                mm(pb[:, :], condbc[:, k, :], Wa[s_][:, k, :], k == 0, k == 7, ["condbc", "Wa%d" % s_], ["b%d" % (blk % 2)])
            part = [0, 0, 1, 1, 2, 2][blk]
            tt(mrow[:, part, (blk % 2) * 512:(blk % 2 + 1) * 512], pb[:, :], brow[s_], ALU.add,
               ["b%d" % (blk % 2), "brow%d" % s_], ["mrow"])
        stt(mrow[:, 1, :], mrow[:, 1, :], 1.0, nwrow, ALU.add, ALU.mult, ["mrow", "nwrow"], ["mrow"])
        P.fence()
        AR.reset()
        junk = AR.alloc([128, D], BF16)
        tmp1 = [AR.alloc([128, D], F32) for _ in range(2)]
        hnb = [AR.alloc([128, D], BF16) for _ in range(2)]
        hnTs = [AR.alloc([128, 8, 128], BF16) for _ in range(2)]
        rs = [AR.alloc([128, 4], F32) for _ in range(2)]
        for c in range(NCH):
            s_ = c % 2
            nm = "p1_%d" % s_
            rstd, rn = rms_scale(x_sb[:, c, :], "x%d" % c, D, rs[s_], junk, nm)
            stt(tmp1[s_], x_sb[:, c, :], rstd, mrow[:, 1, :], ALU.mult, ALU.mult, ["x%d" % c, rn, "mrow"], [nm + "t"])
            tt(hnb[s_], tmp1[s_], mrow[:, 0, :], ALU.add, [nm + "t", "mrow"], [nm + "hn"], eng="pool")
            tb = banks[6 + s_][:, :].bitcast(BF16).rearrange("p (a b) -> p a b", b=128)
            for k in range(8):
                tr(tb[:, k, :], hnb[s_][:, k * 128:(k + 1) * 128], [nm + "hn"], ["b%d" % (6 + s_)])
            cp(hnTs[s_], tb, ["b%d" % (6 + s_)], [nm + "hnT"], eng="act")
            dma("sp", hnT_d[:, :, c * 128:(c + 1) * 128], hnTs[s_], [nm + "hnT"], ["hnT%d" % c])
        P.fence()

        if 0 in mix:
            AR.reset()
            W = AR.alloc([128, 8, 1536], BF16)
            Wm = AR.alloc([128, 8, 16], BF16)
            wout = AR.alloc([128, 4, D], BF16)
            hn = [AR.alloc([128, 8, 128], BF16) for _ in range(2)]
            raw = AR.alloc([128, 8, 131], F32)
            cv = AR.alloc([128, 128], F32)
            xbcT = AR.alloc([128, 8, 128], BF16)
            xTM = AR.alloc([128, 512], F32)
            BTM = AR.alloc([128, 2, 128], BF16)
            sz = AR.alloc([128, 512], F32)
            sm = AR.alloc([128, 96], F32)
            prm = AR.alloc([128, 32], F32)
            cprm = AR.alloc([128, 40], F32)
            nwr = AR.alloc([128, 512], F32)
            dabc = AR.alloc([128, 8, 128], F32)
            DTt = AR.alloc([128, 8, 128], F32)
            mT = AR.alloc([128, 2, 128], BF16)
            kw = AR.alloc([128, 8, 128], BF16)
            xd = AR.alloc([128, 8, 64], BF16)
            St = AR.alloc([128, 8, 64], F32)
            Sb = AR.alloc([128, 8, 64], BF16)
            yo = AR.alloc([128, 8, 64], F32)
            t3 = AR.alloc([128, 8, 64], F32)
            ybf = AR.alloc([128, 512], BF16)
            yT = AR.alloc([128, 4, 128], BF16)
            junk = AR.alloc([128, 512], BF16)
            rs = AR.alloc([128, 4], F32)
            AR_tmp[0] = AR.alloc([128, D], F32)
            load_W(W, l, SEG["ssd_z"][0], 1536, "W")
            load_W(Wm, l, SEG["misc"][0], 16, "Wm")
            dma("pool", wout, w_out_d[l, 0:512, :].rearrange("(k p) n -> p k n", p=128), [], ["wout"])
            rowload(prm[:, 0:8], l, "dt_bias", wname="prm")
            rowload(prm[:, 8:16], l, "a_log", wname="prm")
            rowload(prm[:, 16:24], l, "ssd_d", wname="prm")
            rowload(nwr, l, "ssd_nw", wname="nwr")
            dma("sp", cprm, convs_d[l], [], ["cprm"])
            act(prm[:, 8:16], prm[:, 8:16], AF.Exp, ["prm"], ["prm"])
            ts(prm[:, 8:16], prm[:, 8:16], -1.0, None, ALU.mult, None, ["prm"], ["prm"])
            memset(raw[:, :, 0:3], 0.0, ["raw"])
            cpv = cprm.rearrange("p (t k) -> p t k", k=5)
            for c in range(NCH):
                s_ = c % 2
                hnm = "hn%d" % s_
                load_hn(hn[s_], c, hnm)
                proj_TM(banks[0][:, :], hn[s_], hnm, W, "W", 0, 512, "b0")
                act(sz, banks[0][:, :], AF.Silu, ["b0"], ["sz"])
                proj_TM(banks[7][:, 0:8], hn[s_], hnm, Wm, "Wm", 0, 8, "b7a")
                tt(sm[:, 0:8], banks[7][:, 0:8], prm[:, 0:8], ALU.add, ["b7a", "prm"], ["sm_dt"])
                ts(sm[:, 0:8], sm[:, 0:8], 30.0, None, ALU.min, None, ["sm_dt"], ["sm_dt"])
                act(sm[:, 0:8], sm[:, 0:8], AF.Exp, ["sm_dt"], ["sm_dt"])
                act(sm[:, 0:8], sm[:, 0:8], AF.Ln, ["sm_dt"], ["sm_dt"], bias=1.0)
                tt(sm[:, 8:16], sm[:, 0:8], prm[:, 8:16], ALU.mult, ["sm_dt", "prm"], ["sm_da"])
                for t_ in range(8):
                    pp = banks[1][:, (t_ % 4) * 128:(t_ % 4 + 1) * 128]
                    pn = "b1_%d" % (t_ % 4)
                    proj_FM(pp, hn[s_], hnm, W, "W", 512 + t_ * 128, pn)
                    cp(raw[:, t_, 3:131], pp, [pn], ["raw"], eng="act")
                    ts(cv, raw[:, t_, 0:128], cpv[:, t_, 0:1], cpv[:, t_, 4:5], ALU.mult, ALU.add, ["raw", "cprm"], ["cv"])
                    for k_ in range(1, 4):
                        stt(cv, raw[:, t_, k_:k_ + 128], cpv[:, t_, k_:k_ + 1], cv, ALU.mult, ALU.add, ["raw", "cprm", "cv"], ["cv"])
                    act(xbcT[:, t_, :], cv, AF.Silu, ["cv"], ["xbcT%d" % t_])
                cp(raw[:, :, 0:3], raw[:, :, 128:131], ["raw"], ["raw"], eng="pool")
                tb = banks[6][:, :].bitcast(BF16).rearrange("p (a b) -> p a b", b=128)
                for t_ in range(6):
                    tr(tb[:, t_, :], xbcT[:, t_, :], ["xbcT%d" % t_], ["b6"])
                cp(xTM, tb[:, 0:4, :].rearrange("p a b -> p (a b)"), ["b6"], ["xTM"], eng="act")
                cp(BTM, tb[:, 4:6, :], ["b6"], ["BTM"], eng="act")
                mm(banks[7][:, 16:24], Uf, sm[:, 8:16], True, True, ["cst", "sm_da"], ["b7b"])
                mm(banks[7][:, 24:32], onesf, sm[:, 8:16], True, True, ["cst", "sm_da"], ["b7c"])
                cp(sm[:, 16:24], banks[7][:, 16:24], ["b7b"], ["sm_ac"])
                act(sm[:, 24:32], banks[7][:, 16:24], AF.Exp, ["b7b"], ["sm_ea"])
                tt(sm[:, 32:40], banks[7][:, 24:32], sm[:, 16:24], ALU.subtract, ["b7c", "sm_ac"], ["sm_we"])
                act(sm[:, 32:40], sm[:, 32:40], AF.Exp, ["sm_we"], ["sm_we"])
                act(sm[:, 40:48], banks[7][:, 24:32], AF.Exp, ["b7c"], ["sm_dec"])
                ts(sm[:, 48:56], sm[:, 16:24], -1.0, None, ALU.mult, None, ["sm_ac"], ["sm_nac"])
                cp(dabc, sm[:, 8:16].unsqueeze(2).to_broadcast([128, 8, 128]), ["sm_da"], ["dabc"], eng="pool")
                for h in range(8):
                    rp = banks[7][:, 128 + (h % 3) * 128:256 + (h % 3) * 128]
                    rn = "b7r%d" % (h % 3)
                    mm(rp, dabc[:, h, :], Uf, True, True, ["dabc", "cst"], [rn])
                    tt(DTt[:, h, :], rp, maskneg, ALU.add, [rn, "cst"], ["DT%d" % h])
                    act(DTt[:, h, :], DTt[:, h, :], AF.Exp, ["DT%d" % h, "sm_nac"], ["DT%d" % h], bias=sm[:, 48 + h:49 + h])
                xv = xTM.rearrange("p (h d) -> p h d", d=64)
                tt(xd, xv, sm[:, 0:8].unsqueeze(2).to_broadcast([128, 8, 64]), ALU.mult, ["xTM", "sm_dt"], ["xd"])
                attn_chunk(c, 8, 64,
                           lambda h: (xbcT[:, 6 + h // 4, :], ["xbcT%d" % (6 + h // 4)]),
                           lambda h: (xbcT[:, 4 + h // 4, :], ["xbcT%d" % (4 + h // 4)]),
                           lambda h: (BTM[:, h // 4, :], ["BTM"]),
                           lambda h: (xd[:, h, :], ["xd"]),
                           lambda h: (DTt[:, h, :], ["DT%d" % h]),
                           (sm[:, 24:32], ["sm_ea"]), (sm[:, 32:40], ["sm_we"]),
                           lambda h: (sm[:, 40 + h:41 + h], ["sm_dec"]),
                           St, Sb, yo, {"mT": mT, "kw": kw}, "ssd")
                tt(t3, xv, prm[:, 16:24].unsqueeze(2).to_broadcast([128, 8, 64]), ALU.mult, ["xTM", "prm"], ["t3"], eng="pool")
                yf = yo.rearrange("p h d -> p (h d)")
                tt(yf, yf, t3.rearrange("p h d -> p (h d)"), ALU.add, ["ssdyout", "t3"], ["ssdyout"])
                tt(yf, yf, sz, ALU.mult, ["ssdyout", "sz"], ["ssdyout"])
                rstd, rn = rms_scale(yf, "ssdyout", 512, rs, junk, "ssdn")
                stt(ybf, yf, rstd, nwr, ALU.mult, ALU.mult, ["ssdyout", rn, "nwr"], ["ybf"])
                finish(l, c, ybf, "ybf", 4, wout, "wout", yT, "ssd")
            P.fence()

        for kind in (1, 3):
            if kind not in mix:
                continue
            for half in range(2):
                isml = kind == 1
                AR.reset()
                ncol = 1280 if isml else 1024
                seg = SEG[("ml%d" if isml else "ret%d") % half][0]
                W = AR.alloc([128, 8, ncol], BF16)
                Wm = AR.alloc([128, 8, 16], BF16)
                wout = AR.alloc([128, 2, D], BF16)
                hn = [AR.alloc([128, 8, 128], BF16) for _ in range(2)]
                raw = AR.alloc([128, 4, 131], F32)
                cv = AR.alloc([128, 128], F32)
                qkT = AR.alloc([128, 4, 128], BF16)
                kTMt = AR.alloc([128, 2, 128], BF16)
                qkrot = AR.alloc([128, 4, 128], BF16)
                rt = AR.alloc([128, 4, 2, 32], F32)
                rt2 = AR.alloc([128, 4, 2, 32], F32)
                sz = AR.alloc([128, 256], F32)
                so = AR.alloc([128, 256], F32)
                vb = AR.alloc([128, 2, 129], BF16)
                sm = AR.alloc([128, 64], F32)
                prm = AR.alloc([128, 16], F32)
                cprm = AR.alloc([128, 40], F32)
                nwr = AR.alloc([128, 256], F32)
                dabc = AR.alloc([128, 2, 128], F32)
                DTt = AR.alloc([128, 2, 128], F32)
                mT = AR.alloc([128, 2, 128], BF16)
                kw = AR.alloc([128, 2, 128], BF16)
                St = AR.alloc([128, 2, 129], F32)
                Sb = AR.alloc([128, 2, 129], BF16)
                yo = AR.alloc([128, 2, 129], F32)
                hh = AR.alloc([128, 2, 128], F32)
                sq = AR.alloc([128, 2, 128], F32)
                ybf = AR.alloc([128, 256], BF16)
                yT = AR.alloc([128, 4, 128], BF16)
                AR_tmp[0] = AR.alloc([128, D], F32)
                dv = 129 if isml else 128
                load_W(W, l, seg, ncol, "W")
                r0 = (512 if isml else 1536) + half * 256
                dma("pool", wout, w_out_d[l, r0:r0 + 256, :].rearrange("(k p) n -> p k n", p=128), [], ["wout"])
                rowload(nwr, l, "ml_nw" if isml else "ret_nw", half * 256, 256, "nwr")
                if isml:
                    load_W(Wm, l, SEG["misc"][0], 16, "Wm")
                    rowload(prm[:, 0:2], l, "ml_ib", half * 2, 2, "prm")
                    rowload(prm[:, 2:4], l, "ml_fb", half * 2, 2, "prm")
                    dma("sp", cprm, convm_d[l], [], ["cprm"])
                    memset(raw[:, :, 0:3], 0.0, ["raw"])
                    memset(vb[:, :, 128:129], 1.0, ["vb"])
                else:
                    memset(qkrot, 0.0, ["qkrot"])
                cpv = cprm.rearrange("p (t k) -> p t k", k=5)
                KS = 128.0 ** -0.5
                for c in range(NCH):
                    s_ = c % 2
                    hnm = "hn%d" % s_
                    load_hn(hn[s_], c, hnm)
                    proj_TM(banks[0][:, 0:256], hn[s_], hnm, W, "W", 0, 256, "b0")
                    act(sz, banks[0][:, 0:256], AF.Silu, ["b0"], ["sz"])
                    proj_TM(banks[0][:, 256:512], hn[s_], hnm, W, "W", 768, 256, "b0v")
                    cp(vb[:, :, 0:128], banks[0][:, 256:512].rearrange("p (h d) -> p h d", d=128), ["b0v"], ["vb"], eng="act")
                    if isml:
                        proj_TM(banks[1][:, 0:256], hn[s_], hnm, W, "W", 1024, 256, "b1o")
                        act(so, banks[1][:, 0:256], AF.Sigmoid, ["b1o"], ["so"])
                        proj_TM(banks[7][:, 0:2], hn[s_], hnm, Wm, "Wm", 8 + 2 * half, 2, "b7a")
                        proj_TM(banks[7][:, 2:4], hn[s_], hnm, Wm, "Wm", 12 + 2 * half, 2, "b7a2")
                        tt(sm[:, 0:2], banks[7][:, 0:2], prm[:, 0:2], ALU.add, ["b7a", "prm"], ["sm_i"])
                        tt(sm[:, 2:4], banks[7][:, 2:4], prm[:, 2:4], ALU.add, ["b7a2", "prm"], ["sm_f"])
                        ts(sm[:, 2:4], sm[:, 2:4], -30.0, None, ALU.max, None, ["sm_f"], ["sm_f"])
                        act(sm[:, 2:4], sm[:, 2:4], AF.Exp, ["sm_f"], ["sm_f"], scale=-1.0)
                        act(sm[:, 2:4], sm[:, 2:4], AF.Ln, ["sm_f"], ["sm_f"], bias=1.0)
                        ts(sm[:, 2:4], sm[:, 2:4], -1.0, None, ALU.mult, None, ["sm_f"], ["sm_f"])
                        for t_ in range(4):
                            pp = banks[1][:, 256 + (t_ % 2) * 128:384 + (t_ % 2) * 128]
                            pn = "b1_%d" % (t_ % 2)
                            proj_FM(pp, hn[s_], hnm, W, "W", 256 + t_ * 128, pn)
                            cp(raw[:, t_, 3:131], pp, [pn], ["raw"], eng="act")
                            ct = half * 4 + t_
                            ts(cv, raw[:, t_, 0:128], cpv[:, ct, 0:1], cpv[:, ct, 4:5], ALU.mult, ALU.add, ["raw", "cprm"], ["cv"])
                            for k_ in range(1, 4):
                                stt(cv, raw[:, t_, k_:k_ + 128], cpv[:, ct, k_:k_ + 1], cv, ALU.mult, ALU.add, ["raw", "cprm", "cv"], ["cv"])
                            act(qkT[:, t_, :], cv, AF.Silu, ["cv"], ["qkT%d" % t_])
                        cp(raw[:, :, 0:3], raw[:, :, 128:131], ["raw"], ["raw"], eng="pool")
                        tb = banks[6][:, :].bitcast(BF16).rearrange("p (a b) -> p a b", b=128)
                        for t_ in range(2):
                            tr(tb[:, t_, :], qkT[:, 2 + t_, :], ["qkT%d" % (2 + t_)], ["b6"])
                        cp(kTMt, tb[:, 0:2, :], ["b6"], ["kTM"], eng="act")
                        mm(banks[7][:, 16:18], Uf, sm[:, 2:4], True, True, ["cst", "sm_f"], ["b7b"])
                        mm(banks[7][:, 24:26], onesf, sm[:, 2:4], True, True, ["cst", "sm_f"], ["b7c"])
                        cp(sm[:, 16:18], banks[7][:, 16:18], ["b7b"], ["sm_ac"])
                        act(sm[:, 24:26], banks[7][:, 16:18], AF.Exp, ["b7b"], ["sm_ea"])
                        tt(sm[:, 32:34], banks[7][:, 24:26], sm[:, 16:18], ALU.subtract, ["b7c", "sm_ac"], ["sm_we"])
                        tt(sm[:, 32:34], sm[:, 32:34], sm[:, 0:2], ALU.add, ["sm_we", "sm_i"], ["sm_we"])
                        act(sm[:, 32:34], sm[:, 32:34], AF.Exp, ["sm_we"], ["sm_we"])
                        ts(sm[:, 32:34], sm[:, 32:34], KS, None, ALU.mult, None, ["sm_we"], ["sm_we"])
                        act(sm[:, 40:42], banks[7][:, 24:26], AF.Exp, ["b7c"], ["sm_dec"])
                        tt(sm[:, 48:50], sm[:, 0:2], sm[:, 16:18], ALU.subtract, ["sm_i", "sm_ac"], ["sm_nac"])
                        cp(dabc, sm[:, 2:4].unsqueeze(2).to_broadcast([128, 2, 128]), ["sm_f"], ["dabc"], eng="pool")
                        for h in range(2):
                            rp = banks[7][:, 128 + h * 128:256 + h * 128]
                            rn = "b7r%d" % h
                            mm(rp, dabc[:, h, :], Uf, True, True, ["dabc", "cst"], [rn])
                            tt(DTt[:, h, :], rp, maskneg, ALU.add, [rn, "cst"], ["DT%d" % h])
                            act(DTt[:, h, :], DTt[:, h, :], AF.Exp, ["DT%d" % h, "sm_nac"], ["DT%d" % h], bias=sm[:, 48 + h:49 + h])
                            ts(DTt[:, h, :], DTt[:, h, :], KS, None, ALU.mult, None, ["DT%d" % h], ["DT%d" % h])
                        qTf = lambda h: (qkT[:, h, :], ["qkT%d" % h])
                        kTf = lambda h: (qkT[:, 2 + h, :], ["qkT%d" % (2 + h)])
                        DTf = lambda h: (DTt[:, h, :], ["DT%d" % h])
                        eaf = (sm[:, 24:26], ["sm_ea"])
                        wef = (sm[:, 32:34], ["sm_we"])
                        decf = lambda h: (sm[:, 40 + h:41 + h], ["sm_dec"])
                    else:
                        proj_TM(banks[1][:, 0:512], hn[s_], hnm, W, "W", 256, 512, "b1o")
                        qv = banks[1][:, 0:512].rearrange("p (a d) -> p a d", d=128)
                        cosb = rope[:, 0, c, :].unsqueeze(1).to_broadcast([128, 4, 32])
                        sinb = rope[:, 1, c, :].unsqueeze(1).to_broadcast([128, 4, 32])
                        x1 = qv[:, :, 0:32]
                        x2 = qv[:, :, 32:64]
                        tt(rt[:, :, 0, :], x1, cosb, ALU.mult, ["b1o"], ["rt"])
                        tt(rt[:, :, 1, :], x2, cosb, ALU.mult, ["b1o"], ["rt"])
                        tt(rt2[:, :, 0, :], x2, sinb, ALU.mult, ["b1o"], ["rt2"])
                        tt(rt2[:, :, 1, :], x1, sinb, ALU.mult, ["b1o"], ["rt2"])
                        tt(qkrot[:, :, 0:32], rt[:, :, 0, :], rt2[:, :, 0, :], ALU.subtract, ["rt", "rt2"], ["qkrot"])
                        tt(qkrot[:, :, 32:64], rt[:, :, 1, :], rt2[:, :, 1, :], ALU.add, ["rt", "rt2"], ["qkrot"])
                        tb = banks[6][:, :].bitcast(BF16).rearrange("p (a b) -> p a b", b=128)
                        for t_ in range(4):
                            tr(tb[:, t_, :], qkrot[:, t_, :], ["qkrot"], ["b6"])
                        cp(qkT, tb[:, 0:4, :], ["b6"], ["qkT0", "qkT1", "qkT2", "qkT3"], eng="act")
                        qTf = lambda h: (qkT[:, h, :], ["qkT%d" % h])
                        kTf = lambda h: (qkT[:, 2 + h, :], ["qkT%d" % (2 + h)])
                        DTf = lambda h: (retDT[:, 2 * half + h, :], ["cst"])
                        eaf = (ret_ea[:, 2 * half:2 * half + 2], ["cst"])
                        wef = (ret_wend[:, 2 * half:2 * half + 2], ["cst"])
                        decf = lambda h: RET_DEC[2 * half + h]
                    if isml:
                        kTMf = lambda h: (kTMt[:, h, :], ["kTM"])
                    else:
                        kTMf = lambda h: (qkrot[:, 2 + h, :], ["qkrot"])
                    attn_chunk(c, 2, dv, qTf, kTf, kTMf,
                               lambda h: (vb[:, h, 0:dv], ["vb"]),
                               DTf, eaf, wef, decf, St[:, :, 0:dv], Sb[:, :, 0:dv], yo[:, :, 0:dv],
                               {"mT": mT, "kw": kw}, "at")
                    if isml:
                        dn_ = yo[:, :, 128:129].rearrange("p h d -> p (h d)")
                        stt(sm[:, 56:58], dn_, -1.0, dn_, ALU.mult, ALU.max, ["atyout"], ["sm_den"])
                        ts(sm[:, 56:58], sm[:, 56:58], 1.0, None, ALU.max, None, ["sm_den"], ["sm_den"])
                        recip(sm[:, 56:58], sm[:, 56:58], ["sm_den"], ["sm_den"])
                        tt(hh, yo[:, :, 0:128], sm[:, 56:58].unsqueeze(2).to_broadcast([128, 2, 128]), ALU.mult, ["atyout", "sm_den"], ["hh"])
                        tt(hh, hh, so.rearrange("p (h d) -> p h d", d=128), ALU.mult, ["hh", "so"], ["hh"])
                    else:
                        cp(hh, yo[:, :, 0:128], ["atyout"], ["hh"], eng="pool")
                    tt(sq, hh, hh, ALU.mult, ["hh"], ["sq"], eng="pool")
                    red(sm[:, 58:60], sq, ["sq"], ["sm_ss"])
                    ts(sm[:, 58:60], sm[:, 58:60], 1.0 / 128, EPS, ALU.mult, ALU.add, ["sm_ss"], ["sm_ss"])
                    act(sm[:, 58:60], sm[:, 58:60], AF.Sqrt, ["sm_ss"], ["sm_ss"])
                    recip(sm[:, 58:60], sm[:, 58:60], ["sm_ss"], ["sm_ss"])
                    tt(hh, hh, sm[:, 58:60].unsqueeze(2).to_broadcast([128, 2, 128]), ALU.mult, ["hh", "sm_ss"], ["hh"])
                    hf_ = hh.rearrange("p h d -> p (h d)")
                    tt(hf_, hf_, nwr, ALU.mult, ["hh", "nwr"], ["hh"])
                    tt(ybf, hf_, sz, ALU.mult, ["hh", "sz"], ["ybf"])
                    finish(l, c, ybf, "ybf", 2, wout, "wout", yT, "at")
                P.fence()

        if 2 in mix:
            AR.reset()
            gT = AR.alloc([128, 4, S], BF16)
            keep = AR.off
            bu = [AR.alloc([128, S], F32) for _ in range(4)]
            uT = AR.alloc([128, S], BF16)
            Wu = AR.alloc([128, 8, 128], BF16)
            bb = AR.alloc([128, 2, 512], BF16)
            Cp = AR.alloc([128, 2, 4, 128], BF16)
            xb = AR.alloc([128, 2, S], BF16)
            hb = [AR.alloc([128, 8, 256], BF16) for _ in range(2)]
            diagD = AR.alloc([128, 4, 128], BF16)
            scol = AR.alloc([128, 52], F32)
            cw = AR.alloc([128, 8, 16], F32)
            cwi = AR.alloc([128, 16], I32)
            pw = AR.alloc([128, 3, 11, 16], F32)
            ysb = AR.alloc([128, 512], F32)
            gt_ = AR.alloc([128, 512], F32)
            dma("sp", scol, s5col_d[l], [], ["scol"])
            for k in range(4):
                ts(diagD[:, k, :], identF, scol[:, k:k + 1], None, ALU.mult, None, ["cst", "scol"], ["diagD"])
            lre, lim, lst = scol[:, 4:20], scol[:, 20:36], scol[:, 36:52]
            act(cw[:, 0, :], lst, AF.Exp, ["scol"], ["cw0"])
            ts(cw[:, 1, :], lre, -1e-4, None, ALU.min, None, ["scol"], ["cw1"])
            tt(cw[:, 2, :], cw[:, 1, :], cw[:, 0, :], ALU.mult, ["cw0", "cw1"], ["cw2"])
            tt(cw[:, 3, :], lim, cw[:, 0, :], ALU.mult, ["cw0", "scol"], ["cAang"])
            ts(cw[:, 3, :], cw[:, 3, :], 0.0, None, ALU.max, None, ["cAang"], ["cAang"])
            act(cw[:, 2, :], cw[:, 2, :], AF.Exp, ["cw2"], ["cw2"])
            sincos(cw[:, 4, :], cw[:, 5, :], cw[:, 3, :], cw[:, 6, :], cwi, "cA")
            tt(pw[:, 0, 0, :], cw[:, 2, :], cw[:, 5, :], ALU.mult, ["cw2", "cAcos"], ["pw"])
            tt(pw[:, 1, 0, :], cw[:, 2, :], cw[:, 4, :], ALU.mult, ["cw2", "cAsin"], ["pw"])
            for lv in range(1, 11):
                a_, b_ = pw[:, 0, lv - 1, :], pw[:, 1, lv - 1, :]
                tt(cw[:, 6, :], a_, a_, ALU.mult, ["pw"], ["cw6"])
                tt(cw[:, 7, :], b_, b_, ALU.mult, ["pw"], ["cw7"])
                tt(pw[:, 0, lv, :], cw[:, 6, :], cw[:, 7, :], ALU.subtract, ["cw6", "cw7"], ["pw"])
                tt(cw[:, 6, :], a_, b_, ALU.mult, ["pw"], ["cw6"])
                ts(pw[:, 1, lv, :], cw[:, 6, :], 2.0, None, ALU.mult, None, ["cw6"], ["pw"])
            ts(pw[:, 2, :, :], pw[:, 1, :, :], -1.0, None, ALU.mult, None, ["pw"], ["pw"])
            ybanks = [3, 4, 5, 6]
            for k in range(4):
                load_W(Wu, l, SEG["s5_u"][0] + k * 128, 128, "Wu")
                for tb_ in range(8):
                    s_ = tb_ % 2
                    dma("sp", hb[s_], hnT_d[:, :, tb_ * 256:(tb_ + 1) * 256],
                        ["hnT%d" % (2 * tb_), "hnT%d" % (2 * tb_ + 1)], ["hb%d" % s_])
                    pp = banks[s_][:, 0:256]
                    for kk in range(8):
                        mm(pp, Wu[:, kk, :], hb[s_][:, kk, :], kk == 0, kk == 7, ["Wu", "hb%d" % s_], ["b%d" % s_])
                    cp(uT[:, tb_ * 256:(tb_ + 1) * 256], pp, ["b%d" % s_], ["uT"], eng="act")
                R = [bu[i][:, j * 512:(j + 1) * 512] for i in range(4) for j in range(4)]
                rn_ = ["bu%d" % i for i in range(4) for j in range(4)]
                for i_, nm_ in enumerate(("lam_re", "lam_im", "lstep")):
                    rowload(R[i_], l, nm_, k * 512, 512, rn_[i_] + "_r%d" % i_)
                dma("sp", R[3], s5b_d[l, 0, k], [], ["bu0_r3"])
                dma("sp", R[4], s5b_d[l, 1, k], [], ["bu1_r4"])
                RN = lambda i: [rn_[i] + "_r%d" % i]
                act(R[2], R[2], AF.Exp, RN(2), RN(2))
                ts(R[0], R[0], -1e-4, None, ALU.min, None, RN(0), RN(0))
                tt(R[5], R[0], R[2], ALU.mult, RN(0) + RN(2), RN(5))
                tt(R[6], R[1], R[2], ALU.mult, RN(1) + RN(2), ["rAang"])
                ts(R[6], R[6], 0.0, None, ALU.max, None, ["rAang"], ["rAang"])
                act(R[5], R[5], AF.Exp, RN(5), RN(5))
                sincos(R[7], R[8], R[6], R[9], R[10].bitcast(I32), "rA")
                tt(R[7], R[7], R[5], ALU.mult, ["rAsin"] + RN(5), ["rAsin"])
                tt(R[8], R[8], R[5], ALU.mult, ["rAcos"] + RN(5), ["rAcos"])
                ts(R[8], R[8], -1.0, None, ALU.add, None, ["rAcos"], ["rAcos"])
                tt(R[9], R[0], R[0], ALU.mult, RN(0), ["rAta"])
                tt(R[10], R[1], R[1], ALU.mult, RN(1), ["rAti"])
                tt(R[9], R[9], R[10], ALU.add, ["rAta", "rAti"], ["rAta"])
                recip(R[9], R[9], ["rAta"], ["rAta"])
                tt(R[10], R[8], R[0], ALU.mult, ["rAcos"] + RN(0), ["rAti"])
                tt(R[11], R[7], R[1], ALU.mult, ["rAsin"] + RN(1), RN(11))
                tt(R[10], R[10], R[11], ALU.add, ["rAti"] + RN(11), ["rAti"])
                tt(R[10], R[10], R[9], ALU.mult, ["rAti", "rAta"], ["rAti"])
                tt(R[11], R[7], R[0], ALU.mult, ["rAsin"] + RN(0), RN(11))
                tt(R[12], R[8], R[1], ALU.mult, ["rAcos"] + RN(1), RN(12))
                tt(R[11], R[11], R[12], ALU.subtract, RN(11) + RN(12), RN(11))
                tt(R[11], R[11], R[9], ALU.mult, RN(11) + ["rAta"], RN(11))
                tt(R[12], R[10], R[3], ALU.mult, ["rAti", "bu0_r3"], RN(12))
                tt(R[13], R[11], R[4], ALU.mult, RN(11) + ["bu1_r4"], RN(13))
                tt(bb[:, 0, :], R[12], R[13], ALU.subtract, RN(12) + RN(13), ["bb"])
                tt(R[12], R[10], R[4], ALU.mult, ["rAti", "bu1_r4"], RN(12))
                tt(R[13], R[11], R[3], ALU.mult, RN(11) + ["bu0_r3"], RN(13))
                tt(bb[:, 1, :], R[12], R[13], ALU.add, RN(12) + RN(13), ["bb"])
                dma("pool", Cp[:, 0, :, :], s5c_d[l, 0, 4 * k:4 * k + 4].rearrange("a p n -> p a n"), [], ["Cp"])
                dma("pool", Cp[:, 1, :, :], s5c_d[l, 1, 4 * k:4 * k + 4].rearrange("a p n -> p a n"), [], ["Cp"])
                P.fence()
                for jj in range(4):
                    jt = 4 * k + jj
                    for ri in range(2):
                        for tb_ in range(4):
                            pp = banks[tb_ % 2][:, :]
                            mm(pp, bb[:, ri, jj * 128:(jj + 1) * 128], uT[:, tb_ * 512:(tb_ + 1) * 512], True, True,
                               ["bb", "uT"], ["b%d" % (tb_ % 2)])
                            cp(bu[ri][:, tb_ * 512:(tb_ + 1) * 512], pp, ["b%d" % (tb_ % 2)], ["bu%d" % ri], eng="act")
                    cur = 0
                    for lv in range(11):
                        d_ = 1 << lv
                        sr, si = bu[2 * cur], bu[2 * cur + 1]
                        dr, di = bu[2 * (1 - cur)], bu[2 * (1 - cur) + 1]
                        srn, sin_, drn, din_ = "bu%d" % (2 * cur), "bu%d" % (2 * cur + 1), "bu%d" % (2 - 2 * cur), "bu%d" % (3 - 2 * cur)
                        ar_ = pw[:, 0, lv, jt:jt + 1]
                        ai_ = pw[:, 1, lv, jt:jt + 1]
                        nai_ = pw[:, 2, lv, jt:jt + 1]
                        stt(dr[:, d_:S], sr[:, 0:S - d_], ar_, sr[:, d_:S], ALU.mult, ALU.add, [srn, "pw"], [drn])
                        stt(dr[:, d_:S], si[:, 0:S - d_], nai_, dr[:, d_:S], ALU.mult, ALU.add, [sin_, "pw", drn], [drn])
                        cp(dr[:, 0:d_], sr[:, 0:d_], [srn], [drn], eng="act")
                        stt(di[:, d_:S], si[:, 0:S - d_], ar_, si[:, d_:S], ALU.mult, ALU.add, [sin_, "pw"], [din_])
                        stt(di[:, d_:S], sr[:, 0:S - d_], ai_, di[:, d_:S], ALU.mult, ALU.add, [srn, "pw", din_], [din_])
                        cp(di[:, 0:d_], si[:, 0:d_], [sin_], [din_], eng="act")
                        cur = 1 - cur
                    fr, fi = bu[2 * cur], bu[2 * cur + 1]
                    cp(xb[:, 0, :], fr, ["bu%d" % (2 * cur)], ["xb"], eng="pool")
                    act(xb[:, 1, :], fi, AF.Copy, ["bu%d" % (2 * cur + 1)], ["xb"], scale=-1.0)
                    for tb_ in range(4):
                        yb = banks[ybanks[tb_]][:, :]
                        mm(yb, Cp[:, 0, jj, :], xb[:, 0, tb_ * 512:(tb_ + 1) * 512], jj == 0, False, ["Cp", "xb"], ["b%d" % ybanks[tb_]])
                        mm(yb, Cp[:, 1, jj, :], xb[:, 1, tb_ * 512:(tb_ + 1) * 512], False, False, ["Cp", "xb"], ["b%d" % ybanks[tb_]])
                        if jj == 3:
                            mm(yb, diagD[:, k, :], uT[:, tb_ * 512:(tb_ + 1) * 512], False, True, ["diagD", "uT"], ["b%d" % ybanks[tb_]])
                for tb_ in range(4):
                    yb = banks[ybanks[tb_]][:, :]
                    cp(ysb, yb, ["b%d" % ybanks[tb_]], ["ysb"], eng="act")
                    tt(gt_, ysb, ysb, ALU.mult, ["ysb"], ["gt"])
                    ts(gt_, gt_, 0.044715, 1.0, ALU.mult, ALU.add, ["gt"], ["gt"])
                    tt(gt_, gt_, ysb, ALU.mult, ["gt", "ysb"], ["gt"])
                    act(gt_, gt_, AF.Sigmoid, ["gt"], ["gt"], scale=2.0 * math.sqrt(2.0 / math.pi))
                    tt(gT[:, k, tb_ * 512:(tb_ + 1) * 512], gt_, ysb, ALU.mult, ["gt", "ysb"], ["gT"])
                P.fence()
            AR.reset(keep)
            W = AR.alloc([128, 8, 512], BF16)
            wglu = AR.alloc([128, 4, D], BF16)
            wout = AR.alloc([128, 4, D], BF16)
            hn = [AR.alloc([128, 8, 128], BF16) for _ in range(2)]
            sz = AR.alloc([128, 512], F32)
            gab = AR.alloc([128, D], F32)
            bgl = AR.alloc([128, D], F32)
            nwr = AR.alloc([128, 512], F32)
            yv = AR.alloc([128, 512], F32)
            ybf = AR.alloc([128, 512], BF16)
            yT = AR.alloc([128, 4, 128], BF16)
            junk = AR.alloc([128, 512], BF16)
            rs = AR.alloc([128, 4], F32)
            AR_tmp[0] = AR.alloc([128, D], F32)
            load_W(W, l, SEG["s5_z"][0], 512, "W")
            dma("pool", wglu, w_glu_d[l].rearrange("(k p) n -> p k n", p=128), [], ["wglu"])
            dma("pool", wout, w_out_d[l, 1024:1536, :].rearrange("(k p) n -> p k n", p=128), [], ["wout"])
            rowload(bgl, l, "b_glu", wname="bgl")
            rowload(nwr, l, "s5_nw", wname="nwr")
            for c in range(NCH):
                s_ = c % 2
                hnm = "hn%d" % s_
                load_hn(hn[s_], c, hnm)
                proj_TM(banks[2][:, :], hn[s_], hnm, W, "W", 0, 512, "b2z")
                act(sz, banks[2][:, :], AF.Silu, ["b2z"], ["sz"])
                for hf in range(2):
                    for kk in range(4):
                        mm(banks[3 + hf][:, :], gT[:, kk, c * 128:(c + 1) * 128], wglu[:, kk, hf * 512:(hf + 1) * 512],
                           kk == 0, kk == 3, ["gT", "wglu"], ["b%d" % (3 + hf)])
                    tt(gab[:, hf * 512:(hf + 1) * 512], banks[3 + hf][:, :], bgl[:, hf * 512:(hf + 1) * 512], ALU.add,
                       ["b%d" % (3 + hf), "bgl"], ["gab%d" % hf])
                act(gab[:, 512:1024], gab[:, 512:1024], AF.Sigmoid, ["gab1"], ["gab1"])
                tt(yv, gab[:, 0:512], gab[:, 512:1024], ALU.mult, ["gab0", "gab1"], ["yv"])
                rstd, rn = rms_scale(yv, "yv", 512, rs, junk, "s5n")
                stt(yv, yv, rstd, nwr, ALU.mult, ALU.mult, ["yv", rn, "nwr"], ["yv"])
                tt(ybf, yv, sz, ALU.mult, ["yv", "sz"], ["ybf"])
                finish(l, c, ybf, "ybf", 4, wout, "wout", yT, "s5")
            P.fence()

    AR.reset()
    fnw = AR.alloc([128, D], F32)
    junk = AR.alloc([128, D], BF16)
    ob = [AR.alloc([128, D], F32) for _ in range(2)]
    rs = [AR.alloc([128, 4], F32) for _ in range(2)]
    dma("sp", fnw, fnw_d[0, :].partition_broadcast(128), [], ["fnw"])
    outnames = []
    for c in range(NCH):
        s_ = c % 2
        rstd, rn = rms_scale(x_sb[:, c, :], "x%d" % c, D, rs[s_], junk, "fn%d" % s_)
        stt(ob[s_], x_sb[:, c, :], rstd, fnw, ALU.mult, ALU.mult, ["x%d" % c, rn, "fnw"], ["ob%d" % s_])
        dma("sp", out_d[c * 128:(c + 1) * 128, :], ob[s_], ["ob%d" % s_], ["out%d" % c])
        outnames.append("out%d" % c)
    P.add("sp", lambda e: e.nop(), outnames, ["done"])
    n = P.finalize(es)
    return nc, es, n


def _host_layout(inp, nlw=NL):
    f = np.float32
    g = {k: np.asarray(v) for k, v in inp.items()}
    w_in = g["w_in"]
    offs = np.cumsum([0, 512, 1024, 8, 512, 512, 512, 512, 512, 4, 4, 512, 512, 512, 256, 256, 512])
    (s_z, s_xbc, s_dt, m_z, m_q, m_k, m_v, m_o, m_i, m_f, c_z, c_u, r_z, r_q, r_k, r_v) = [
        (int(offs[i]), int(offs[i + 1])) for i in range(16)]
    W = np.zeros((NL, D, NCOLS), f)

    def put(name, off, src_lo, n):
        o = SEG[name][0] + off
        W[:, :, o:o + n] = w_in[:, :, src_lo:src_lo + n]
    put("ssd_z", 0, s_z[0], 512)
    put("ssd_xbc", 0, s_xbc[0], 1024)
    for h in range(2):
        nm = "ml%d" % h
        for j, sg in enumerate((m_z, m_q, m_k, m_v, m_o)):
            put(nm, j * 256, sg[0] + h * 256, 256)
        nm = "ret%d" % h
        put(nm, 0, r_z[0] + h * 256, 256)
        for hh in range(2):
            put(nm, 256 + hh * 128, r_q[0] + (2 * h + hh) * 64, 64)
            put(nm, 512 + hh * 128, r_k[0] + (2 * h + hh) * 64, 64)
        put(nm, 768, r_v[0] + h * 256, 256)
    put("s5_z", 0, c_z[0], 512)
    put("s5_u", 0, c_u[0], 512)
    put("misc", 0, s_dt[0], 8)
    put("misc", 8, m_i[0], 4)
    put("misc", 12, m_f[0], 4)
    rows = np.zeros((NL, NROW), f)

    def prow(name, arr):
        o, w = ROW[name]
        rows[:, o:o + w] = arr.reshape(NL, w)
    prow("norm_w", g["norm_w"]); prow("b_ada", g["b_ada"]); prow("dt_bias", g["ssd_dt_bias"])
    prow("a_log", g["ssd_a_log"]); prow("ssd_d", g["ssd_d"]); prow("ssd_nw", g["ssd_norm_w"])
    prow("ml_ib", g["ml_i_bias"]); prow("ml_fb", g["ml_f_bias"]); prow("ml_nw", g["ml_norm_w"])
    prow("b_glu", g["s5_b_glu"]); prow("s5_nw", g["s5_norm_w"]); prow("ret_nw", g["ret_norm_w"])
    prow("lam_re", g["s5_lambda_re"]); prow("lam_im", g["s5_lambda_im"])
    prow("lstep", np.repeat(g["s5_log_step"], 64, axis=1))
    cs = np.concatenate([g["ssd_conv_w"], g["ssd_conv_b"][:, None, :]], axis=1)
    convs = cs.reshape(NL, 5, 8, 128).transpose(0, 3, 2, 1).reshape(NL, 128, 40)
    cm = np.concatenate([g["ml_conv_w"], g["ml_conv_b"][:, None, :]], axis=1)
    cm = cm.reshape(NL, 5, 8, 128)
    order = [0, 1, 4, 5, 2, 3, 6, 7]
    convm = cm[:, :, order, :].transpose(0, 3, 2, 1).reshape(NL, 128, 40)
    s5col = np.zeros((NL, 128, 52), f)
    s5col[:, :, 0:4] = g["s5_d"].reshape(NL, 4, 128).transpose(0, 2, 1)
    s5col[:, :, 4:20] = g["s5_lambda_re"].reshape(NL, 16, 128).transpose(0, 2, 1)
    s5col[:, :, 20:36] = g["s5_lambda_im"].reshape(NL, 16, 128).transpose(0, 2, 1)
    s5col[:, :, 36:52] = np.repeat(g["s5_log_step"], 64, axis=1).reshape(NL, 16, 128).transpose(0, 2, 1)
    s5b = np.zeros((NL, 2, 4, 128, 512), f)
    s5c = np.zeros((NL, 2, 16, 128, 128), f)
    for ri, (bsrc, csrc) in enumerate(((g["s5_b_re"], g["s5_c_re"]), (g["s5_b_im"], g["s5_c_im"]))):
        for gi in range(32):
            k, gl = gi // 8, gi % 8
            s5b[:, ri, k, gl * 16:(gl + 1) * 16, gl * 64:(gl + 1) * 64] = bsrc[:, gi].transpose(0, 2, 1)
            jt, g2 = gi // 2, gi % 2
            s5c[:, ri, jt, g2 * 64:(g2 + 1) * 64, gl * 16:(gl + 1) * 16] = csrc[:, gi].transpose(0, 2, 1)
    cst = np.zeros((128, 128 * 5 + 32 + 512 + 8), f)
    ii = np.arange(128)
    cst[:, 0:128] = np.eye(128)
    cst[:, 128:256] = (ii[:, None] <= ii[None, :])
    cst[:, 256:384] = 1.0
    cst[:, 384:512] = np.where(ii[:, None] <= ii[None, :], 0.0, -30000.0)
    cst[:, 512:640] = np.eye(128)
    cst[:, 640:672] = np.exp(-math.log(10000.0) * np.arange(32, dtype=np.float64) / 32)[None, :]
    for h in range(4):
        lg = math.log1p(-2.0 ** -(5 + h))
        rel = ii[None, :] - ii[:, None]
        cst[:, 672 + h * 128:672 + (h + 1) * 128] = np.where(rel >= 0, np.exp(np.maximum(rel, 0) * lg), 0.0) * 64 ** -0.5
        cst[:, 1184 + h] = np.exp((ii + 1.0) * lg)
        cst[:, 1188 + h] = np.exp((127.0 - ii) * lg) * 64 ** -0.5
    shared = dict(w_in=W, w_out=np.ascontiguousarray(g["w_out"], f), w_ada=np.ascontiguousarray(g["w_ada"], f),
                  w_glu=np.ascontiguousarray(g["s5_w_glu"], f), rows=rows,
                  fnw=np.ascontiguousarray(g["final_norm_w"].reshape(1, D), f),
                  convs=np.ascontiguousarray(convs, f), convm=np.ascontiguousarray(convm, f), s5col=s5col,
                  s5b=s5b, s5c=s5c, cst=cst)
    if nlw < NL:
        shared = {k: (np.ascontiguousarray(v[:nlw]) if k not in ("fnw", "cst") else v) for k, v in shared.items()}
    maps = []
    for b in range(8):
        m = dict(shared)
        m["x"] = np.ascontiguousarray(g["x"][b], f)
        m["cT"] = np.ascontiguousarray(g["c"][b].reshape(8, 128).T, f)
        m["posT"] = np.ascontiguousarray(g["positions"][b].reshape(NCH, 128).T.astype(np.int32))
        maps.append(m)
    return maps


_CACHE = {}


def kernel(_nl=NL, _mix=(0, 1, 2, 3), **inputs):
    key = (_nl, tuple(_mix))
    if key not in _CACHE:
        nc, es, n = build(_nl, _mix)
        _CACHE[key] = nc
    nc = _CACHE[key]
    maps = _host_layout(inputs, max(_nl, 1))
    res = run_bass_kernel_spmd(nc, maps, core_ids=list(range(8)))
    out = np.stack([np.asarray(r["out"]).reshape(S, D) for r in res.results], axis=0)
    return out.astype(np.float32)
```

```python
import math
from contextlib import ExitStack
import numpy as np
import concourse.bass as bass
import concourse.mybir as mybir
from concourse.bass_utils import run_bass_kernel_spmd

F32 = mybir.dt.float32
BF16 = mybir.dt.bfloat16
I32 = mybir.dt.int32
AF = mybir.ActivationFunctionType
ALU = mybir.AluOpType
AX = mybir.AxisListType

NL = 4
D = 1024
S = 2048
T = 128
NCH = 16
EPS = 1e-6
SEG = {}
_o = 0
for _n, _w in [("ssd_z", 512), ("ssd_xbc", 1024),
               ("ml0", 1280), ("ml1", 1280),
               ("s5_z", 512), ("s5_u", 512),
               ("ret0", 1024), ("ret1", 1024),
               ("misc", 16)]:
    SEG[_n] = (_o, _w)
    _o += _w
NCOLS = _o
ROW = {}
_o = 0
for _n, _w in [("norm_w", 1024), ("b_ada", 3072), ("dt_bias", 8), ("a_log", 8), ("ssd_d", 8), ("ssd_nw", 512),
               ("ml_ib", 4), ("ml_fb", 4), ("ml_nw", 512), ("b_glu", 1024), ("s5_nw", 512), ("ret_nw", 512),
               ("lam_re", 2048), ("lam_im", 2048), ("lstep", 2048)]:
    ROW[_n] = (_o, _w)
    _o += _w
NROW = _o
ENGS = ["pe", "act", "dve", "pool", "sp"]


class Prog:
    def __init__(self, nc):
        self.nc = nc
        self.ops = []

    @staticmethod
    def _nz(names):
        out = []
        for x in names:
            if len(x) >= 2 and x[0] == "b" and x[1].isdigit() and (len(x) == 2 or not x[2].isalpha() or x[2] in "_abcdorvz"):
                x = x[:2]
            out.append(x)
        return tuple(out)

    def add(self, eng, fn, r=(), w=(), dma=False):
        self.ops.append((eng, fn, self._nz(r), self._nz(w), dma))

    def fence(self):
        self.ops.append(("FENCE", None, (), (), False))

    def finalize(self, es):
        nc = self.nc
        GEN = 30000
        NDS = 12
        cnt = {e: 0 for e in ENGS}
        dma_i = {e: 0 for e in ENGS}
        dma_cnt = {}
        memo = {e: {} for e in ENGS}
        lastw, readers = {}, {}
        last_tok = {}
        dma_toks = []
        pend = {e: [] for e in ENGS}
        plan = []
        semnames = set()
        for (eng, fn, r, w, dma) in self.ops:
            if eng == "FENCE":
                toks = list(last_tok.values()) + dma_toks
                dma_toks = []
                for e in ENGS:
                    pend[e] = list(toks)
                continue
            deps = set(pend[eng])
            pend[eng] = []
            for x in r:
                if x in lastw:
                    deps.add(lastw[x])
            for x in w:
                if x in lastw:
                    deps.add(lastw[x])
                for t in readers.get(x, ()):
                    deps.add(t)
            if dma:
                k = dma_i[eng] % NDS
                dma_i[eng] += 1
                sn = "d_%s_%d" % (eng, k)
                prev = dma_cnt.get(sn, 0)
                tok = (sn, prev + 16)
                dma_cnt[sn] = prev + 16
                if prev > 0:
                    deps.add((sn, prev))
                inc = 16
                dma_toks.append(tok)
            else:
                g = cnt[eng] // GEN
                sn = "e_%s_%d" % (eng, g)
                cnt[eng] += 1
                tok = (sn, cnt[eng] - g * GEN)
                inc = 1
                last_tok[eng] = tok
            semnames.add(sn)
            best = {}
            for (dsn, dv) in deps:
                if eng == "pe" and dsn.startswith("e_pe_"):
                    continue
                if memo[eng].get(dsn, 0) >= dv:
                    continue
                if best.get(dsn, 0) < dv:
                    best[dsn] = dv
            for dsn, dv in best.items():
                memo[eng][dsn] = dv
            plan.append((eng, fn, sorted(best.items()), sn, inc))
            for x in r:
                readers.setdefault(x, []).append(tok)
            for x in w:
                lastw[x] = tok
                readers[x] = []
        sems = {}
        for sn in sorted(semnames):
            sems[sn] = es.enter_context(nc.semaphore(sn))
        block = es.enter_context(nc.Block())

        def emit(engname, eobj):
            for (eng, fn, waits, sn, inc) in plan:
                if eng != engname:
                    continue
                for (dsn, dv) in waits:
                    eobj.wait_ge(sems[dsn], dv)
                ins = fn(eobj)
                ins.then_inc(sems[sn], inc)

        @block.tensor
        def _(e):
            emit("pe", e)

        @block.scalar
        def _(e):
            emit("act", e)

        @block.vector
        def _(e):
            emit("dve", e)

        @block.gpsimd
        def _(e):
            emit("pool", e)

        @block.sync
        def _(e):
            emit("sp", e)
        return len(plan)


class Arena:
    def __init__(self, ap, nbytes):
        self.ap = ap
        self.n = nbytes
        self.off = 0

    def reset(self, off=0):
        self.off = off

    def alloc(self, shape, dt):
        nel = 1
        for s_ in shape[1:]:
            nel *= s_
        nb = nel * (4 if dt in (F32, I32) else 2)
        nb = (nb + 31) // 32 * 32
        assert self.off + nb <= self.n, ("arena overflow", self.off, nb, self.n)
        v = self.ap[:, self.off // 4:(self.off + nb) // 4]
        self.off += nb
        if dt != F32:
            v = v.bitcast(dt)
        v = v[:, 0:nel]
        if len(shape) == 3:
            v = v.rearrange("p (a b) -> p a b", b=shape[2])
        elif len(shape) == 4:
            v = v.rearrange("p (a b c) -> p a b c", b=shape[2], c=shape[3])
        return v


def build(nl=NL, mix=(0, 1, 2, 3)):
    NLW = max(nl, 1)
    nc = bass.Bass("TRN2", target_bir_lowering=False)
    P = Prog(nc)
    es = ExitStack()

    def din(name, shape, dt=F32):
        return nc.dram_tensor(name, list(shape), dt, kind="ExternalInput").ap()
    x_d = din("x", [S, D])
    cT_d = din("cT", [128, 8])
    pos_d = din("posT", [128, NCH], I32)
    w_in_d = din("w_in", [NLW, D, NCOLS])
    w_out_d = din("w_out", [NLW, 2048, D])
    w_ada_d = din("w_ada", [NLW, D, 3072])
    w_glu_d = din("w_glu", [NLW, 512, 1024])
    rows_d = din("rows", [NLW, NROW])
    fnw_d = din("fnw", [1, D])
    convs_d = din("convs", [NLW, 128, 8 * 5])
    convm_d = din("convm", [NLW, 128, 8 * 5])
    s5col_d = din("s5col", [NLW, 128, 4 + 48])
    s5b_d = din("s5b", [NLW, 2, 4, 128, 512])
    s5c_d = din("s5c", [NLW, 2, 16, 128, 128])
    cst_d = din("cst", [128, 128 * 5 + 32 + 512 + 16])
    out_d = nc.dram_tensor("out", [S, D], F32, kind="ExternalOutput").ap()
    hnT_d = nc.dram_tensor("hnT_scr", [128, 8, S], BF16, kind="Internal").ap()

    def sb(name, shape, dt=F32):
        return es.enter_context(nc.sbuf_tensor(name, list(shape), dt))
    x_sb = sb("x_sb", [128, NCH, D])
    cst = sb("cst_sb", [128, 128 * 5 + 32 + 512 + 16])
    identb = sb("identb", [128, 128], BF16)
    Ub = sb("Ub", [128, 128], BF16)
    rope = sb("rope", [128, 2, NCH, 32])
    mrow = sb("mrow", [128, 3, D])
    condbc = sb("condbc", [128, 8, 128], BF16)
    ARB = 88 * 1024
    arena_t = sb("arena", [128, ARB // 4])
    AR = Arena(arena_t, ARB)
    banks = [es.enter_context(nc.psum_tensor("bank%d" % i, [128, 512], F32)) for i in range(8)]

    identF = cst[:, 0:128]
    Uf = cst[:, 128:256]
    onesf = cst[:, 256:384]
    maskneg = cst[:, 384:512]
    invf = cst[:, 640:672]
    iota_row = cst[:, 512:640]
    scol_s = cst[:, 1192:1193]
    scol_ns = cst[:, 1193:1194]
    retDT = cst[:, 672:1184].rearrange("p (h t) -> p h t", t=128)
    ret_ea = cst[:, 1184:1188]
    ret_wend = cst[:, 1188:1192]
    RET_DEC = [float((1.0 - 2.0 ** -(5 + h)) ** 128) for h in range(4)]

    def dma(eng, out, in_, r, w):
        P.add(eng, lambda e: e.dma_start(out=out, in_=in_), r, w, dma=True)

    def mm(out, lhsT, rhs, start, stop, r, w):
        P.add("pe", lambda e: e.matmul(out, lhsT, rhs, start=start, stop=stop), r, w)

    def tr(out, in_, r, w):
        P.add("pe", lambda e: e.transpose(out, in_, identb[:, :]), r + ["identb"], w)

    def act(out, in_, func, r, w, bias=None, scale=None, accum=None, eng="act"):
        kw = {}
        if bias is not None:
            kw["bias"] = bias
        if scale is not None:
            kw["scale"] = scale
        if accum is not None:
            kw["accum_out"] = accum
        P.add(eng, lambda e: e.activation(out, in_, func, **kw), r, w)

    def tt(out, in0, in1, op, r, w, eng="dve"):
        P.add(eng, lambda e: e.tensor_tensor(out, in0, in1, op), r, w)

    def ts(out, in0, s1, s2, op0, op1, r, w, eng="dve"):
        if op1 is None:
            P.add(eng, lambda e: e.tensor_scalar(out, in0, s1, None, op0), r, w)
        else:
            P.add(eng, lambda e: e.tensor_scalar(out, in0, s1, s2, op0, op1), r, w)

    def stt(out, in0, scalar, in1, op0, op1, r, w):
        P.add("dve", lambda e: e.scalar_tensor_tensor(out, in0, scalar, in1, op0, op1), r, w)

    def cp(out, in_, r, w, eng="dve"):
        if eng == "act":
            P.add(eng, lambda e: e.activation(out, in_, AF.Copy), r, w)
        else:
            P.add(eng, lambda e: e.tensor_copy(out, in_), r, w)

    def memset(ap, val, w, eng="dve"):
        P.add(eng, lambda e: e.memset(ap, val), [], w)

    def recip(out, in_, r, w):
        P.add("dve", lambda e: e.reciprocal(out, in_), r, w)

    def red(out, in_, r, w):
        P.add("dve", lambda e: e.tensor_reduce(out, in_, AX.X, ALU.add), r, w)

    def rowload(dst, l, name, r0=0, n=None, wname=None, eng="sp"):
        o, wd = ROW[name]
        n = wd if n is None else n
        dma(eng, dst, rows_d[l, o + r0:o + r0 + n].partition_broadcast(128), [], [wname])

    def sincos(sin_out, cos_out, ang, tmpa, tmpi, nm, shape_is3=False):
        ts(tmpa, ang, 1.0 / (2 * math.pi), None, ALU.mult, None, [nm + "ang"], [nm + "ta"])
        cp(tmpi, tmpa, [nm + "ta"], [nm + "ti"])
        cp(tmpa, tmpi, [nm + "ti"], [nm + "ta"])
        stt(tmpa, tmpa, -2 * math.pi, ang, ALU.mult, ALU.add, [nm + "ta", nm + "ang"], [nm + "ta"])
        for (o_, sh, on) in ((sin_out, 0.0, nm + "sin"), (cos_out, math.pi / 2, nm + "cos")):
            ts(o_, tmpa, sh, None, ALU.add, None, [nm + "ta"], [on])
            ts(tmpi.bitcast(F32), o_, math.pi, 2 * math.pi, ALU.is_gt, ALU.mult, [on], [nm + "ti"])
            tt(o_, o_, tmpi.bitcast(F32), ALU.subtract, [on, nm + "ti"], [on])
            ts(tmpi.bitcast(F32), o_, -math.pi, 2 * math.pi, ALU.is_lt, ALU.mult, [on], [nm + "ti"])
            tt(o_, o_, tmpi.bitcast(F32), ALU.add, [on, nm + "ti"], [on])
            act(o_, o_, AF.Sin, [on], [on])

    dma("sp", cst[:, :], cst_d[:, :], [], ["cst"])
    for c in range(NCH):
        dma("sp", x_sb[:, c, :], x_d[c * 128:(c + 1) * 128, :], [], ["x%d" % c])
    cp(identb[:, :], identF, ["cst"], ["identb"])
    cp(Ub[:, :], Uf, ["cst"], ["Ub"])
    AR.reset()
    cTs = AR.alloc([128, 8], F32)
    posi = AR.alloc([128, NCH], I32)
    posf = AR.alloc([128, NCH], F32)
    ang = AR.alloc([128, NCH, 32], F32)
    tmpa = AR.alloc([128, NCH, 32], F32)
    tmpi = AR.alloc([128, NCH, 32], I32)
    condb = AR.alloc([128, 8], BF16)
    dma("sp", cTs, cT_d[:, :], [], ["cTs"])
    dma("sp", posi, pos_d[:, :], [], ["posi"])
    act(condb, cTs, AF.Silu, ["cTs"], ["condb"])
    cp(condbc[:, :, :], condb.unsqueeze(2).to_broadcast([128, 8, 128]), ["condb"], ["condbc"])
    cp(posf, posi, ["posi"], ["posf"])
    tt(ang, invf.unsqueeze(1).to_broadcast([128, NCH, 32]), posf.unsqueeze(2).to_broadcast([128, NCH, 32]),
       ALU.mult, ["cst", "posf"], ["ropeang"])
    sincos(rope[:, 1, :, :], rope[:, 0, :, :], ang, tmpa, tmpi, "rope")
    P.fence()

    def load_W(W, l, col0, ncols, wname):
        src = w_in_d[l].rearrange("(k p) n -> p k n", p=128)
        c = 0
        while c < ncols:
            n = min(512, ncols - c)
            dma("pool", W[:, :, c:c + n], src[:, :, col0 + c:col0 + c + n], [], [wname])
            c += n

    def load_hn(buf, c, nm):
        dma("sp", buf, hnT_d[:, :, c * 128:(c + 1) * 128], ["hnT%d" % c], [nm])

    def proj_TM(ps, hn, hnm, W, wname, col0, ncols, pname):
        for k in range(8):
            mm(ps, hn[:, k, :], W[:, k, col0:col0 + ncols], k == 0, k == 7, [hnm, wname], [pname])

    def proj_FM(ps, hn, hnm, W, wname, col0, pname):
        for k in range(8):
            mm(ps, W[:, k, col0:col0 + 128], hn[:, k, :], k == 0, k == 7, [hnm, wname], [pname])

    def rms_scale(ycur, yname, n, rs, junk, nm):
        act(junk[:, 0:n], ycur, AF.Square, [yname], [nm + "junk", nm + "rs"], accum=rs[:, 0:1])
        ts(rs[:, 1:2], rs[:, 0:1], 1.0 / n, EPS, ALU.mult, ALU.add, [nm + "rs"], [nm + "rs1"])
        act(rs[:, 1:2], rs[:, 1:2], AF.Sqrt, [nm + "rs1"], [nm + "rs1"])
        recip(rs[:, 2:3], rs[:, 1:2], [nm + "rs1"], [nm + "rs2"])
        return rs[:, 2:3], nm + "rs2"

    def finish(l, c, ybf, ybname, nk, wout, woname, yT, nm):
        tb = banks[6][:, 0:256].bitcast(BF16).rearrange("p (a b) -> p a b", b=128)
        for k in range(nk):
            tr(tb[:, k, :], ybf[:, k * 128:(k + 1) * 128], [ybname], ["b6"])
        cp(yT[:, 0:nk, :], tb[:, 0:nk, :], ["b6"], [nm + "yT"], eng="act")
        for hf in range(2):
            for k in range(nk):
                mm(banks[hf][:, :], yT[:, k, :], wout[:, k, hf * 512:(hf + 1) * 512], k == 0, k == nk - 1,
                   [nm + "yT", woname], ["b%d" % hf])
        for hf in range(2):
            xs = x_sb[:, c, hf * 512:(hf + 1) * 512]
            tmp = AR_tmp[0][:, hf * 512:(hf + 1) * 512]
            tt(tmp, banks[hf][:, :], mrow[:, 2, hf * 512:(hf + 1) * 512], ALU.mult, ["b%d" % hf, "mrow"], ["fin_tmp%d" % hf])
            tt(xs, xs, tmp, ALU.add, ["fin_tmp%d" % hf, "x%d" % c], ["x%d" % c], eng="pool")

    AR_tmp = [None]

    def attn_chunk(c, H, dv, qT, kT, kTM, v, DT, ea, wend, dec, St, Sb, yout, ops, nm, kscale=None):
        grp = max(1, 512 // dv)
        mT, kw = ops["mT"], ops["kw"]
        for h in range(H):
            (kta, ktn) = kTM(h)
            tt(kw[:, h, :], kta, wend[0][:, h:h + 1].to_broadcast([128, 128]), ALU.mult, ktn + wend[1], [nm + "kw%d" % h])
        for g0 in range(0, H, grp):
            hs = list(range(g0, min(H, g0 + grp)))
            for h in hs:
                j = h - g0
                sslot = h % 4
                scp = banks[2][:, sslot * 128:(sslot + 1) * 128]
                (qa, qn) = qT(h)
                (ka, kn) = kT(h)
                mm(scp, ka, qa, True, True, qn + kn, ["b2_%d" % sslot])
                (da, dn) = DT(h)
                ms = mT[:, h % 2, :]
                tt(ms, scp, da, ALU.mult, ["b2_%d" % sslot] + dn, [nm + "mT%d" % (h % 2)])
                (va, vn) = v(h)
                mm(banks[3][:, j * dv:(j + 1) * dv], ms, va, True, True, [nm + "mT%d" % (h % 2)] + vn, ["b3"])
                if c > 0:
                    mm(banks[4][:, j * dv:(j + 1) * dv], qa, Sb[:, h, :], True, True, qn + [nm + "Sb%d" % h], ["b4"])
                stslot = h % 2
                stp = banks[5][:, stslot * 256:stslot * 256 + dv]
                mm(stp, kw[:, h, :], va, True, True, [nm + "kw%d" % h] + vn, ["b5_%d" % stslot])
                if c == 0:
                    cp(St[:, h, :], stp, ["b5_%d" % stslot], [nm + "St%d" % h])
                else:
                    d_ = dec(h)
                    if isinstance(d_, float):
                        stt(St[:, h, :], St[:, h, :], d_, stp, ALU.mult, ALU.add,
                            ["b5_%d" % stslot, nm + "St%d" % h], [nm + "St%d" % h])
                    else:
                        stt(St[:, h, :], St[:, h, :], d_[0], stp, ALU.mult, ALU.add,
                            ["b5_%d" % stslot, nm + "St%d" % h] + d_[1], [nm + "St%d" % h])
                if c < NCH - 1:
                    cp(Sb[:, h, :], St[:, h, :], [nm + "St%d" % h], [nm + "Sb%d" % h], eng="act")
            n = len(hs)
            yv = yout[:, g0:g0 + n, :]
            b3v = banks[3][:, 0:n * dv].rearrange("p (h d) -> p h d", d=dv)
            if c > 0:
                b4v = banks[4][:, 0:n * dv].rearrange("p (h d) -> p h d", d=dv)
                tt(yv, b4v, ea[0][:, g0:g0 + n].unsqueeze(2).to_broadcast([128, n, dv]), ALU.mult,
                   ["b4"] + ea[1], [nm + "yout"])
                tt(yv, yv, b3v, ALU.add, ["b3", nm + "yout"], [nm + "yout"])
            else:
                cp(yv, b3v, ["b3"], [nm + "yout"])

    for l in range(nl):
        AR.reset()
        Wa = [AR.alloc([128, 8, 512], BF16) for _ in range(2)]
        brow = [AR.alloc([128, 512], F32) for _ in range(2)]
        nwrow = AR.alloc([128, D], F32)
        src = w_ada_d[l].rearrange("(k p) n -> p k n", p=128)
        rowload(nwrow, l, "norm_w", wname="nwrow")
        for blk in range(6):
            s_ = blk % 2
            dma("pool", Wa[s_], src[:, :, blk * 512:(blk + 1) * 512], [], ["Wa%d" % s_])
            rowload(brow[s_], l, "b_ada", blk * 512, 512, "brow%d" % s_)
            pb = banks[blk % 2]
            for k in range(8):
                mm(pb[:, :], condbc[:, k, :], Wa[s_][:, k, :], k == 0, k == 7, ["condbc", "Wa%d" % s_], ["b%d" % (blk % 2)])
            part = [0, 0, 1, 1, 2, 2][blk]
            tt(mrow[:, part, (blk % 2) * 512:(blk % 2 + 1) * 512], pb[:, :], brow[s_], ALU.add,
               ["b%d" % (blk % 2), "brow%d" % s_], ["mrow"])
        stt(mrow[:, 1, :], mrow[:, 1, :], 1.0, nwrow, ALU.add, ALU.mult, ["mrow", "nwrow"], ["mrow"])
        P.fence()
        AR.reset()
        junk = AR.alloc([128, D], BF16)
        tmp1 = [AR.alloc([128, D], F32) for _ in range(2)]
        hnb = [AR.alloc([128, D], BF16) for _ in range(2)]
        hnTs = [AR.alloc([128, 8, 128], BF16) for _ in range(2)]
        rs = [AR.alloc([128, 4], F32) for _ in range(2)]
        for c in range(NCH):
            s_ = c % 2
            nm = "p1_%d" % s_
            rstd, rn = rms_scale(x_sb[:, c, :], "x%d" % c, D, rs[s_], junk, nm)
            stt(tmp1[s_], x_sb[:, c, :], rstd, mrow[:, 1, :], ALU.mult, ALU.mult, ["x%d" % c, rn, "mrow"], [nm + "t"])
            tt(hnb[s_], tmp1[s_], mrow[:, 0, :], ALU.add, [nm + "t", "mrow"], [nm + "hn"], eng="pool")
            tb = banks[6 + s_][:, :].bitcast(BF16).rearrange("p (a b) -> p a b", b=128)
            for k in range(8):
                tr(tb[:, k, :], hnb[s_][:, k * 128:(k + 1) * 128], [nm + "hn"], ["b%d" % (6 + s_)])
            cp(hnTs[s_], tb, ["b%d" % (6 + s_)], [nm + "hnT"], eng="act")
            dma("sp", hnT_d[:, :, c * 128:(c + 1) * 128], hnTs[s_], [nm + "hnT"], ["hnT%d" % c])
        P.fence()

        if 0 in mix:
            AR.reset()
            W = AR.alloc([128, 8, 1536], BF16)
            Wm = AR.alloc([128, 8, 16], BF16)
            wout = AR.alloc([128, 4, D], BF16)
            hn = [AR.alloc([128, 8, 128], BF16) for _ in range(2)]
            raw = AR.alloc([128, 8, 131], F32)
            cv = AR.alloc([128, 128], F32)
            xbcT = AR.alloc([128, 8, 128], BF16)
            xTM = AR.alloc([128, 512], F32)
            BTM = AR.alloc([128, 2, 128], BF16)
            sz = AR.alloc([128, 512], F32)
            sm = AR.alloc([128, 96], F32)
            prm = AR.alloc([128, 32], F32)
            cprm = AR.alloc([128, 40], F32)
            nwr = AR.alloc([128, 512], F32)
            dabc = AR.alloc([128, 8, 128], F32)
            DTt = AR.alloc([128, 8, 128], F32)
            mT = AR.alloc([128, 2, 128], BF16)
            kw = AR.alloc([128, 8, 128], BF16)
            xd = AR.alloc([128, 8, 64], BF16)
            St = AR.alloc([128, 8, 64], F32)
            Sb = AR.alloc([128, 8, 64], BF16)
            yo = AR.alloc([128, 8, 64], F32)
            t3 = AR.alloc([128, 8, 64], F32)
            ybf = AR.alloc([128, 512], BF16)
            yT = AR.alloc([128, 4, 128], BF16)
            junk = AR.alloc([128, 512], BF16)
            rs = AR.alloc([128, 4], F32)
            AR_tmp[0] = AR.alloc([128, D], F32)
            load_W(W, l, SEG["ssd_z"][0], 1536, "W")
            load_W(Wm, l, SEG["misc"][0], 16, "Wm")
            dma("pool", wout, w_out_d[l, 0:512, :].rearrange("(k p) n -> p k n", p=128), [], ["wout"])
            rowload(prm[:, 0:8], l, "dt_bias", wname="prm")
            rowload(prm[:, 8:16], l, "a_log", wname="prm")
            rowload(prm[:, 16:24], l, "ssd_d", wname="prm")
            rowload(nwr, l, "ssd_nw", wname="nwr")
            dma("sp", cprm, convs_d[l], [], ["cprm"])
            act(prm[:, 8:16], prm[:, 8:16], AF.Exp, ["prm"], ["prm"])
            ts(prm[:, 8:16], prm[:, 8:16], -1.0, None, ALU.mult, None, ["prm"], ["prm"])
            memset(raw[:, :, 0:3], 0.0, ["raw"])
            cpv = cprm.rearrange("p (t k) -> p t k", k=5)
            for c in range(NCH):
                s_ = c % 2
                hnm = "hn%d" % s_
                load_hn(hn[s_], c, hnm)
                proj_TM(banks[0][:, :], hn[s_], hnm, W, "W", 0, 512, "b0")
                act(sz, banks[0][:, :], AF.Silu, ["b0"], ["sz"])
                proj_TM(banks[7][:, 0:8], hn[s_], hnm, Wm, "Wm", 0, 8, "b7a")
                tt(sm[:, 0:8], banks[7][:, 0:8], prm[:, 0:8], ALU.add, ["b7a", "prm"], ["sm_dt"])
                ts(sm[:, 0:8], sm[:, 0:8], 30.0, None, ALU.min, None, ["sm_dt"], ["sm_dt"])
                act(sm[:, 0:8], sm[:, 0:8], AF.Exp, ["sm_dt"], ["sm_dt"])
                act(sm[:, 0:8], sm[:, 0:8], AF.Ln, ["sm_dt"], ["sm_dt"], bias=1.0)
                tt(sm[:, 8:16], sm[:, 0:8], prm[:, 8:16], ALU.mult, ["sm_dt", "prm"], ["sm_da"])
                for t_ in range(8):
                    pp = banks[1][:, (t_ % 4) * 128:(t_ % 4 + 1) * 128]
                    pn = "b1_%d" % (t_ % 4)
                    proj_FM(pp, hn[s_], hnm, W, "W", 512 + t_ * 128, pn)
                    cp(raw[:, t_, 3:131], pp, [pn], ["raw"], eng="act")
                    ts(cv, raw[:, t_, 0:128], cpv[:, t_, 0:1], cpv[:, t_, 4:5], ALU.mult, ALU.add, ["raw", "cprm"], ["cv"])
                    for k_ in range(1, 4):
                        stt(cv, raw[:, t_, k_:k_ + 128], cpv[:, t_, k_:k_ + 1], cv, ALU.mult, ALU.add, ["raw", "cprm", "cv"], ["cv"])
                    act(xbcT[:, t_, :], cv, AF.Silu, ["cv"], ["xbcT%d" % t_])
                cp(raw[:, :, 0:3], raw[:, :, 128:131], ["raw"], ["raw"], eng="pool")
                tb = banks[6][:, :].bitcast(BF16).rearrange("p (a b) -> p a b", b=128)
                for t_ in range(6):
                    tr(tb[:, t_, :], xbcT[:, t_, :], ["xbcT%d" % t_], ["b6"])
                cp(xTM, tb[:, 0:4, :].rearrange("p a b -> p (a b)"), ["b6"], ["xTM"], eng="act")
                cp(BTM, tb[:, 4:6, :], ["b6"], ["BTM"], eng="act")
                mm(banks[7][:, 16:24], Uf, sm[:, 8:16], True, True, ["cst", "sm_da"], ["b7b"])
                mm(banks[7][:, 24:32], onesf, sm[:, 8:16], True, True, ["cst", "sm_da"], ["b7c"])
                cp(sm[:, 16:24], banks[7][:, 16:24], ["b7b"], ["sm_ac"])
                act(sm[:, 24:32], banks[7][:, 16:24], AF.Exp, ["b7b"], ["sm_ea"])
                tt(sm[:, 32:40], banks[7][:, 24:32], sm[:, 16:24], ALU.subtract, ["b7c", "sm_ac"], ["sm_we"])
                act(sm[:, 32:40], sm[:, 32:40], AF.Exp, ["sm_we"], ["sm_we"])
                act(sm[:, 40:48], banks[7][:, 24:32], AF.Exp, ["b7c"], ["sm_dec"])
                ts(sm[:, 48:56], sm[:, 16:24], -1.0, None, ALU.mult, None, ["sm_ac"], ["sm_nac"])
                cp(dabc, sm[:, 8:16].unsqueeze(2).to_broadcast([128, 8, 128]), ["sm_da"], ["dabc"], eng="pool")
                for h in range(8):
                    rp = banks[7][:, 128 + (h % 3) * 128:256 + (h % 3) * 128]
                    rn = "b7r%d" % (h % 3)
                    mm(rp, dabc[:, h, :], Uf, True, True, ["dabc", "cst"], [rn])
                    tt(DTt[:, h, :], rp, maskneg, ALU.add, [rn, "cst"], ["DT%d" % h])
                    act(DTt[:, h, :], DTt[:, h, :], AF.Exp, ["DT%d" % h, "sm_nac"], ["DT%d" % h], bias=sm[:, 48 + h:49 + h])
                xv = xTM.rearrange("p (h d) -> p h d", d=64)
                tt(xd, xv, sm[:, 0:8].unsqueeze(2).to_broadcast([128, 8, 64]), ALU.mult, ["xTM", "sm_dt"], ["xd"])
                attn_chunk(c, 8, 64,
                           lambda h: (xbcT[:, 6 + h // 4, :], ["xbcT%d" % (6 + h // 4)]),
                           lambda h: (xbcT[:, 4 + h // 4, :], ["xbcT%d" % (4 + h // 4)]),
                           lambda h: (BTM[:, h // 4, :], ["BTM"]),
                           lambda h: (xd[:, h, :], ["xd"]),
                           lambda h: (DTt[:, h, :], ["DT%d" % h]),
                           (sm[:, 24:32], ["sm_ea"]), (sm[:, 32:40], ["sm_we"]),
                           lambda h: (sm[:, 40 + h:41 + h], ["sm_dec"]),
                           St, Sb, yo, {"mT": mT, "kw": kw}, "ssd")
                tt(t3, xv, prm[:, 16:24].unsqueeze(2).to_broadcast([128, 8, 64]), ALU.mult, ["xTM", "prm"], ["t3"], eng="pool")
                yf = yo.rearrange("p h d -> p (h d)")
                tt(yf, yf, t3.rearrange("p h d -> p (h d)"), ALU.add, ["ssdyout", "t3"], ["ssdyout"])
                tt(yf, yf, sz, ALU.mult, ["ssdyout", "sz"], ["ssdyout"])
                rstd, rn = rms_scale(yf, "ssdyout", 512, rs, junk, "ssdn")
                stt(ybf, yf, rstd, nwr, ALU.mult, ALU.mult, ["ssdyout", rn, "nwr"], ["ybf"])
                finish(l, c, ybf, "ybf", 4, wout, "wout", yT, "ssd")
            P.fence()

        for kind in (1, 3):
            if kind not in mix:
                continue
            for half in range(2):
                isml = kind == 1
                AR.reset()
                ncol = 1280 if isml else 1024
                seg = SEG[("ml%d" if isml else "ret%d") % half][0]
                W = AR.alloc([128, 8, ncol], BF16)
                Wm = AR.alloc([128, 8, 16], BF16)
                wout = AR.alloc([128, 2, D], BF16)
                hn = [AR.alloc([128, 8, 128], BF16) for _ in range(2)]
                raw = AR.alloc([128, 4, 131], F32)
                cv = AR.alloc([128, 128], F32)
                qkT = AR.alloc([128, 4, 128], BF16)
                kTMt = AR.alloc([128, 2, 128], BF16)
                qkrot = AR.alloc([128, 4, 128], BF16)
                rt = AR.alloc([128, 4, 2, 32], F32)
                rt2 = AR.alloc([128, 4, 2, 32], F32)
                sz = AR.alloc([128, 256], F32)
                so = AR.alloc([128, 256], F32)
                vb = AR.alloc([128, 2, 129], BF16)
                sm = AR.alloc([128, 64], F32)
                prm = AR.alloc([128, 16], F32)
                cprm = AR.alloc([128, 40], F32)
                nwr = AR.alloc([128, 256], F32)
                dabc = AR.alloc([128, 2, 128], F32)
                DTt = AR.alloc([128, 2, 128], F32)
                mT = AR.alloc([128, 2, 128], BF16)
                kw = AR.alloc([128, 2, 128], BF16)
                St = AR.alloc([128, 2, 129], F32)
                Sb = AR.alloc([128, 2, 129], BF16)
                yo = AR.alloc([128, 2, 129], F32)
                hh = AR.alloc([128, 2, 128], F32)
                sq = AR.alloc([128, 2, 128], F32)
                ybf = AR.alloc([128, 256], BF16)
                yT = AR.alloc([128, 4, 128], BF16)
                AR_tmp[0] = AR.alloc([128, D], F32)
                dv = 129 if isml else 128
                load_W(W, l, seg, ncol, "W")
                r0 = (512 if isml else 1536) + half * 256
                dma("pool", wout, w_out_d[l, r0:r0 + 256, :].rearrange("(k p) n -> p k n", p=128), [], ["wout"])
                rowload(nwr, l, "ml_nw" if isml else "ret_nw", half * 256, 256, "nwr")
                if isml:
                    load_W(Wm, l, SEG["misc"][0], 16, "Wm")
                    rowload(prm[:, 0:2], l, "ml_ib", half * 2, 2, "prm")
                    rowload(prm[:, 2:4], l, "ml_fb", half * 2, 2, "prm")
                    dma("sp", cprm, convm_d[l], [], ["cprm"])
                    memset(raw[:, :, 0:3], 0.0, ["raw"])
                    memset(vb[:, :, 128:129], 1.0, ["vb"])
                else:
                    memset(qkrot, 0.0, ["qkrot"])
                cpv = cprm.rearrange("p (t k) -> p t k", k=5)
                KS = 128.0 ** -0.5
                for c in range(NCH):
                    s_ = c % 2
                    hnm = "hn%d" % s_
                    load_hn(hn[s_], c, hnm)
                    proj_TM(banks[0][:, 0:256], hn[s_], hnm, W, "W", 0, 256, "b0")
                    act(sz, banks[0][:, 0:256], AF.Silu, ["b0"], ["sz"])
                    proj_TM(banks[0][:, 256:512], hn[s_], hnm, W, "W", 768, 256, "b0v")
                    cp(vb[:, :, 0:128], banks[0][:, 256:512].rearrange("p (h d) -> p h d", d=128), ["b0v"], ["vb"], eng="act")
                    if isml:
                        proj_TM(banks[1][:, 0:256], hn[s_], hnm, W, "W", 1024, 256, "b1o")
                        act(so, banks[1][:, 0:256], AF.Sigmoid, ["b1o"], ["so"])
                        proj_TM(banks[7][:, 0:2], hn[s_], hnm, Wm, "Wm", 8 + 2 * half, 2, "b7a")
                        proj_TM(banks[7][:, 2:4], hn[s_], hnm, Wm, "Wm", 12 + 2 * half, 2, "b7a2")
                        tt(sm[:, 0:2], banks[7][:, 0:2], prm[:, 0:2], ALU.add, ["b7a", "prm"], ["sm_i"])
                        tt(sm[:, 2:4], banks[7][:, 2:4], prm[:, 2:4], ALU.add, ["b7a2", "prm"], ["sm_f"])
                        ts(sm[:, 2:4], sm[:, 2:4], -30.0, None, ALU.max, None, ["sm_f"], ["sm_f"])
                        act(sm[:, 2:4], sm[:, 2:4], AF.Exp, ["sm_f"], ["sm_f"], scale=-1.0)
                        act(sm[:, 2:4], sm[:, 2:4], AF.Ln, ["sm_f"], ["sm_f"], bias=1.0)
                        ts(sm[:, 2:4], sm[:, 2:4], -1.0, None, ALU.mult, None, ["sm_f"], ["sm_f"])
                        for t_ in range(4):
                            pp = banks[1][:, 256 + (t_ % 2) * 128:384 + (t_ % 2) * 128]
                            pn = "b1_%d" % (t_ % 2)
                            proj_FM(pp, hn[s_], hnm, W, "W", 256 + t_ * 128, pn)
                            cp(raw[:, t_, 3:131], pp, [pn], ["raw"], eng="act")
                            ct = half * 4 + t_
                            ts(cv, raw[:, t_, 0:128], cpv[:, ct, 0:1], cpv[:, ct, 4:5], ALU.mult, ALU.add, ["raw", "cprm"], ["cv"])
                            for k_ in range(1, 4):
                                stt(cv, raw[:, t_, k_:k_ + 128], cpv[:, ct, k_:k_ + 1], cv, ALU.mult, ALU.add, ["raw", "cprm", "cv"], ["cv"])
                            act(qkT[:, t_, :], cv, AF.Silu, ["cv"], ["qkT%d" % t_])
                        cp(raw[:, :, 0:3], raw[:, :, 128:131], ["raw"], ["raw"], eng="pool")
                        tb = banks[6][:, :].bitcast(BF16).rearrange("p (a b) -> p a b", b=128)
                        for t_ in range(2):
                            tr(tb[:, t_, :], qkT[:, 2 + t_, :], ["qkT%d" % (2 + t_)], ["b6"])
                        cp(kTMt, tb[:, 0:2, :], ["b6"], ["kTM"], eng="act")
                        mm(banks[7][:, 16:18], Uf, sm[:, 2:4], True, True, ["cst", "sm_f"], ["b7b"])
                        mm(banks[7][:, 24:26], onesf, sm[:, 2:4], True, True, ["cst", "sm_f"], ["b7c"])
                        cp(sm[:, 16:18], banks[7][:, 16:18], ["b7b"], ["sm_ac"])
                        act(sm[:, 24:26], banks[7][:, 16:18], AF.Exp, ["b7b"], ["sm_ea"])
                        tt(sm[:, 32:34], banks[7][:, 24:26], sm[:, 16:18], ALU.subtract, ["b7c", "sm_ac"], ["sm_we"])
                        tt(sm[:, 32:34], sm[:, 32:34], sm[:, 0:2], ALU.add, ["sm_we", "sm_i"], ["sm_we"])
                        act(sm[:, 32:34], sm[:, 32:34], AF.Exp, ["sm_we"], ["sm_we"])
                        ts(sm[:, 32:34], sm[:, 32:34], KS, None, ALU.mult, None, ["sm_we"], ["sm_we"])
                        act(sm[:, 40:42], banks[7][:, 24:26], AF.Exp, ["b7c"], ["sm_dec"])
                        tt(sm[:, 48:50], sm[:, 0:2], sm[:, 16:18], ALU.subtract, ["sm_i", "sm_ac"], ["sm_nac"])
                        cp(dabc, sm[:, 2:4].unsqueeze(2).to_broadcast([128, 2, 128]), ["sm_f"], ["dabc"], eng="pool")
                        for h in range(2):
                            rp = banks[7][:, 128 + h * 128:256 + h * 128]
                            rn = "b7r%d" % h
                            mm(rp, dabc[:, h, :], Uf, True, True, ["dabc", "cst"], [rn])
                            tt(DTt[:, h, :], rp, maskneg, ALU.add, [rn, "cst"], ["DT%d" % h])
                            act(DTt[:, h, :], DTt[:, h, :], AF.Exp, ["DT%d" % h, "sm_nac"], ["DT%d" % h], bias=sm[:, 48 + h:49 + h])
                            ts(DTt[:, h, :], DTt[:, h, :], KS, None, ALU.mult, None, ["DT%d" % h], ["DT%d" % h])
                        qTf = lambda h: (qkT[:, h, :], ["qkT%d" % h])
                        kTf = lambda h: (qkT[:, 2 + h, :], ["qkT%d" % (2 + h)])
                        DTf = lambda h: (DTt[:, h, :], ["DT%d" % h])
                        eaf = (sm[:, 24:26], ["sm_ea"])
                        wef = (sm[:, 32:34], ["sm_we"])
                        decf = lambda h: (sm[:, 40 + h:41 + h], ["sm_dec"])
                    else:
                        proj_TM(banks[1][:, 0:512], hn[s_], hnm, W, "W", 256, 512, "b1o")
                        qv = banks[1][:, 0:512].rearrange("p (a d) -> p a d", d=128)
                        cosb = rope[:, 0, c, :].unsqueeze(1).to_broadcast([128, 4, 32])
                        sinb = rope[:, 1, c, :].unsqueeze(1).to_broadcast([128, 4, 32])
                        x1 = qv[:, :, 0:32]
                        x2 = qv[:, :, 32:64]
                        tt(rt[:, :, 0, :], x1, cosb, ALU.mult, ["b1o"], ["rt"])
                        tt(rt[:, :, 1, :], x2, cosb, ALU.mult, ["b1o"], ["rt"])
                        tt(rt2[:, :, 0, :], x2, sinb, ALU.mult, ["b1o"], ["rt2"])
                        tt(rt2[:, :, 1, :], x1, sinb, ALU.mult, ["b1o"], ["rt2"])
                        tt(qkrot[:, :, 0:32], rt[:, :, 0, :], rt2[:, :, 0, :], ALU.subtract, ["rt", "rt2"], ["qkrot"])
                        tt(qkrot[:, :, 32:64], rt[:, :, 1, :], rt2[:, :, 1, :], ALU.add, ["rt", "rt2"], ["qkrot"])
                        tb = banks[6][:, :].bitcast(BF16).rearrange("p (a b) -> p a b", b=128)
                        for t_ in range(4):
                            tr(tb[:, t_, :], qkrot[:, t_, :], ["qkrot"], ["b6"])
                        cp(qkT, tb[:, 0:4, :], ["b6"], ["qkT0", "qkT1", "qkT2", "qkT3"], eng="act")
                        qTf = lambda h: (qkT[:, h, :], ["qkT%d" % h])
                        kTf = lambda h: (qkT[:, 2 + h, :], ["qkT%d" % (2 + h)])
                        DTf = lambda h: (retDT[:, 2 * half + h, :], ["cst"])
                        eaf = (ret_ea[:, 2 * half:2 * half + 2], ["cst"])
                        wef = (ret_wend[:, 2 * half:2 * half + 2], ["cst"])
                        decf = lambda h: RET_DEC[2 * half + h]
                    if isml:
                        kTMf = lambda h: (kTMt[:, h, :], ["kTM"])
                    else:
                        kTMf = lambda h: (qkrot[:, 2 + h, :], ["qkrot"])
                    attn_chunk(c, 2, dv, qTf, kTf, kTMf,
                               lambda h: (vb[:, h, 0:dv], ["vb"]),
                               DTf, eaf, wef, decf, St[:, :, 0:dv], Sb[:, :, 0:dv], yo[:, :, 0:dv],
                               {"mT": mT, "kw": kw}, "at")
                    if isml:
                        dn_ = yo[:, :, 128:129].rearrange("p h d -> p (h d)")
                        stt(sm[:, 56:58], dn_, -1.0, dn_, ALU.mult, ALU.max, ["atyout"], ["sm_den"])
                        ts(sm[:, 56:58], sm[:, 56:58], 1.0, None, ALU.max, None, ["sm_den"], ["sm_den"])
                        recip(sm[:, 56:58], sm[:, 56:58], ["sm_den"], ["sm_den"])
                        tt(hh, yo[:, :, 0:128], sm[:, 56:58].unsqueeze(2).to_broadcast([128, 2, 128]), ALU.mult, ["atyout", "sm_den"], ["hh"])
                        tt(hh, hh, so.rearrange("p (h d) -> p h d", d=128), ALU.mult, ["hh", "so"], ["hh"])
                    else:
                        cp(hh, yo[:, :, 0:128], ["atyout"], ["hh"], eng="pool")
                    tt(sq, hh, hh, ALU.mult, ["hh"], ["sq"], eng="pool")
                    red(sm[:, 58:60], sq, ["sq"], ["sm_ss"])
                    ts(sm[:, 58:60], sm[:, 58:60], 1.0 / 128, EPS, ALU.mult, ALU.add, ["sm_ss"], ["sm_ss"])
                    act(sm[:, 58:60], sm[:, 58:60], AF.Sqrt, ["sm_ss"], ["sm_ss"])
                    recip(sm[:, 58:60], sm[:, 58:60], ["sm_ss"], ["sm_ss"])
                    tt(hh, hh, sm[:, 58:60].unsqueeze(2).to_broadcast([128, 2, 128]), ALU.mult, ["hh", "sm_ss"], ["hh"])
                    hf_ = hh.rearrange("p h d -> p (h d)")
                    tt(hf_, hf_, nwr, ALU.mult, ["hh", "nwr"], ["hh"])
                    tt(ybf, hf_, sz, ALU.mult, ["hh", "sz"], ["ybf"])
                    finish(l, c, ybf, "ybf", 2, wout, "wout", yT, "at")
                P.fence()

        if 2 in mix:
            AR.reset()
            gT = AR.alloc([128, 4, S], BF16)
            keep = AR.off
            uT = AR.alloc([128, S], BF16)
            Wu = AR.alloc([128, 8, 128], BF16)
            bb = AR.alloc([128, 2, 512], BF16)
            Cp = AR.alloc([128, 2, 4, 128], BF16)
            hb = [AR.alloc([128, 8, 256], BF16) for _ in range(2)]
            diagD = AR.alloc([128, 4, 128], BF16)
            scol = AR.alloc([128, 52], F32)
            cw = AR.alloc([128, 9, 16], F32)
            cwi = AR.alloc([128, 16], I32)
            ca = AR.alloc([128, 2, 16], F32)
            scr = AR.alloc([128, 14, 512], F32)
            TA = AR.alloc([128, 2, 512], F32)
            Ft = AR.alloc([128, 3, 512], F32)
            bup = [AR.alloc([128, 2, 512], BF16) for _ in range(2)]
            xbf = AR.alloc([128, 2, 512], BF16)
            carry = AR.alloc([128, 2, 4], F32)
            ct = AR.alloc([128, 8, 4], F32)
            ysb = AR.alloc([128, 512], F32)
            gt_ = AR.alloc([128, 512], F32)
            dma("sp", scol, s5col_d[l], [], ["scol"])
            for k in range(4):
                ts(diagD[:, k, :], identF, scol[:, k:k + 1], None, ALU.mult, None, ["cst", "scol"], ["diagD"])
            lre, lim, lst = scol[:, 4:20], scol[:, 20:36], scol[:, 36:52]
            act(cw[:, 0, :], lst, AF.Exp, ["scol"], ["cw0"])
            ts(cw[:, 1, :], lre, -1e-4, None, ALU.min, None, ["scol"], ["cw1"])
            tt(cw[:, 7, :], cw[:, 1, :], cw[:, 0, :], ALU.mult, ["cw0", "cw1"], ["cw7"])
            tt(cw[:, 3, :], lim, cw[:, 0, :], ALU.mult, ["cw0", "scol"], ["cAang"])
            act(cw[:, 2, :], cw[:, 7, :], AF.Exp, ["cw7"], ["cw2"])
            sincos(cw[:, 4, :], cw[:, 5, :], cw[:, 3, :], cw[:, 6, :], cwi, "cA")
            tt(ca[:, 0, :], cw[:, 2, :], cw[:, 5, :], ALU.mult, ["cw2", "cAcos"], ["ca"])
            tt(ca[:, 1, :], cw[:, 2, :], cw[:, 4, :], ALU.mult, ["cw2", "cAsin"], ["ca"])
            R = [scr[:, i, :] for i in range(14)]
            for k in range(4):
                load_W(Wu, l, SEG["s5_u"][0] + k * 128, 128, "Wu")
                for tb_ in range(8):
                    s_ = tb_ % 2
                    dma("sp", hb[s_], hnT_d[:, :, tb_ * 256:(tb_ + 1) * 256],
                        ["hnT%d" % (2 * tb_), "hnT%d" % (2 * tb_ + 1)], ["hb%d" % s_])
                    pp = banks[s_][:, 0:256]
                    for kk in range(8):
                        mm(pp, Wu[:, kk, :], hb[s_][:, kk, :], kk == 0, kk == 7, ["Wu", "hb%d" % s_], ["b%d" % s_])
                    cp(uT[:, tb_ * 256:(tb_ + 1) * 256], pp, ["b%d" % s_], ["uT"], eng="act")
                RN = lambda i: ["R%d" % i]
                for i_, nm_ in enumerate(("lam_re", "lam_im", "lstep")):
                    rowload(R[i_], l, nm_, k * 512, 512, "R%d" % i_)
                dma("sp", R[3], s5b_d[l, 0, k], [], RN(3))
                dma("sp", R[4], s5b_d[l, 1, k], [], RN(4))
                act(R[2], R[2], AF.Exp, RN(2), RN(2))
                ts(R[0], R[0], -1e-4, None, ALU.min, None, RN(0), RN(0))
                tt(R[5], R[0], R[2], ALU.mult, RN(0) + RN(2), RN(5))
                tt(R[6], R[1], R[2], ALU.mult, RN(1) + RN(2), ["rAang"])
                act(R[2], R[5], AF.Exp, RN(5), RN(2))
                sincos(R[7], R[8], R[6], R[9], R[10].bitcast(I32), "rA")
                tt(R[7], R[7], R[2], ALU.mult, ["rAsin"] + RN(2), ["rAsin"])
                tt(R[8], R[8], R[2], ALU.mult, ["rAcos"] + RN(2), ["rAcos"])
                ts(R[8], R[8], -1.0, None, ALU.add, None, ["rAcos"], ["rAcos"])
                tt(R[9], R[0], R[0], ALU.mult, RN(0), ["rAta"])
                tt(R[10], R[1], R[1], ALU.mult, RN(1), ["rAti"])
                tt(R[9], R[9], R[10], ALU.add, ["rAta", "rAti"], ["rAta"])
                recip(R[9], R[9], ["rAta"], ["rAta"])
                tt(R[10], R[8], R[0], ALU.mult, ["rAcos"] + RN(0), ["rAti"])
                tt(R[11], R[7], R[1], ALU.mult, ["rAsin"] + RN(1), RN(11))
                tt(R[10], R[10], R[11], ALU.add, ["rAti"] + RN(11), ["rAti"])
                tt(R[10], R[10], R[9], ALU.mult, ["rAti", "rAta"], ["rAti"])
                tt(R[11], R[7], R[0], ALU.mult, ["rAsin"] + RN(0), RN(11))
                tt(R[12], R[8], R[1], ALU.mult, ["rAcos"] + RN(1), RN(12))
                tt(R[11], R[11], R[12], ALU.subtract, RN(11) + RN(12), RN(11))
                tt(R[11], R[11], R[9], ALU.mult, RN(11) + ["rAta"], RN(11))
                tt(R[12], R[10], R[3], ALU.mult, ["rAti"] + RN(3), RN(12))
                tt(R[13], R[11], R[4], ALU.mult, RN(11) + RN(4), RN(13))
                tt(bb[:, 0, :], R[12], R[13], ALU.subtract, RN(12) + RN(13), ["bb"])
                tt(R[12], R[10], R[4], ALU.mult, ["rAti"] + RN(4), RN(12))
                tt(R[13], R[11], R[3], ALU.mult, RN(11) + RN(3), RN(13))
                tt(bb[:, 1, :], R[12], R[13], ALU.add, RN(12) + RN(13), ["bb"])
                P.fence()
                ts(R[7], R[6], scol_s, None, ALU.mult, None, ["rAang", "cst"], ["tAang"])
                sincos(R[8], R[9], R[7], R[10], R[11].bitcast(I32), "tA")
                act(R[12], R[5], AF.Exp, RN(5) + ["cst"], RN(12), scale=scol_ns)
                tt(TA[:, 0, :], R[12], R[9], ALU.mult, RN(12) + ["tAcos"], ["TA"])
                stt(TA[:, 1, :], R[12], -1.0, R[8], ALU.mult, ALU.mult, RN(12) + ["tAsin"], ["TA"])
                P.fence()
                io3 = iota_row.unsqueeze(1).to_broadcast([128, 4, 128])
                v3 = lambda ap: ap.rearrange("p (a b) -> p a b", b=128)
                tt(v3(R[7]), io3, cw[:, 3, 4 * k:4 * k + 4].unsqueeze(2).to_broadcast([128, 4, 128]), ALU.mult,
                   ["cst", "cAang"], ["fAang"])
                sincos(R[8], R[9], R[7], R[10], R[11].bitcast(I32), "fA")
                tt(v3(R[12]), io3, cw[:, 7, 4 * k:4 * k + 4].unsqueeze(2).to_broadcast([128, 4, 128]), ALU.mult,
                   ["cst", "cw7"], RN(12))
                act(R[12], R[12], AF.Exp, RN(12), RN(12))
                tt(Ft[:, 0, :], R[12], R[9], ALU.mult, RN(12) + ["fAcos"], ["Ft"])
                tt(Ft[:, 1, :], R[12], R[8], ALU.mult, RN(12) + ["fAsin"], ["Ft"])
                ts(Ft[:, 2, :], Ft[:, 1, :], -1.0, None, ALU.mult, None, ["Ft"], ["Ft"])
                dma("pool", Cp[:, 0, :, :], s5c_d[l, 0, 4 * k:4 * k + 4].rearrange("a p n -> p a n"), [], ["Cp"])
                dma("pool", Cp[:, 1, :, :], s5c_d[l, 1, 4 * k:4 * k + 4].rearrange("a p n -> p a n"), [], ["Cp"])
                P.fence()
                m = [scr[:, i, :] for i in range(8)]
                Pp = [scr[:, 8, :], scr[:, 9, :]]
                car = ca[:, 0, 4 * k:4 * k + 4]
                cai = ca[:, 1, 4 * k:4 * k + 4]

                def bu_mm(c):
                    b_ = c % 2
                    for ri in range(2):
                        mm(banks[2 * b_ + ri][:, :], uT[:, c * 128:(c + 1) * 128], bb[:, ri, :], True, True,
                           ["uT", "bb"], ["b%d" % (2 * b_ + ri)])

                def rotin(c):
                    b_ = c % 2
                    bre, bim = banks[2 * b_][:, :], banks[2 * b_ + 1][:, :]
                    nre, nim = "b%d" % (2 * b_), "b%d" % (2 * b_ + 1)
                    tt(m[0], bre, TA[:, 0, :], ALU.mult, [nre, "TA"], ["m0"])
                    tt(m[1], bim, TA[:, 1, :], ALU.mult, [nim, "TA"], ["m1"])
                    tt(bup[b_][:, 0, :], m[0], m[1], ALU.subtract, ["m0", "m1"], ["bup%d" % b_], eng="pool")
                    tt(m[2], bre, TA[:, 1, :], ALU.mult, [nre, "TA"], ["m2"])
                    tt(m[3], bim, TA[:, 0, :], ALU.mult, [nim, "TA"], ["m3"])
                    tt(bup[b_][:, 1, :], m[2], m[3], ALU.add, ["m2", "m3"], ["bup%d" % b_], eng="pool")

                bu_mm(0)
                rotin(0)
                for c in range(NCH):
                    b_ = c % 2
                    for ri in range(2):
                        for jj in range(4):
                            mm(banks[4 + ri][:, jj * 128:(jj + 1) * 128], bup[b_][:, ri, jj * 128:(jj + 1) * 128], Ub[:, :],
                               True, True, ["bup%d" % b_, "Ub"], ["b%d" % (4 + ri)])
                    if c + 1 < NCH:
                        bu_mm(c + 1)
                        rotin(c + 1)
                    for ri in range(2):
                        if c == 0:
                            cp(Pp[ri], banks[4 + ri][:, :], ["b%d" % (4 + ri)], ["Pp%d" % ri], eng="act")
                        else:
                            tt(v3(Pp[ri]), v3(banks[4 + ri][:, :]), carry[:, ri, :].unsqueeze(2).to_broadcast([128, 4, 128]),
                               ALU.add, ["b%d" % (4 + ri), "carry"], ["Pp%d" % ri])
                    tt(m[4], Pp[0], Ft[:, 0, :], ALU.mult, ["Pp0", "Ft"], ["m4"])
                    tt(m[5], Pp[1], Ft[:, 1, :], ALU.mult, ["Pp1", "Ft"], ["m5"])
                    tt(m[6], Pp[0], Ft[:, 2, :], ALU.mult, ["Pp0", "Ft"], ["m6"], eng="pool")
                    tt(m[7], Pp[1], Ft[:, 0, :], ALU.mult, ["Pp1", "Ft"], ["m7"], eng="pool")
                    tt(xbf[:, 0, :], m[4], m[5], ALU.subtract, ["m4", "m5"], ["xbf0"])
                    tt(xbf[:, 1, :], m[6], m[7], ALU.subtract, ["m6", "m7"], ["xbf1"], eng="pool")
                    if c + 1 < NCH:
                        l4 = lambda ap: v3(ap)[:, :, 127]
                        tt(ct[:, 0, :], l4(m[4]), l4(m[5]), ALU.subtract, ["m4", "m5"], ["ct0"])
                        tt(ct[:, 1, :], l4(m[6]), l4(m[7]), ALU.subtract, ["m6", "m7"], ["ct1"])
                        tt(ct[:, 2, :], ct[:, 0, :], car, ALU.mult, ["ct0", "ca"], ["ct2"])
                        tt(ct[:, 3, :], ct[:, 1, :], cai, ALU.mult, ["ct1", "ca"], ["ct3"])
                        tt(carry[:, 0, :], ct[:, 2, :], ct[:, 3, :], ALU.add, ["ct2", "ct3"], ["carry"])
                        tt(ct[:, 4, :], ct[:, 0, :], cai, ALU.mult, ["ct0", "ca"], ["ct4"])
                        tt(ct[:, 5, :], ct[:, 1, :], car, ALU.mult, ["ct1", "ca"], ["ct5"])
                        tt(carry[:, 1, :], ct[:, 4, :], ct[:, 5, :], ALU.subtract, ["ct4", "ct5"], ["carry"])
                    ybn = 6 + (c // 4) % 2
                    yb = banks[ybn][:, (c % 4) * 128:(c % 4 + 1) * 128]
                    for jj in range(4):
                        mm(yb, Cp[:, 0, jj, :], xbf[:, 0, jj * 128:(jj + 1) * 128], jj == 0, False, ["Cp", "xbf0"], ["b%d" % ybn])
                        mm(yb, Cp[:, 1, jj, :], xbf[:, 1, jj * 128:(jj + 1) * 128], False, False, ["Cp", "xbf1"], ["b%d" % ybn])
                    mm(yb, diagD[:, k, :], uT[:, c * 128:(c + 1) * 128], False, True, ["diagD", "uT"], ["b%d" % ybn])
                    if c % 4 == 3:
                        tb_ = c // 4
                        cp(ysb, banks[ybn][:, :], ["b%d" % ybn], ["ysb"], eng="act")
                        tt(gt_, ysb, ysb, ALU.mult, ["ysb"], ["gt"])
                        ts(gt_, gt_, 0.044715, 1.0, ALU.mult, ALU.add, ["gt"], ["gt"])
                        tt(gt_, gt_, ysb, ALU.mult, ["gt", "ysb"], ["gt"])
                        act(gt_, gt_, AF.Sigmoid, ["gt"], ["gt"], scale=2.0 * math.sqrt(2.0 / math.pi))
                        tt(gT[:, k, tb_ * 512:(tb_ + 1) * 512], gt_, ysb, ALU.mult, ["gt", "ysb"], ["gT"])
                P.fence()
            AR.reset(keep)
            W = AR.alloc([128, 8, 512], BF16)
            wglu = AR.alloc([128, 4, D], BF16)
            wout = AR.alloc([128, 4, D], BF16)
            hn = [AR.alloc([128, 8, 128], BF16) for _ in range(2)]
            sz = AR.alloc([128, 512], F32)
            gab = AR.alloc([128, D], F32)
            bgl = AR.alloc([128, D], F32)
            nwr = AR.alloc([128, 512], F32)
            yv = AR.alloc([128, 512], F32)
            ybf = AR.alloc([128, 512], BF16)
            yT = AR.alloc([128, 4, 128], BF16)
            junk = AR.alloc([128, 512], BF16)
            rs = AR.alloc([128, 4], F32)
            AR_tmp[0] = AR.alloc([128, D], F32)
            load_W(W, l, SEG["s5_z"][0], 512, "W")
            dma("pool", wglu, w_glu_d[l].rearrange("(k p) n -> p k n", p=128), [], ["wglu"])
            dma("pool", wout, w_out_d[l, 1024:1536, :].rearrange("(k p) n -> p k n", p=128), [], ["wout"])
            rowload(bgl, l, "b_glu", wname="bgl")
            rowload(nwr, l, "s5_nw", wname="nwr")
            for c in range(NCH):
                s_ = c % 2
                hnm = "hn%d" % s_
                load_hn(hn[s_], c, hnm)
                proj_TM(banks[2][:, :], hn[s_], hnm, W, "W", 0, 512, "b2z")
                act(sz, banks[2][:, :], AF.Silu, ["b2z"], ["sz"])
                for hf in range(2):
                    for kk in range(4):
                        mm(banks[3 + hf][:, :], gT[:, kk, c * 128:(c + 1) * 128], wglu[:, kk, hf * 512:(hf + 1) * 512],
                           kk == 0, kk == 3, ["gT", "wglu"], ["b%d" % (3 + hf)])
                    tt(gab[:, hf * 512:(hf + 1) * 512], banks[3 + hf][:, :], bgl[:, hf * 512:(hf + 1) * 512], ALU.add,
                       ["b%d" % (3 + hf), "bgl"], ["gab%d" % hf])
                act(gab[:, 512:1024], gab[:, 512:1024], AF.Sigmoid, ["gab1"], ["gab1"])
                tt(yv, gab[:, 0:512], gab[:, 512:1024], ALU.mult, ["gab0", "gab1"], ["yv"])
                rstd, rn = rms_scale(yv, "yv", 512, rs, junk, "s5n")
                stt(yv, yv, rstd, nwr, ALU.mult, ALU.mult, ["yv", rn, "nwr"], ["yv"])
                tt(ybf, yv, sz, ALU.mult, ["yv", "sz"], ["ybf"])
                finish(l, c, ybf, "ybf", 4, wout, "wout", yT, "s5")
            P.fence()

    AR.reset()
    fnw = AR.alloc([128, D], F32)
    junk = AR.alloc([128, D], BF16)
    ob = [AR.alloc([128, D], F32) for _ in range(2)]
    rs = [AR.alloc([128, 4], F32) for _ in range(2)]
    dma("sp", fnw, fnw_d[0, :].partition_broadcast(128), [], ["fnw"])
    outnames = []
    for c in range(NCH):
        s_ = c % 2
        rstd, rn = rms_scale(x_sb[:, c, :], "x%d" % c, D, rs[s_], junk, "fn%d" % s_)
        stt(ob[s_], x_sb[:, c, :], rstd, fnw, ALU.mult, ALU.mult, ["x%d" % c, rn, "fnw"], ["ob%d" % s_])
        dma("sp", out_d[c * 128:(c + 1) * 128, :], ob[s_], ["ob%d" % s_], ["out%d" % c])
        outnames.append("out%d" % c)
    P.add("sp", lambda e: e.nop(), outnames, ["done"])
    n = P.finalize(es)
    return nc, es, n


def _host_layout(inp, nlw=NL):
    f = np.float32
    g = {k: np.asarray(v) for k, v in inp.items()}
    w_in = g["w_in"]
    offs = np.cumsum([0, 512, 1024, 8, 512, 512, 512, 512, 512, 4, 4, 512, 512, 512, 256, 256, 512])
    (s_z, s_xbc, s_dt, m_z, m_q, m_k, m_v, m_o, m_i, m_f, c_z, c_u, r_z, r_q, r_k, r_v) = [
        (int(offs[i]), int(offs[i + 1])) for i in range(16)]
    W = np.zeros((NL, D, NCOLS), f)

    def put(name, off, src_lo, n):
        o = SEG[name][0] + off
        W[:, :, o:o + n] = w_in[:, :, src_lo:src_lo + n]
    put("ssd_z", 0, s_z[0], 512)
    put("ssd_xbc", 0, s_xbc[0], 1024)
    for h in range(2):
        nm = "ml%d" % h
        for j, sg in enumerate((m_z, m_q, m_k, m_v, m_o)):
            put(nm, j * 256, sg[0] + h * 256, 256)
        nm = "ret%d" % h
        put(nm, 0, r_z[0] + h * 256, 256)
        for hh in range(2):
            put(nm, 256 + hh * 128, r_q[0] + (2 * h + hh) * 64, 64)
            put(nm, 512 + hh * 128, r_k[0] + (2 * h + hh) * 64, 64)
        put(nm, 768, r_v[0] + h * 256, 256)
    put("s5_z", 0, c_z[0], 512)
    put("s5_u", 0, c_u[0], 512)
    put("misc", 0, s_dt[0], 8)
    put("misc", 8, m_i[0], 4)
    put("misc", 12, m_f[0], 4)
    rows = np.zeros((NL, NROW), f)

    def prow(name, arr):
        o, w = ROW[name]
        rows[:, o:o + w] = arr.reshape(NL, w)
    prow("norm_w", g["norm_w"]); prow("b_ada", g["b_ada"]); prow("dt_bias", g["ssd_dt_bias"])
    prow("a_log", g["ssd_a_log"]); prow("ssd_d", g["ssd_d"]); prow("ssd_nw", g["ssd_norm_w"])
    prow("ml_ib", g["ml_i_bias"]); prow("ml_fb", g["ml_f_bias"]); prow("ml_nw", g["ml_norm_w"])
    prow("b_glu", g["s5_b_glu"]); prow("s5_nw", g["s5_norm_w"]); prow("ret_nw", g["ret_norm_w"])
    prow("lam_re", g["s5_lambda_re"]); prow("lam_im", g["s5_lambda_im"])
    prow("lstep", np.repeat(g["s5_log_step"], 64, axis=1))
    cs = np.concatenate([g["ssd_conv_w"], g["ssd_conv_b"][:, None, :]], axis=1)
    convs = cs.reshape(NL, 5, 8, 128).transpose(0, 3, 2, 1).reshape(NL, 128, 40)
    cm = np.concatenate([g["ml_conv_w"], g["ml_conv_b"][:, None, :]], axis=1)
    cm = cm.reshape(NL, 5, 8, 128)
    order = [0, 1, 4, 5, 2, 3, 6, 7]
    convm = cm[:, :, order, :].transpose(0, 3, 2, 1).reshape(NL, 128, 40)
    s5col = np.zeros((NL, 128, 52), f)
    s5col[:, :, 0:4] = g["s5_d"].reshape(NL, 4, 128).transpose(0, 2, 1)
    s5col[:, :, 4:20] = g["s5_lambda_re"].reshape(NL, 16, 128).transpose(0, 2, 1)
    s5col[:, :, 20:36] = g["s5_lambda_im"].reshape(NL, 16, 128).transpose(0, 2, 1)
    s5col[:, :, 36:52] = np.repeat(g["s5_log_step"], 64, axis=1).reshape(NL, 16, 128).transpose(0, 2, 1)
    s5b = np.zeros((NL, 2, 4, 128, 512), f)
    s5c = np.zeros((NL, 2, 16, 128, 128), f)
    for ri, (bsrc, csrc) in enumerate(((g["s5_b_re"], g["s5_c_re"]), (g["s5_b_im"], g["s5_c_im"]))):
        for gi in range(32):
            k, gl = gi // 8, gi % 8
            s5b[:, ri, k, gl * 16:(gl + 1) * 16, gl * 64:(gl + 1) * 64] = bsrc[:, gi].transpose(0, 2, 1)
            jt, g2 = gi // 2, gi % 2
            s5c[:, ri, jt, g2 * 64:(g2 + 1) * 64, gl * 16:(gl + 1) * 16] = csrc[:, gi].transpose(0, 2, 1)
    cst = np.zeros((128, 128 * 5 + 32 + 512 + 16), f)
    ii = np.arange(128)
    cst[:, 0:128] = np.eye(128)
    cst[:, 128:256] = (ii[:, None] <= ii[None, :])
    cst[:, 256:384] = 1.0
    cst[:, 384:512] = np.where(ii[:, None] <= ii[None, :], 0.0, -30000.0)
    cst[:, 512:640] = ii[None, :]
    cst[:, 1192] = ii
    cst[:, 1193] = -ii
    cst[:, 640:672] = np.exp(-math.log(10000.0) * np.arange(32, dtype=np.float64) / 32)[None, :]
    for h in range(4):
        lg = math.log1p(-2.0 ** -(5 + h))
        rel = ii[None, :] - ii[:, None]
        cst[:, 672 + h * 128:672 + (h + 1) * 128] = np.where(rel >= 0, np.exp(np.maximum(rel, 0) * lg), 0.0) * 64 ** -0.5
        cst[:, 1184 + h] = np.exp((ii + 1.0) * lg)
        cst[:, 1188 + h] = np.exp((127.0 - ii) * lg) * 64 ** -0.5
    shared = dict(w_in=W, w_out=np.ascontiguousarray(g["w_out"], f), w_ada=np.ascontiguousarray(g["w_ada"], f),
                  w_glu=np.ascontiguousarray(g["s5_w_glu"], f), rows=rows,
                  fnw=np.ascontiguousarray(g["final_norm_w"].reshape(1, D), f),
                  convs=np.ascontiguousarray(convs, f), convm=np.ascontiguousarray(convm, f), s5col=s5col,
                  s5b=s5b, s5c=s5c, cst=cst)
    if nlw < NL:
        shared = {k: (np.ascontiguousarray(v[:nlw]) if k not in ("fnw", "cst") else v) for k, v in shared.items()}
    maps = []
    for b in range(8):
        m = dict(shared)
        m["x"] = np.ascontiguousarray(g["x"][b], f)
        m["cT"] = np.ascontiguousarray(g["c"][b].reshape(8, 128).T, f)
        m["posT"] = np.ascontiguousarray(g["positions"][b].reshape(NCH, 128).T.astype(np.int32))
        maps.append(m)
    return maps


_CACHE = {}


def kernel(_nl=NL, _mix=(0, 1, 2, 3), **inputs):
    key = (_nl, tuple(_mix))
    if key not in _CACHE:
        nc, es, n = build(_nl, _mix)
        _CACHE[key] = nc
    nc = _CACHE[key]
    maps = _host_layout(inputs, max(_nl, 1))
    res = run_bass_kernel_spmd(nc, maps, core_ids=list(range(8)))
    out = np.stack([np.asarray(r["out"]).reshape(S, D) for r in res.results], axis=0)
    return out.astype(np.float32)
```

```python
import math
from contextlib import ExitStack
import numpy as np
import concourse.bass as bass
import concourse.mybir as mybir
from concourse.bass_utils import run_bass_kernel_spmd

F32 = mybir.dt.float32
BF16 = mybir.dt.bfloat16
I32 = mybir.dt.int32
AF = mybir.ActivationFunctionType
ALU = mybir.AluOpType
AX = mybir.AxisListType

NL = 4
D = 1024
S = 2048
T = 128
NCH = 16
EPS = 1e-6
SEG = {}
_o = 0
for _n, _w in [("ssd_z", 512), ("ssd_xbc", 1024),
               ("ml0", 1280), ("ml1", 1280),
               ("s5_z", 512), ("s5_u", 512),
               ("ret0", 1024), ("ret1", 1024),
               ("misc", 16)]:
    SEG[_n] = (_o, _w)
    _o += _w
NCOLS = _o
ROW = {}
_o = 0
for _n, _w in [("norm_w", 1024), ("b_ada", 3072), ("dt_bias", 8), ("a_log", 8), ("ssd_d", 8), ("ssd_nw", 512),
               ("ml_ib", 4), ("ml_fb", 4), ("ml_nw", 512), ("b_glu", 1024), ("s5_nw", 512), ("ret_nw", 512),
               ("lam_re", 2048), ("lam_im", 2048), ("lstep", 2048)]:
    ROW[_n] = (_o, _w)
    _o += _w
NROW = _o
ENGS = ["pe", "act", "dve", "pool", "sp"]


class Prog:
    def __init__(self, nc):
        self.nc = nc
        self.ops = []

    @staticmethod
    def _nz(names):
        out = []
        for x in names:
            if len(x) >= 2 and x[0] == "b" and x[1].isdigit() and (len(x) == 2 or not x[2].isalpha() or x[2] in "_abcdorvz"):
                x = x[:2]
            out.append(x)
        return tuple(out)

    PAR_EXACT = {"sz", "so", "vb", "xTM", "BTM", "xd", "kTM", "qkrot"}
    PAR_PREFIX = ("sm_", "xbcT", "qkT")
    sfx = None
    stages = None

    def _par(self, names):
        if self.sfx is None:
            return names
        return tuple(n + self.sfx if (n in self.PAR_EXACT or n.startswith(self.PAR_PREFIX)) else n for n in names)

    def add(self, eng, fn, r=(), w=(), dma=False):
        self.ops.append((eng, fn, self._par(self._nz(r)), self._par(self._nz(w)), dma))

    def begin(self, c):
        self.sfx = "_%d" % (c % 2)
        self._save = self.ops
        self.ops = []
        if self.stages is None:
            self.stages = []

    def split(self):
        self._A = self.ops
        self.ops = []

    def end(self):
        self.stages.append((self._A, self.ops))
        self.ops = self._save
        self.sfx = None

    def flush(self):
        st = self.stages
        self.stages = None
        self.ops += st[0][0]
        for c in range(len(st)):
            Bc = st[c][1]
            An = st[c + 1][0] if c + 1 < len(st) else []
            i = j = 0
            nb, na = len(Bc), len(An)
            while i < nb or j < na:
                if j >= na or (i < nb and i * max(na, 1) <= j * max(nb, 1)):
                    self.ops.append(Bc[i]); i += 1
                else:
                    self.ops.append(An[j]); j += 1

    def fence(self):
        self.ops.append(("FENCE", None, (), (), False))

    def finalize(self, es):
        nc = self.nc
        GEN = 30000
        NDS = 12
        cnt = {e: 0 for e in ENGS}
        dma_i = {e: 0 for e in ENGS}
        dma_cnt = {}
        memo = {e: {} for e in ENGS}
        lastw, readers = {}, {}
        last_tok = {}
        dma_toks = []
        pend = {e: [] for e in ENGS}
        plan = []
        semnames = set()
        for (eng, fn, r, w, dma) in self.ops:
            if eng == "FENCE":
                toks = list(last_tok.values()) + dma_toks
                dma_toks = []
                for e in ENGS:
                    pend[e] = list(toks)
                continue
            deps = set(pend[eng])
            pend[eng] = []
            for x in r:
                if x in lastw:
                    deps.add(lastw[x])
            for x in w:
                if x in lastw:
                    deps.add(lastw[x])
                for t in readers.get(x, ()):
                    deps.add(t)
            if dma:
                k = dma_i[eng] % NDS
                dma_i[eng] += 1
                sn = "d_%s_%d" % (eng, k)
                prev = dma_cnt.get(sn, 0)
                tok = (sn, prev + 16)
                dma_cnt[sn] = prev + 16
                if prev > 0:
                    deps.add((sn, prev))
                inc = 16
                dma_toks.append(tok)
            else:
                g = cnt[eng] // GEN
                sn = "e_%s_%d" % (eng, g)
                cnt[eng] += 1
                tok = (sn, cnt[eng] - g * GEN)
                inc = 1
                last_tok[eng] = tok
            semnames.add(sn)
            best = {}
            for (dsn, dv) in deps:
                if eng == "pe" and dsn.startswith("e_pe_"):
                    continue
                if memo[eng].get(dsn, 0) >= dv:
                    continue
                if best.get(dsn, 0) < dv:
                    best[dsn] = dv
            for dsn, dv in best.items():
                memo[eng][dsn] = dv
            plan.append((eng, fn, sorted(best.items()), sn, inc))
            for x in r:
                readers.setdefault(x, []).append(tok)
            for x in w:
                lastw[x] = tok
                readers[x] = []
        sems = {}
        for sn in sorted(semnames):
            sems[sn] = es.enter_context(nc.semaphore(sn))
        block = es.enter_context(nc.Block())

        def emit(engname, eobj):
            for (eng, fn, waits, sn, inc) in plan:
                if eng != engname:
                    continue
                for (dsn, dv) in waits:
                    eobj.wait_ge(sems[dsn], dv)
                ins = fn(eobj)
                ins.then_inc(sems[sn], inc)

        @block.tensor
        def _(e):
            emit("pe", e)

        @block.scalar
        def _(e):
            emit("act", e)

        @block.vector
        def _(e):
            emit("dve", e)

        @block.gpsimd
        def _(e):
            emit("pool", e)

        @block.sync
        def _(e):
            emit("sp", e)
        return len(plan)


class Arena:
    def __init__(self, ap, nbytes):
        self.ap = ap
        self.n = nbytes
        self.off = 0

    def reset(self, off=0):
        self.off = off

    def alloc(self, shape, dt):
        nel = 1
        for s_ in shape[1:]:
            nel *= s_
        nb = nel * (4 if dt in (F32, I32) else 2)
        nb = (nb + 31) // 32 * 32
        assert self.off + nb <= self.n, ("arena overflow", self.off, nb, self.n)
        v = self.ap[:, self.off // 4:(self.off + nb) // 4]
        self.off += nb
        if dt != F32:
            v = v.bitcast(dt)
        v = v[:, 0:nel]
        if len(shape) == 3:
            v = v.rearrange("p (a b) -> p a b", b=shape[2])
        elif len(shape) == 4:
            v = v.rearrange("p (a b c) -> p a b c", b=shape[2], c=shape[3])
        return v


def build(nl=NL, mix=(0, 1, 2, 3)):
    NLW = max(nl, 1)
    nc = bass.Bass("TRN2", target_bir_lowering=False)
    P = Prog(nc)
    es = ExitStack()

    def din(name, shape, dt=F32):
        return nc.dram_tensor(name, list(shape), dt, kind="ExternalInput").ap()
    x_d = din("x", [S, D])
    cT_d = din("cT", [128, 8])
    pos_d = din("posT", [128, NCH], I32)
    w_in_d = din("w_in", [NLW, D, NCOLS])
    w_out_d = din("w_out", [NLW, 2048, D])
    w_ada_d = din("w_ada", [NLW, D, 3072])
    w_glu_d = din("w_glu", [NLW, 512, 1024])
    rows_d = din("rows", [NLW, NROW])
    fnw_d = din("fnw", [1, D])
    convs_d = din("convs", [NLW, 128, 8 * 5])
    convm_d = din("convm", [NLW, 128, 8 * 5])
    s5col_d = din("s5col", [NLW, 128, 4 + 48])
    s5b_d = din("s5b", [NLW, 2, 4, 128, 512])
    s5c_d = din("s5c", [NLW, 2, 16, 128, 128])
    cst_d = din("cst", [128, 128 * 5 + 32 + 512 + 16])
    out_d = nc.dram_tensor("out", [S, D], F32, kind="ExternalOutput").ap()
    hnT_d = nc.dram_tensor("hnT_scr", [128, 8, S], BF16, kind="Internal").ap()

    def sb(name, shape, dt=F32):
        return es.enter_context(nc.sbuf_tensor(name, list(shape), dt))
    x_sb = sb("x_sb", [128, NCH, D])
    cst = sb("cst_sb", [128, 128 * 5 + 32 + 512 + 16])
    identb = sb("identb", [128, 128], BF16)
    Ub = sb("Ub", [128, 128], BF16)
    rope = sb("rope", [128, 2, NCH, 32])
    mrow = sb("mrow", [128, 3, D])
    condbc = sb("condbc", [128, 8, 128], BF16)
    ARB = 88 * 1024
    arena_t = sb("arena", [128, ARB // 4])
    AR = Arena(arena_t, ARB)
    banks = [es.enter_context(nc.psum_tensor("bank%d" % i, [128, 512], F32)) for i in range(8)]

    identF = cst[:, 0:128]
    Uf = cst[:, 128:256]
    onesf = cst[:, 256:384]
    maskneg = cst[:, 384:512]
    invf = cst[:, 640:672]
    iota_row = cst[:, 512:640]
    scol_s = cst[:, 1192:1193]
    scol_ns = cst[:, 1193:1194]
    retDT = cst[:, 672:1184].rearrange("p (h t) -> p h t", t=128)
    ret_ea = cst[:, 1184:1188]
    ret_wend = cst[:, 1188:1192]
    RET_DEC = [float((1.0 - 2.0 ** -(5 + h)) ** 128) for h in range(4)]

    def dma(eng, out, in_, r, w):
        P.add(eng, lambda e: e.dma_start(out=out, in_=in_), r, w, dma=True)

    def mm(out, lhsT, rhs, start, stop, r, w):
        P.add("pe", lambda e: e.matmul(out, lhsT, rhs, start=start, stop=stop), r, w)

    def tr(out, in_, r, w):
        P.add("pe", lambda e: e.transpose(out, in_, identb[:, :]), r + ["identb"], w)

    def trf(out, in_, r, w):
        P.add("pe", lambda e: e.transpose(out, in_, identF), r + ["cst"], w)

    def act(out, in_, func, r, w, bias=None, scale=None, accum=None, eng="act"):
        kw = {}
        if bias is not None:
            kw["bias"] = bias
        if scale is not None:
            kw["scale"] = scale
        if accum is not None:
            kw["accum_out"] = accum
        P.add(eng, lambda e: e.activation(out, in_, func, **kw), r, w)

    def tt(out, in0, in1, op, r, w, eng="dve"):
        P.add(eng, lambda e: e.tensor_tensor(out, in0, in1, op), r, w)

    def ts(out, in0, s1, s2, op0, op1, r, w, eng="dve"):
        if op1 is None:
            P.add(eng, lambda e: e.tensor_scalar(out, in0, s1, None, op0), r, w)
        else:
            P.add(eng, lambda e: e.tensor_scalar(out, in0, s1, s2, op0, op1), r, w)

    def stt(out, in0, scalar, in1, op0, op1, r, w):
        P.add("dve", lambda e: e.scalar_tensor_tensor(out, in0, scalar, in1, op0, op1), r, w)

    def cp(out, in_, r, w, eng="dve"):
        if eng == "act":
            P.add(eng, lambda e: e.activation(out, in_, AF.Copy), r, w)
        else:
            P.add(eng, lambda e: e.tensor_copy(out, in_), r, w)

    def memset(ap, val, w, eng="dve"):
        P.add(eng, lambda e: e.memset(ap, val), [], w)

    def recip(out, in_, r, w):
        P.add("dve", lambda e: e.reciprocal(out, in_), r, w)

    def red(out, in_, r, w):
        P.add("dve", lambda e: e.tensor_reduce(out, in_, AX.X, ALU.add), r, w)

    def rowload(dst, l, name, r0=0, n=None, wname=None, eng="sp"):
        o, wd = ROW[name]
        n = wd if n is None else n
        dma(eng, dst, rows_d[l, o + r0:o + r0 + n].partition_broadcast(128), [], [wname])

    def sincos(sin_out, cos_out, ang, tmpa, tmpi, nm, shape_is3=False):
        ts(tmpa, ang, 1.0 / (2 * math.pi), None, ALU.mult, None, [nm + "ang"], [nm + "ta"])
        cp(tmpi, tmpa, [nm + "ta"], [nm + "ti"])
        cp(tmpa, tmpi, [nm + "ti"], [nm + "ta"])
        stt(tmpa, tmpa, -2 * math.pi, ang, ALU.mult, ALU.add, [nm + "ta", nm + "ang"], [nm + "ta"])
        for (o_, sh, on) in ((sin_out, 0.0, nm + "sin"), (cos_out, math.pi / 2, nm + "cos")):
            ts(o_, tmpa, sh, None, ALU.add, None, [nm + "ta"], [on])
            ts(tmpi.bitcast(F32), o_, math.pi, 2 * math.pi, ALU.is_gt, ALU.mult, [on], [nm + "ti"])
            tt(o_, o_, tmpi.bitcast(F32), ALU.subtract, [on, nm + "ti"], [on])
            ts(tmpi.bitcast(F32), o_, -math.pi, 2 * math.pi, ALU.is_lt, ALU.mult, [on], [nm + "ti"])
            tt(o_, o_, tmpi.bitcast(F32), ALU.add, [on, nm + "ti"], [on])
            act(o_, o_, AF.Sin, [on], [on])

    dma("sp", cst[:, :], cst_d[:, :], [], ["cst"])
    for c in range(NCH):
        dma("sp", x_sb[:, c, :], x_d[c * 128:(c + 1) * 128, :], [], ["x%d" % c])
    cp(identb[:, :], identF, ["cst"], ["identb"])
    cp(Ub[:, :], Uf, ["cst"], ["Ub"])
    AR.reset()
    cTs = AR.alloc([128, 8], F32)
    posi = AR.alloc([128, NCH], I32)
    posf = AR.alloc([128, NCH], F32)
    ang = AR.alloc([128, NCH, 32], F32)
    tmpa = AR.alloc([128, NCH, 32], F32)
    tmpi = AR.alloc([128, NCH, 32], I32)
    condb = AR.alloc([128, 8], BF16)
    dma("sp", cTs, cT_d[:, :], [], ["cTs"])
    dma("sp", posi, pos_d[:, :], [], ["posi"])
    act(condb, cTs, AF.Silu, ["cTs"], ["condb"])
    cp(condbc[:, :, :], condb.unsqueeze(2).to_broadcast([128, 8, 128]), ["condb"], ["condbc"])
    cp(posf, posi, ["posi"], ["posf"])
    tt(ang, invf.unsqueeze(1).to_broadcast([128, NCH, 32]), posf.unsqueeze(2).to_broadcast([128, NCH, 32]),
       ALU.mult, ["cst", "posf"], ["ropeang"])
    sincos(rope[:, 1, :, :], rope[:, 0, :, :], ang, tmpa, tmpi, "rope")
    P.fence()

    def load_W(W, l, col0, ncols, wname):
        src = w_in_d[l].rearrange("(k p) n -> p k n", p=128)
        c = 0
        while c < ncols:
            n = min(512, ncols - c)
            dma("pool", W[:, :, c:c + n], src[:, :, col0 + c:col0 + c + n], [], [wname])
            c += n

    def load_hn(buf, c, nm):
        dma("sp", buf, hnT_d[:, :, c * 128:(c + 1) * 128], ["hnT%d" % c], [nm])

    def proj_TM(ps, hn, hnm, W, wname, col0, ncols, pname):
        for k in range(8):
            mm(ps, hn[:, k, :], W[:, k, col0:col0 + ncols], k == 0, k == 7, [hnm, wname], [pname])

    def proj_FM(ps, hn, hnm, W, wname, col0, pname):
        for k in range(8):
            mm(ps, W[:, k, col0:col0 + 128], hn[:, k, :], k == 0, k == 7, [hnm, wname], [pname])

    def rms_scale(ycur, yname, n, rs, junk, nm):
        act(junk[:, 0:n], ycur, AF.Square, [yname], [nm + "junk", nm + "rs"], accum=rs[:, 0:1])
        ts(rs[:, 1:2], rs[:, 0:1], 1.0 / n, EPS, ALU.mult, ALU.add, [nm + "rs"], [nm + "rs1"])
        act(rs[:, 1:2], rs[:, 1:2], AF.Ln, [nm + "rs1"], [nm + "rs1"])
        act(rs[:, 2:3], rs[:, 1:2], AF.Exp, [nm + "rs1"], [nm + "rs2"], scale=-0.5)
        return rs[:, 2:3], nm + "rs2"

    def finish(l, c, ybf, ybname, nk, wout, woname, yT, nm):
        tb = banks[2][:, 0:256].bitcast(BF16).rearrange("p (a b) -> p a b", b=128)
        for k in range(nk):
            tr(tb[:, k, :], ybf[:, k * 128:(k + 1) * 128], [ybname], ["b2"])
        cp(yT[:, 0:nk, :], tb[:, 0:nk, :], ["b2"], [nm + "yT"], eng="act")
        for hf in range(2):
            for k in range(nk):
                mm(banks[3 + hf][:, :], yT[:, k, :], wout[:, k, hf * 512:(hf + 1) * 512], k == 0, k == nk - 1,
                   [nm + "yT", woname], ["b%d" % (3 + hf)])
        for hf in range(2):
            xs = x_sb[:, c, hf * 512:(hf + 1) * 512]
            tmp = AR_tmp[0][:, hf * 512:(hf + 1) * 512]
            tt(tmp, banks[3 + hf][:, :], mrow[:, 2, hf * 512:(hf + 1) * 512], ALU.mult, ["b%d" % (3 + hf), "mrow"], ["fin_tmp%d" % hf])
            tt(xs, xs, tmp, ALU.add, ["fin_tmp%d" % hf, "x%d" % c], ["x%d" % c], eng="pool")

    AR_tmp = [None]

    def attn_chunk(c, H, dv, qT, kT, kTM, v, DT, ea, wend, dec, St, Sb, yout, ops, nm, kscale=None):
        grp = max(1, 512 // dv)
        mT, kw = ops["mT"], ops["kw"]
        for h in range(H):
            (kta, ktn) = kTM(h)
            tt(kw[:, h, :], kta, wend[0][:, h:h + 1].to_broadcast([128, 128]), ALU.mult, ktn + wend[1], [nm + "kw%d" % h])
        for g0 in range(0, H, grp):
            hs = list(range(g0, min(H, g0 + grp)))
            for h in hs:
                j = h - g0
                sslot = h % 4
                scp = banks[2][:, sslot * 128:(sslot + 1) * 128]
                (qa, qn) = qT(h)
                (ka, kn) = kT(h)
                mm(scp, ka, qa, True, True, qn + kn, ["b2_%d" % sslot])
                (da, dn) = DT(h)
                ms = mT[:, h % 2, :]
                tt(ms, scp, da, ALU.mult, ["b2_%d" % sslot] + dn, [nm + "mT%d" % (h % 2)])
                (va, vn) = v(h)
                mm(banks[3][:, j * dv:(j + 1) * dv], ms, va, True, True, [nm + "mT%d" % (h % 2)] + vn, ["b3"])
                if c > 0:
                    mm(banks[4][:, j * dv:(j + 1) * dv], qa, Sb[:, h, :], True, True, qn + [nm + "Sb%d" % h], ["b4"])
                stslot = h % 2
                stp = banks[5][:, stslot * 256:stslot * 256 + dv]
                mm(stp, kw[:, h, :], va, True, True, [nm + "kw%d" % h] + vn, ["b5_%d" % stslot])
                if c == 0:
                    cp(St[:, h, :], stp, ["b5_%d" % stslot], [nm + "St%d" % h])
                else:
                    d_ = dec(h)
                    if isinstance(d_, float):
                        stt(St[:, h, :], St[:, h, :], d_, stp, ALU.mult, ALU.add,
                            ["b5_%d" % stslot, nm + "St%d" % h], [nm + "St%d" % h])
                    else:
                        stt(St[:, h, :], St[:, h, :], d_[0], stp, ALU.mult, ALU.add,
                            ["b5_%d" % stslot, nm + "St%d" % h] + d_[1], [nm + "St%d" % h])
                if c < NCH - 1:
                    cp(Sb[:, h, :], St[:, h, :], [nm + "St%d" % h], [nm + "Sb%d" % h], eng="act")
            n = len(hs)
            yv = yout[:, g0:g0 + n, :]
            b3v = banks[3][:, 0:n * dv].rearrange("p (h d) -> p h d", d=dv)
            if c > 0:
                b4v = banks[4][:, 0:n * dv].rearrange("p (h d) -> p h d", d=dv)
                tt(yv, b4v, ea[0][:, g0:g0 + n].unsqueeze(2).to_broadcast([128, n, dv]), ALU.mult,
                   ["b4"] + ea[1], [nm + "yout"])
                tt(yv, yv, b3v, ALU.add, ["b3", nm + "yout"], [nm + "yout"])
            else:
                cp(yv, b3v, ["b3"], [nm + "yout"])

    for l in range(nl):
        AR.reset()
        Wa = [AR.alloc([128, 8, 512], BF16) for _ in range(2)]
        brow = [AR.alloc([128, 512], F32) for _ in range(2)]
        nwrow = AR.alloc([128, D], F32)
        src = w_ada_d[l].rearrange("(k p) n -> p k n", p=128)
        rowload(nwrow, l, "norm_w", wname="nwrow")
        for blk in range(6):
            s_ = blk % 2
            dma("pool", Wa[s_], src[:, :, blk * 512:(blk + 1) * 512], [], ["Wa%d" % s_])
            rowload(brow[s_], l, "b_ada", blk * 512, 512, "brow%d" % s_)
            pb = banks[blk % 2]
            for k in range(8):
                mm(pb[:, :], condbc[:, k, :], Wa[s_][:, k, :], k == 0, k == 7, ["condbc", "Wa%d" % s_], ["b%d" % (blk % 2)])
            part = [0, 0, 1, 1, 2, 2][blk]
            tt(mrow[:, part, (blk % 2) * 512:(blk % 2 + 1) * 512], pb[:, :], brow[s_], ALU.add,
               ["b%d" % (blk % 2), "brow%d" % s_], ["mrow"])
        stt(mrow[:, 1, :], mrow[:, 1, :], 1.0, nwrow, ALU.add, ALU.mult, ["mrow", "nwrow"], ["mrow"])
        P.fence()
        AR.reset()
        junk = AR.alloc([128, D], BF16)
        tmp1 = [AR.alloc([128, D], F32) for _ in range(2)]
        hnb = [AR.alloc([128, D], BF16) for _ in range(2)]
        hnTs = [AR.alloc([128, 8, 128], BF16) for _ in range(2)]
        rs = [AR.alloc([128, 4], F32) for _ in range(2)]
        for c in range(NCH):
            s_ = c % 2
            nm = "p1_%d" % s_
            rstd, rn = rms_scale(x_sb[:, c, :], "x%d" % c, D, rs[s_], junk, nm)
            stt(tmp1[s_], x_sb[:, c, :], rstd, mrow[:, 1, :], ALU.mult, ALU.mult, ["x%d" % c, rn, "mrow"], [nm + "t"])
            tt(hnb[s_], tmp1[s_], mrow[:, 0, :], ALU.add, [nm + "t", "mrow"], [nm + "hn"], eng="pool")
            tb = banks[6 + s_][:, :].bitcast(BF16).rearrange("p (a b) -> p a b", b=128)
            for k in range(8):
                tr(tb[:, k, :], hnb[s_][:, k * 128:(k + 1) * 128], [nm + "hn"], ["b%d" % (6 + s_)])
            cp(hnTs[s_], tb, ["b%d" % (6 + s_)], [nm + "hnT"], eng="act")
            dma("sp", hnT_d[:, :, c * 128:(c + 1) * 128], hnTs[s_], [nm + "hnT"], ["hnT%d" % c])
        P.fence()

        if 0 in mix:
            AR.reset()
            W = AR.alloc([128, 8, 1536], BF16)
            Wm = AR.alloc([128, 8, 16], BF16)
            wout = AR.alloc([128, 4, D], BF16)
            hn = [AR.alloc([128, 8, 128], BF16) for _ in range(2)]
            raw = AR.alloc([128, 8, 131], F32)
            cv = AR.alloc([128, 128], F32)
            xtm = AR.alloc([128, 1024], F32)
            xbcT2 = [AR.alloc([128, 8, 128], BF16) for _ in range(2)]
            xTM2 = [AR.alloc([128, 512], F32) for _ in range(2)]
            BTM2 = [AR.alloc([128, 2, 128], BF16) for _ in range(2)]
            sz2 = [AR.alloc([128, 512], F32) for _ in range(2)]
            sm2 = [AR.alloc([128, 96], F32) for _ in range(2)]
            xd2 = [AR.alloc([128, 8, 64], BF16) for _ in range(2)]
            prm = AR.alloc([128, 32], F32)
            cprm = AR.alloc([128, 40], F32)
            nwr = AR.alloc([128, 512], F32)
            dabc = AR.alloc([128, 8, 128], F32)
            DTt = AR.alloc([128, 8, 128], F32)
            mT = AR.alloc([128, 2, 128], BF16)
            kw = AR.alloc([128, 8, 128], BF16)
            St = AR.alloc([128, 8, 64], F32)
            Sb = AR.alloc([128, 8, 64], BF16)
            yo = AR.alloc([128, 8, 64], F32)
            t3 = AR.alloc([128, 8, 64], F32)
            ybf = AR.alloc([128, 512], BF16)
            yT = AR.alloc([128, 4, 128], BF16)
            junk = AR.alloc([128, 512], BF16)
            rs = AR.alloc([128, 4], F32)
            AR_tmp[0] = AR.alloc([128, D], F32)
            load_W(W, l, SEG["ssd_z"][0], 1536, "W")
            load_W(Wm, l, SEG["misc"][0], 16, "Wm")
            dma("pool", wout, w_out_d[l, 0:512, :].rearrange("(k p) n -> p k n", p=128), [], ["wout"])
            rowload(prm[:, 0:8], l, "dt_bias", wname="prm")
            rowload(prm[:, 8:16], l, "a_log", wname="prm")
            rowload(prm[:, 16:24], l, "ssd_d", wname="prm")
            rowload(nwr, l, "ssd_nw", wname="nwr")
            dma("sp", cprm, convs_d[l], [], ["cprm"])
            act(prm[:, 8:16], prm[:, 8:16], AF.Exp, ["prm"], ["prm"])
            ts(prm[:, 8:16], prm[:, 8:16], -1.0, None, ALU.mult, None, ["prm"], ["prm"])
            memset(raw[:, :, 0:3], 0.0, ["raw"])
            cpv = cprm.rearrange("p (t k) -> p t k", k=5)
            for c in range(NCH):
                s_ = c % 2
                hnm = "hn%d" % s_
                xbcT, xTM, BTM, sz, sm, xd = xbcT2[s_], xTM2[s_], BTM2[s_], sz2[s_], sm2[s_], xd2[s_]
                P.begin(c)
                load_hn(hn[s_], c, hnm)
                proj_TM(banks[0][:, :], hn[s_], hnm, W, "W", 0, 512, "b0")
                act(sz, banks[0][:, :], AF.Silu, ["b0"], ["sz"])
                proj_TM(banks[7][:, 0:8], hn[s_], hnm, Wm, "Wm", 0, 8, "b7a")
                tt(sm[:, 0:8], banks[7][:, 0:8], prm[:, 0:8], ALU.add, ["b7a", "prm"], ["sm_dt"])
                ts(sm[:, 0:8], sm[:, 0:8], 30.0, None, ALU.min, None, ["sm_dt"], ["sm_dt"])
                act(sm[:, 0:8], sm[:, 0:8], AF.Exp, ["sm_dt"], ["sm_dt"])
                act(sm[:, 0:8], sm[:, 0:8], AF.Ln, ["sm_dt"], ["sm_dt"], bias=1.0)
                tt(sm[:, 8:16], sm[:, 0:8], prm[:, 8:16], ALU.mult, ["sm_dt", "prm"], ["sm_da"])
                for hf in range(2):
                    pb, bn = (banks[1], "b1") if hf == 0 else (banks[6], "b6")
                    proj_TM(pb[:, :], hn[s_], hnm, W, "W", 512 + hf * 512, 512, bn)
                    cp(xtm[:, hf * 512:(hf + 1) * 512], pb[:, :], [bn], ["xtm%d" % hf], eng="act")
                for hf in range(2):
                    pb, bn = (banks[1], "b1") if hf == 0 else (banks[6], "b6")
                    for t4 in range(4):
                        t_ = hf * 4 + t4
                        trf(pb[:, t4 * 128:(t4 + 1) * 128], xtm[:, t_ * 128:(t_ + 1) * 128], ["xtm%d" % hf], [bn])
                for t_ in range(8):
                    pb, bn = (banks[1], "b1") if t_ < 4 else (banks[6], "b6")
                    pp = pb[:, (t_ % 4) * 128:(t_ % 4 + 1) * 128]
                    cp(raw[:, t_, 3:131], pp, [bn], ["raw"], eng="act")
                    ts(cv, raw[:, t_, 0:128], cpv[:, t_, 0:1], cpv[:, t_, 4:5], ALU.mult, ALU.add, ["raw", "cprm"], ["cv"])
                    for k_ in range(1, 4):
                        stt(cv, raw[:, t_, k_:k_ + 128], cpv[:, t_, k_:k_ + 1], cv, ALU.mult, ALU.add, ["raw", "cprm", "cv"], ["cv"])
                    act(xbcT[:, t_, :], cv, AF.Silu, ["cv"], ["xbcT%d" % t_])
                cp(raw[:, :, 0:3], raw[:, :, 128:131], ["raw"], ["raw"], eng="pool")
                tb = banks[6][:, :].bitcast(BF16).rearrange("p (a b) -> p a b", b=128)
                for t_ in range(6):
                    tr(tb[:, t_, :], xbcT[:, t_, :], ["xbcT%d" % t_], ["b6"])
                cp(xTM, tb[:, 0:4, :].rearrange("p a b -> p (a b)"), ["b6"], ["xTM"], eng="act")
                cp(BTM, tb[:, 4:6, :], ["b6"], ["BTM"], eng="act")
                mm(banks[7][:, 16:24], Uf, sm[:, 8:16], True, True, ["cst", "sm_da"], ["b7b"])
                mm(banks[7][:, 24:32], onesf, sm[:, 8:16], True, True, ["cst", "sm_da"], ["b7c"])
                cp(sm[:, 16:24], banks[7][:, 16:24], ["b7b"], ["sm_ac"])
                act(sm[:, 24:32], banks[7][:, 16:24], AF.Exp, ["b7b"], ["sm_ea"])
                tt(sm[:, 32:40], banks[7][:, 24:32], sm[:, 16:24], ALU.subtract, ["b7c", "sm_ac"], ["sm_we"])
                act(sm[:, 32:40], sm[:, 32:40], AF.Exp, ["sm_we"], ["sm_we"])
                act(sm[:, 40:48], banks[7][:, 24:32], AF.Exp, ["b7c"], ["sm_dec"])
                ts(sm[:, 48:56], sm[:, 16:24], -1.0, None, ALU.mult, None, ["sm_ac"], ["sm_nac"])
                xv = xTM.rearrange("p (h d) -> p h d", d=64)
                tt(xd, xv, sm[:, 0:8].unsqueeze(2).to_broadcast([128, 8, 64]), ALU.mult, ["xTM", "sm_dt"], ["xd"])
                P.split()
                cp(dabc, sm[:, 8:16].unsqueeze(2).to_broadcast([128, 8, 128]), ["sm_da"], ["dabc"], eng="pool")
                for h in range(8):
                    rp = banks[3 + h % 2][:, 0:128]
                    rn = "b%d" % (3 + h % 2)
                    mm(rp, dabc[:, h, :], Uf, True, True, ["dabc", "cst"], [rn])
                    tt(DTt[:, h, :], rp, maskneg, ALU.add, [rn, "cst"], ["DT%d" % h])
                    act(DTt[:, h, :], DTt[:, h, :], AF.Exp, ["DT%d" % h, "sm_nac"], ["DT%d" % h], bias=sm[:, 48 + h:49 + h])
                attn_chunk(c, 8, 64,
                           lambda h: (xbcT[:, 6 + h // 4, :], ["xbcT%d" % (6 + h // 4)]),
                           lambda h: (xbcT[:, 4 + h // 4, :], ["xbcT%d" % (4 + h // 4)]),
                           lambda h: (BTM[:, h // 4, :], ["BTM"]),
                           lambda h: (xd[:, h, :], ["xd"]),
                           lambda h: (DTt[:, h, :], ["DT%d" % h]),
                           (sm[:, 24:32], ["sm_ea"]), (sm[:, 32:40], ["sm_we"]),
                           lambda h: (sm[:, 40 + h:41 + h], ["sm_dec"]),
                           St, Sb, yo, {"mT": mT, "kw": kw}, "ssd")
                tt(t3, xv, prm[:, 16:24].unsqueeze(2).to_broadcast([128, 8, 64]), ALU.mult, ["xTM", "prm"], ["t3"], eng="pool")
                yf = yo.rearrange("p h d -> p (h d)")
                tt(yf, yf, t3.rearrange("p h d -> p (h d)"), ALU.add, ["ssdyout", "t3"], ["ssdyout"])
                tt(yf, yf, sz, ALU.mult, ["ssdyout", "sz"], ["ssdyout"])
                rstd, rn = rms_scale(yf, "ssdyout", 512, rs, junk, "ssdn")
                stt(ybf, yf, rstd, nwr, ALU.mult, ALU.mult, ["ssdyout", rn, "nwr"], ["ybf"])
                finish(l, c, ybf, "ybf", 4, wout, "wout", yT, "ssd")
                P.end()
            P.flush()
            P.fence()

        for kind in (1, 3):
            if kind not in mix:
                continue
            for half in range(2):
                isml = kind == 1
                AR.reset()
                ncol = 1280 if isml else 1024
                seg = SEG[("ml%d" if isml else "ret%d") % half][0]
                W = AR.alloc([128, 8, ncol], BF16)
                Wm = AR.alloc([128, 8, 16], BF16)
                wout = AR.alloc([128, 2, D], BF16)
                hn = [AR.alloc([128, 8, 128], BF16) for _ in range(2)]
                raw = AR.alloc([128, 4, 131], F32)
                cv = AR.alloc([128, 128], F32)
                xtm = AR.alloc([128, 512], F32)
                qkT2 = [AR.alloc([128, 4, 128], BF16) for _ in range(2)]
                kTMt2 = [AR.alloc([128, 2, 128], BF16) for _ in range(2)]
                qkrot2 = [AR.alloc([128, 4, 128], BF16) for _ in range(2)]
                rt = AR.alloc([128, 4, 2, 32], F32)
                rt2 = AR.alloc([128, 4, 2, 32], F32)
                szb2 = [AR.alloc([128, 256], F32) for _ in range(2)]
                so2 = [AR.alloc([128, 256], F32) for _ in range(2)]
                vb2 = [AR.alloc([128, 2, 129], BF16) for _ in range(2)]
                smb2 = [AR.alloc([128, 64], F32) for _ in range(2)]
                prm = AR.alloc([128, 16], F32)
                cprm = AR.alloc([128, 40], F32)
                nwr = AR.alloc([128, 256], F32)
                dabc = AR.alloc([128, 2, 128], F32)
                DTt = AR.alloc([128, 2, 128], F32)
                mT = AR.alloc([128, 2, 128], BF16)
                kw = AR.alloc([128, 2, 128], BF16)
                St = AR.alloc([128, 2, 129], F32)
                Sb = AR.alloc([128, 2, 129], BF16)
                yo = AR.alloc([128, 2, 129], F32)
                hh = AR.alloc([128, 2, 128], F32)
                sq = AR.alloc([128, 2, 128], F32)
                ybf = AR.alloc([128, 256], BF16)
                yT = AR.alloc([128, 4, 128], BF16)
                AR_tmp[0] = AR.alloc([128, D], F32)
                dv = 129 if isml else 128
                load_W(W, l, seg, ncol, "W")
                r0 = (512 if isml else 1536) + half * 256
                dma("pool", wout, w_out_d[l, r0:r0 + 256, :].rearrange("(k p) n -> p k n", p=128), [], ["wout"])
                rowload(nwr, l, "ml_nw" if isml else "ret_nw", half * 256, 256, "nwr")
                if isml:
                    load_W(Wm, l, SEG["misc"][0], 16, "Wm")
                    rowload(prm[:, 0:2], l, "ml_ib", half * 2, 2, "prm")
                    rowload(prm[:, 2:4], l, "ml_fb", half * 2, 2, "prm")
                    dma("sp", cprm, convm_d[l], [], ["cprm"])
                    memset(raw[:, :, 0:3], 0.0, ["raw"])
                    for q_ in range(2):
                        memset(vb2[q_][:, :, 128:129], 1.0, ["vb_%d" % q_])
                else:
                    for q_ in range(2):
                        memset(qkrot2[q_], 0.0, ["qkrot_%d" % q_])
                cpv = cprm.rearrange("p (t k) -> p t k", k=5)
                KS = 128.0 ** -0.5
                for c in range(NCH):
                    s_ = c % 2
                    hnm = "hn%d" % s_
                    qkT, kTMt, qkrot, sz, so, vb, sm = qkT2[s_], kTMt2[s_], qkrot2[s_], szb2[s_], so2[s_], vb2[s_], smb2[s_]
                    P.begin(c)
                    load_hn(hn[s_], c, hnm)
                    proj_TM(banks[0][:, 0:256], hn[s_], hnm, W, "W", 0, 256, "b0")
                    act(sz, banks[0][:, 0:256], AF.Silu, ["b0"], ["sz"])
                    proj_TM(banks[0][:, 256:512], hn[s_], hnm, W, "W", 768, 256, "b0v")
                    cp(vb[:, :, 0:128], banks[0][:, 256:512].rearrange("p (h d) -> p h d", d=128), ["b0v"], ["vb"], eng="act")
                    if isml:
                        proj_TM(banks[7][:, 256:512], hn[s_], hnm, W, "W", 1024, 256, "b7o")
                        act(so, banks[7][:, 256:512], AF.Sigmoid, ["b7o"], ["so"])
                        proj_TM(banks[7][:, 0:2], hn[s_], hnm, Wm, "Wm", 8 + 2 * half, 2, "b7a")
                        proj_TM(banks[7][:, 2:4], hn[s_], hnm, Wm, "Wm", 12 + 2 * half, 2, "b7a2")
                        tt(sm[:, 0:2], banks[7][:, 0:2], prm[:, 0:2], ALU.add, ["b7a", "prm"], ["sm_i"])
                        tt(sm[:, 2:4], banks[7][:, 2:4], prm[:, 2:4], ALU.add, ["b7a2", "prm"], ["sm_f"])
                        ts(sm[:, 2:4], sm[:, 2:4], -30.0, None, ALU.max, None, ["sm_f"], ["sm_f"])
                        act(sm[:, 2:4], sm[:, 2:4], AF.Exp, ["sm_f"], ["sm_f"], scale=-1.0)
                        act(sm[:, 2:4], sm[:, 2:4], AF.Ln, ["sm_f"], ["sm_f"], bias=1.0)
                        ts(sm[:, 2:4], sm[:, 2:4], -1.0, None, ALU.mult, None, ["sm_f"], ["sm_f"])
                        proj_TM(banks[1][:, :], hn[s_], hnm, W, "W", 256, 512, "b1")
                        cp(xtm, banks[1][:, :], ["b1"], ["xtm"], eng="act")
                        for t_ in range(4):
                            trf(banks[1][:, t_ * 128:(t_ + 1) * 128], xtm[:, t_ * 128:(t_ + 1) * 128], ["xtm"], ["b1"])
                        for t_ in range(4):
                            pp = banks[1][:, t_ * 128:(t_ + 1) * 128]
                            pn = "b1"
                            cp(raw[:, t_, 3:131], pp, [pn], ["raw"], eng="act")
                            ct = half * 4 + t_
                            ts(cv, raw[:, t_, 0:128], cpv[:, ct, 0:1], cpv[:, ct, 4:5], ALU.mult, ALU.add, ["raw", "cprm"], ["cv"])
                            for k_ in range(1, 4):
                                stt(cv, raw[:, t_, k_:k_ + 128], cpv[:, ct, k_:k_ + 1], cv, ALU.mult, ALU.add, ["raw", "cprm", "cv"], ["cv"])
                            act(qkT[:, t_, :], cv, AF.Silu, ["cv"], ["qkT%d" % t_])
                        cp(raw[:, :, 0:3], raw[:, :, 128:131], ["raw"], ["raw"], eng="pool")
                        tb = banks[6][:, :].bitcast(BF16).rearrange("p (a b) -> p a b", b=128)
                        for t_ in range(2):
                            tr(tb[:, t_, :], qkT[:, 2 + t_, :], ["qkT%d" % (2 + t_)], ["b6"])
                        cp(kTMt, tb[:, 0:2, :], ["b6"], ["kTM"], eng="act")
                        mm(banks[7][:, 16:18], Uf, sm[:, 2:4], True, True, ["cst", "sm_f"], ["b7b"])
                        mm(banks[7][:, 24:26], onesf, sm[:, 2:4], True, True, ["cst", "sm_f"], ["b7c"])
                        cp(sm[:, 16:18], banks[7][:, 16:18], ["b7b"], ["sm_ac"])
                        act(sm[:, 24:26], banks[7][:, 16:18], AF.Exp, ["b7b"], ["sm_ea"])
                        tt(sm[:, 32:34], banks[7][:, 24:26], sm[:, 16:18], ALU.subtract, ["b7c", "sm_ac"], ["sm_we"])
                        tt(sm[:, 32:34], sm[:, 32:34], sm[:, 0:2], ALU.add, ["sm_we", "sm_i"], ["sm_we"])
                        act(sm[:, 32:34], sm[:, 32:34], AF.Exp, ["sm_we"], ["sm_we"])
                        ts(sm[:, 32:34], sm[:, 32:34], KS, None, ALU.mult, None, ["sm_we"], ["sm_we"])
                        act(sm[:, 40:42], banks[7][:, 24:26], AF.Exp, ["b7c"], ["sm_dec"])
                        tt(sm[:, 48:50], sm[:, 0:2], sm[:, 16:18], ALU.subtract, ["sm_i", "sm_ac"], ["sm_nac"])
                        qTf = lambda h: (qkT[:, h, :], ["qkT%d" % h])
                        kTf = lambda h: (qkT[:, 2 + h, :], ["qkT%d" % (2 + h)])
                        DTf = lambda h: (DTt[:, h, :], ["DT%d" % h])
                        eaf = (sm[:, 24:26], ["sm_ea"])
                        wef = (sm[:, 32:34], ["sm_we"])
                        decf = lambda h: (sm[:, 40 + h:41 + h], ["sm_dec"])
                    else:
                        proj_TM(banks[1][:, 0:512], hn[s_], hnm, W, "W", 256, 512, "b1o")
                        qv = banks[1][:, 0:512].rearrange("p (a d) -> p a d", d=128)
                        cosb = rope[:, 0, c, :].unsqueeze(1).to_broadcast([128, 4, 32])
                        sinb = rope[:, 1, c, :].unsqueeze(1).to_broadcast([128, 4, 32])
                        x1 = qv[:, :, 0:32]
                        x2 = qv[:, :, 32:64]
                        tt(rt[:, :, 0, :], x1, cosb, ALU.mult, ["b1o"], ["rt"])
                        tt(rt[:, :, 1, :], x2, cosb, ALU.mult, ["b1o"], ["rt"])
                        tt(rt2[:, :, 0, :], x2, sinb, ALU.mult, ["b1o"], ["rt2"])
                        tt(rt2[:, :, 1, :], x1, sinb, ALU.mult, ["b1o"], ["rt2"])
                        tt(qkrot[:, :, 0:32], rt[:, :, 0, :], rt2[:, :, 0, :], ALU.subtract, ["rt", "rt2"], ["qkrot"])
                        tt(qkrot[:, :, 32:64], rt[:, :, 1, :], rt2[:, :, 1, :], ALU.add, ["rt", "rt2"], ["qkrot"])
                        tb = banks[6][:, :].bitcast(BF16).rearrange("p (a b) -> p a b", b=128)
                        for t_ in range(4):
                            tr(tb[:, t_, :], qkrot[:, t_, :], ["qkrot"], ["b6"])
                        cp(qkT, tb[:, 0:4, :], ["b6"], ["qkT0", "qkT1", "qkT2", "qkT3"], eng="act")
                        qTf = lambda h: (qkT[:, h, :], ["qkT%d" % h])
                        kTf = lambda h: (qkT[:, 2 + h, :], ["qkT%d" % (2 + h)])
                        DTf = lambda h: (retDT[:, 2 * half + h, :], ["cst"])
                        eaf = (ret_ea[:, 2 * half:2 * half + 2], ["cst"])
                        wef = (ret_wend[:, 2 * half:2 * half + 2], ["cst"])
                        decf = lambda h: RET_DEC[2 * half + h]
                    if isml:
                        kTMf = lambda h: (kTMt[:, h, :], ["kTM"])
                    else:
                        kTMf = lambda h: (qkrot[:, 2 + h, :], ["qkrot"])
                    P.split()
                    if isml:
                        cp(dabc, sm[:, 2:4].unsqueeze(2).to_broadcast([128, 2, 128]), ["sm_f"], ["dabc"], eng="pool")
                        for h in range(2):
                            rp = banks[3 + h][:, 0:128]
                            rn = "b%d" % (3 + h)
                            mm(rp, dabc[:, h, :], Uf, True, True, ["dabc", "cst"], [rn])
                            tt(DTt[:, h, :], rp, maskneg, ALU.add, [rn, "cst"], ["DT%d" % h])
                            act(DTt[:, h, :], DTt[:, h, :], AF.Exp, ["DT%d" % h, "sm_nac"], ["DT%d" % h], bias=sm[:, 48 + h:49 + h])
                            ts(DTt[:, h, :], DTt[:, h, :], KS, None, ALU.mult, None, ["DT%d" % h], ["DT%d" % h])
                    attn_chunk(c, 2, dv, qTf, kTf, kTMf,
                               lambda h: (vb[:, h, 0:dv], ["vb"]),
                               DTf, eaf, wef, decf, St[:, :, 0:dv], Sb[:, :, 0:dv], yo[:, :, 0:dv],
                               {"mT": mT, "kw": kw}, "at")
                    if isml:
                        dn_ = yo[:, :, 128:129].rearrange("p h d -> p (h d)")
                        stt(sm[:, 56:58], dn_, -1.0, dn_, ALU.mult, ALU.max, ["atyout"], ["sm_den"])
                        ts(sm[:, 56:58], sm[:, 56:58], 1.0, None, ALU.max, None, ["sm_den"], ["sm_den"])
                        recip(sm[:, 56:58], sm[:, 56:58], ["sm_den"], ["sm_den"])
                        tt(hh, yo[:, :, 0:128], sm[:, 56:58].unsqueeze(2).to_broadcast([128, 2, 128]), ALU.mult, ["atyout", "sm_den"], ["hh"])
                        tt(hh, hh, so.rearrange("p (h d) -> p h d", d=128), ALU.mult, ["hh", "so"], ["hh"])
                    else:
                        cp(hh, yo[:, :, 0:128], ["atyout"], ["hh"], eng="pool")
                    tt(sq, hh, hh, ALU.mult, ["hh"], ["sq"], eng="pool")
                    red(sm[:, 58:60], sq, ["sq"], ["sm_ss"])
                    ts(sm[:, 58:60], sm[:, 58:60], 1.0 / 128, EPS, ALU.mult, ALU.add, ["sm_ss"], ["sm_ss"])
                    act(sm[:, 58:60], sm[:, 58:60], AF.Ln, ["sm_ss"], ["sm_ss"])
                    act(sm[:, 58:60], sm[:, 58:60], AF.Exp, ["sm_ss"], ["sm_ss"], scale=-0.5)
                    tt(hh, hh, sm[:, 58:60].unsqueeze(2).to_broadcast([128, 2, 128]), ALU.mult, ["hh", "sm_ss"], ["hh"])
                    hf_ = hh.rearrange("p h d -> p (h d)")
                    tt(hf_, hf_, nwr, ALU.mult, ["hh", "nwr"], ["hh"])
                    tt(ybf, hf_, sz, ALU.mult, ["hh", "sz"], ["ybf"])
                    finish(l, c, ybf, "ybf", 2, wout, "wout", yT, "at")
                    P.end()
                P.flush()
                P.fence()

        if 2 in mix:
            AR.reset()
            gT = AR.alloc([128, 4, S], BF16)
            keep = AR.off
            uT = AR.alloc([128, S], BF16)
            Wu = AR.alloc([128, 8, 128], BF16)
            bb = AR.alloc([128, 2, 512], BF16)
            Cp = AR.alloc([128, 2, 4, 128], BF16)
            hb = [AR.alloc([128, 8, 256], BF16) for _ in range(2)]
            diagD = AR.alloc([128, 4, 128], BF16)
            scol = AR.alloc([128, 52], F32)
            cw = AR.alloc([128, 9, 16], F32)
            cwi = AR.alloc([128, 16], I32)
            ca = AR.alloc([128, 2, 16], F32)
            scr = AR.alloc([128, 14, 512], F32)
            TA = AR.alloc([128, 2, 512], F32)
            Ft = AR.alloc([128, 3, 512], F32)
            bup = [AR.alloc([128, 2, 512], BF16) for _ in range(2)]
            xbf = AR.alloc([128, 2, 512], BF16)
            carry = AR.alloc([128, 2, 4], F32)
            ct = AR.alloc([128, 8, 4], F32)
            ysb = AR.alloc([128, 512], F32)
            gt_ = AR.alloc([128, 512], F32)
            dma("sp", scol, s5col_d[l], [], ["scol"])
            for k in range(4):
                ts(diagD[:, k, :], identF, scol[:, k:k + 1], None, ALU.mult, None, ["cst", "scol"], ["diagD"])
            lre, lim, lst = scol[:, 4:20], scol[:, 20:36], scol[:, 36:52]
            act(cw[:, 0, :], lst, AF.Exp, ["scol"], ["cw0"])
            ts(cw[:, 1, :], lre, -1e-4, None, ALU.min, None, ["scol"], ["cw1"])
            tt(cw[:, 7, :], cw[:, 1, :], cw[:, 0, :], ALU.mult, ["cw0", "cw1"], ["cw7"])
            tt(cw[:, 3, :], lim, cw[:, 0, :], ALU.mult, ["cw0", "scol"], ["cAang"])
            act(cw[:, 2, :], cw[:, 7, :], AF.Exp, ["cw7"], ["cw2"])
            sincos(cw[:, 4, :], cw[:, 5, :], cw[:, 3, :], cw[:, 6, :], cwi, "cA")
            tt(ca[:, 0, :], cw[:, 2, :], cw[:, 5, :], ALU.mult, ["cw2", "cAcos"], ["ca"])
            tt(ca[:, 1, :], cw[:, 2, :], cw[:, 4, :], ALU.mult, ["cw2", "cAsin"], ["ca"])
            R = [scr[:, i, :] for i in range(14)]
            for k in range(4):
                load_W(Wu, l, SEG["s5_u"][0] + k * 128, 128, "Wu")
                for tb_ in range(8):
                    s_ = tb_ % 2
                    dma("sp", hb[s_], hnT_d[:, :, tb_ * 256:(tb_ + 1) * 256],
                        ["hnT%d" % (2 * tb_), "hnT%d" % (2 * tb_ + 1)], ["hb%d" % s_])
                    pp = banks[s_][:, 0:256]
                    for kk in range(8):
                        mm(pp, Wu[:, kk, :], hb[s_][:, kk, :], kk == 0, kk == 7, ["Wu", "hb%d" % s_], ["b%d" % s_])
                    cp(uT[:, tb_ * 256:(tb_ + 1) * 256], pp, ["b%d" % s_], ["uT"], eng="act")
                RN = lambda i: ["R%d" % i]
                for i_, nm_ in enumerate(("lam_re", "lam_im", "lstep")):
                    rowload(R[i_], l, nm_, k * 512, 512, "R%d" % i_)
                dma("sp", R[3], s5b_d[l, 0, k], [], RN(3))
                dma("sp", R[4], s5b_d[l, 1, k], [], RN(4))
                act(R[2], R[2], AF.Exp, RN(2), RN(2))
                ts(R[0], R[0], -1e-4, None, ALU.min, None, RN(0), RN(0))
                tt(R[5], R[0], R[2], ALU.mult, RN(0) + RN(2), RN(5))
                tt(R[6], R[1], R[2], ALU.mult, RN(1) + RN(2), ["rAang"])
                act(R[2], R[5], AF.Exp, RN(5), RN(2))
                sincos(R[7], R[8], R[6], R[9], R[10].bitcast(I32), "rA")
                tt(R[7], R[7], R[2], ALU.mult, ["rAsin"] + RN(2), ["rAsin"])
                tt(R[8], R[8], R[2], ALU.mult, ["rAcos"] + RN(2), ["rAcos"])
                ts(R[8], R[8], -1.0, None, ALU.add, None, ["rAcos"], ["rAcos"])
                tt(R[9], R[0], R[0], ALU.mult, RN(0), ["rAta"])
                tt(R[10], R[1], R[1], ALU.mult, RN(1), ["rAti"])
                tt(R[9], R[9], R[10], ALU.add, ["rAta", "rAti"], ["rAta"])
                recip(R[9], R[9], ["rAta"], ["rAta"])
                tt(R[10], R[8], R[0], ALU.mult, ["rAcos"] + RN(0), ["rAti"])
                tt(R[11], R[7], R[1], ALU.mult, ["rAsin"] + RN(1), RN(11))
                tt(R[10], R[10], R[11], ALU.add, ["rAti"] + RN(11), ["rAti"])
                tt(R[10], R[10], R[9], ALU.mult, ["rAti", "rAta"], ["rAti"])
                tt(R[11], R[7], R[0], ALU.mult, ["rAsin"] + RN(0), RN(11))
                tt(R[12], R[8], R[1], ALU.mult, ["rAcos"] + RN(1), RN(12))
                tt(R[11], R[11], R[12], ALU.subtract, RN(11) + RN(12), RN(11))
                tt(R[11], R[11], R[9], ALU.mult, RN(11) + ["rAta"], RN(11))
                tt(R[12], R[10], R[3], ALU.mult, ["rAti"] + RN(3), RN(12))
                tt(R[13], R[11], R[4], ALU.mult, RN(11) + RN(4), RN(13))
                tt(bb[:, 0, :], R[12], R[13], ALU.subtract, RN(12) + RN(13), ["bb"])
                tt(R[12], R[10], R[4], ALU.mult, ["rAti"] + RN(4), RN(12))
                tt(R[13], R[11], R[3], ALU.mult, RN(11) + RN(3), RN(13))
                tt(bb[:, 1, :], R[12], R[13], ALU.add, RN(12) + RN(13), ["bb"])
                P.fence()
                ts(R[7], R[6], scol_s, None, ALU.mult, None, ["rAang", "cst"], ["tAang"])
                sincos(R[8], R[9], R[7], R[10], R[11].bitcast(I32), "tA")
                act(R[12], R[5], AF.Exp, RN(5) + ["cst"], RN(12), scale=scol_ns)
                tt(TA[:, 0, :], R[12], R[9], ALU.mult, RN(12) + ["tAcos"], ["TA"])
                stt(TA[:, 1, :], R[12], -1.0, R[8], ALU.mult, ALU.mult, RN(12) + ["tAsin"], ["TA"])
                P.fence()
                io3 = iota_row.unsqueeze(1).to_broadcast([128, 4, 128])
                v3 = lambda ap: ap.rearrange("p (a b) -> p a b", b=128)
                tt(v3(R[7]), io3, cw[:, 3, 4 * k:4 * k + 4].unsqueeze(2).to_broadcast([128, 4, 128]), ALU.mult,
                   ["cst", "cAang"], ["fAang"])
                sincos(R[8], R[9], R[7], R[10], R[11].bitcast(I32), "fA")
                tt(v3(R[12]), io3, cw[:, 7, 4 * k:4 * k + 4].unsqueeze(2).to_broadcast([128, 4, 128]), ALU.mult,
                   ["cst", "cw7"], RN(12))
                act(R[12], R[12], AF.Exp, RN(12), RN(12))
                tt(Ft[:, 0, :], R[12], R[9], ALU.mult, RN(12) + ["fAcos"], ["Ft"])
                tt(Ft[:, 1, :], R[12], R[8], ALU.mult, RN(12) + ["fAsin"], ["Ft"])
                ts(Ft[:, 2, :], Ft[:, 1, :], -1.0, None, ALU.mult, None, ["Ft"], ["Ft"])
                dma("pool", Cp[:, 0, :, :], s5c_d[l, 0, 4 * k:4 * k + 4].rearrange("a p n -> p a n"), [], ["Cp"])
                dma("pool", Cp[:, 1, :, :], s5c_d[l, 1, 4 * k:4 * k + 4].rearrange("a p n -> p a n"), [], ["Cp"])
                P.fence()
                m = [scr[:, i, :] for i in range(8)]
                Pp = [scr[:, 8, :], scr[:, 9, :]]
                car = ca[:, 0, 4 * k:4 * k + 4]
                cai = ca[:, 1, 4 * k:4 * k + 4]

                def bu_mm(c):
                    b_ = c % 2
                    for ri in range(2):
                        mm(banks[2 * b_ + ri][:, :], uT[:, c * 128:(c + 1) * 128], bb[:, ri, :], True, True,
                           ["uT", "bb"], ["b%d" % (2 * b_ + ri)])

                def rotin(c):
                    b_ = c % 2
                    bre, bim = banks[2 * b_][:, :], banks[2 * b_ + 1][:, :]
                    nre, nim = "b%d" % (2 * b_), "b%d" % (2 * b_ + 1)
                    tt(m[0], bre, TA[:, 0, :], ALU.mult, [nre, "TA"], ["m0"])
                    tt(m[1], bim, TA[:, 1, :], ALU.mult, [nim, "TA"], ["m1"])
                    tt(bup[b_][:, 0, :], m[0], m[1], ALU.subtract, ["m0", "m1"], ["bup%d" % b_], eng="pool")
                    tt(m[2], bre, TA[:, 1, :], ALU.mult, [nre, "TA"], ["m2"])
                    tt(m[3], bim, TA[:, 0, :], ALU.mult, [nim, "TA"], ["m3"])
                    tt(bup[b_][:, 1, :], m[2], m[3], ALU.add, ["m2", "m3"], ["bup%d" % b_], eng="pool")

                bu_mm(0)
                rotin(0)
                for c in range(NCH):
                    b_ = c % 2
                    for ri in range(2):
                        for jj in range(4):
                            mm(banks[4 + ri][:, jj * 128:(jj + 1) * 128], bup[b_][:, ri, jj * 128:(jj + 1) * 128], Ub[:, :],
                               True, True, ["bup%d" % b_, "Ub"], ["b%d" % (4 + ri)])
                    if c + 1 < NCH:
                        bu_mm(c + 1)
                        rotin(c + 1)
                    for ri in range(2):
                        if c == 0:
                            cp(Pp[ri], banks[4 + ri][:, :], ["b%d" % (4 + ri)], ["Pp%d" % ri], eng="act")
                        else:
                            tt(v3(Pp[ri]), v3(banks[4 + ri][:, :]), carry[:, ri, :].unsqueeze(2).to_broadcast([128, 4, 128]),
                               ALU.add, ["b%d" % (4 + ri), "carry"], ["Pp%d" % ri])
                    tt(m[4], Pp[0], Ft[:, 0, :], ALU.mult, ["Pp0", "Ft"], ["m4"])
                    tt(m[5], Pp[1], Ft[:, 1, :], ALU.mult, ["Pp1", "Ft"], ["m5"])
                    tt(m[6], Pp[0], Ft[:, 2, :], ALU.mult, ["Pp0", "Ft"], ["m6"], eng="pool")
                    tt(m[7], Pp[1], Ft[:, 0, :], ALU.mult, ["Pp1", "Ft"], ["m7"], eng="pool")
                    tt(xbf[:, 0, :], m[4], m[5], ALU.subtract, ["m4", "m5"], ["xbf0"])
                    tt(xbf[:, 1, :], m[6], m[7], ALU.subtract, ["m6", "m7"], ["xbf1"], eng="pool")
                    if c + 1 < NCH:
                        l4 = lambda ap: v3(ap)[:, :, 127]
                        tt(ct[:, 0, :], l4(m[4]), l4(m[5]), ALU.subtract, ["m4", "m5"], ["ct0"])
                        tt(ct[:, 1, :], l4(m[6]), l4(m[7]), ALU.subtract, ["m6", "m7"], ["ct1"])
                        tt(ct[:, 2, :], ct[:, 0, :], car, ALU.mult, ["ct0", "ca"], ["ct2"])
                        tt(ct[:, 3, :], ct[:, 1, :], cai, ALU.mult, ["ct1", "ca"], ["ct3"])
                        tt(carry[:, 0, :], ct[:, 2, :], ct[:, 3, :], ALU.add, ["ct2", "ct3"], ["carry"])
                        tt(ct[:, 4, :], ct[:, 0, :], cai, ALU.mult, ["ct0", "ca"], ["ct4"])
                        tt(ct[:, 5, :], ct[:, 1, :], car, ALU.mult, ["ct1", "ca"], ["ct5"])
                        tt(carry[:, 1, :], ct[:, 4, :], ct[:, 5, :], ALU.subtract, ["ct4", "ct5"], ["carry"])
                    ybn = 6 + (c // 4) % 2
                    yb = banks[ybn][:, (c % 4) * 128:(c % 4 + 1) * 128]
                    for jj in range(4):
                        mm(yb, Cp[:, 0, jj, :], xbf[:, 0, jj * 128:(jj + 1) * 128], jj == 0, False, ["Cp", "xbf0"], ["b%d" % ybn])
                        mm(yb, Cp[:, 1, jj, :], xbf[:, 1, jj * 128:(jj + 1) * 128], False, False, ["Cp", "xbf1"], ["b%d" % ybn])
                    mm(yb, diagD[:, k, :], uT[:, c * 128:(c + 1) * 128], False, True, ["diagD", "uT"], ["b%d" % ybn])
                    if c % 4 == 3:
                        tb_ = c // 4
                        cp(ysb, banks[ybn][:, :], ["b%d" % ybn], ["ysb"], eng="act")
                        tt(gt_, ysb, ysb, ALU.mult, ["ysb"], ["gt"])
                        ts(gt_, gt_, 0.044715, 1.0, ALU.mult, ALU.add, ["gt"], ["gt"])
                        tt(gt_, gt_, ysb, ALU.mult, ["gt", "ysb"], ["gt"])
                        act(gt_, gt_, AF.Sigmoid, ["gt"], ["gt"], scale=2.0 * math.sqrt(2.0 / math.pi))
                        tt(gT[:, k, tb_ * 512:(tb_ + 1) * 512], gt_, ysb, ALU.mult, ["gt", "ysb"], ["gT"])
                P.fence()
            AR.reset(keep)
            W = AR.alloc([128, 8, 512], BF16)
            wglu = AR.alloc([128, 4, D], BF16)
            wout = AR.alloc([128, 4, D], BF16)
            hn = [AR.alloc([128, 8, 128], BF16) for _ in range(2)]
            sz = AR.alloc([128, 512], F32)
            gab = AR.alloc([128, D], F32)
            bgl = AR.alloc([128, D], F32)
            nwr = AR.alloc([128, 512], F32)
            yv = AR.alloc([128, 512], F32)
            ybf = AR.alloc([128, 512], BF16)
            yT = AR.alloc([128, 4, 128], BF16)
            junk = AR.alloc([128, 512], BF16)
            rs = AR.alloc([128, 4], F32)
            AR_tmp[0] = AR.alloc([128, D], F32)
            load_W(W, l, SEG["s5_z"][0], 512, "W")
            dma("pool", wglu, w_glu_d[l].rearrange("(k p) n -> p k n", p=128), [], ["wglu"])
            dma("pool", wout, w_out_d[l, 1024:1536, :].rearrange("(k p) n -> p k n", p=128), [], ["wout"])
            rowload(bgl, l, "b_glu", wname="bgl")
            rowload(nwr, l, "s5_nw", wname="nwr")
            for c in range(NCH):
                s_ = c % 2
                hnm = "hn%d" % s_
                load_hn(hn[s_], c, hnm)
                proj_TM(banks[2][:, :], hn[s_], hnm, W, "W", 0, 512, "b2z")
                act(sz, banks[2][:, :], AF.Silu, ["b2z"], ["sz"])
                for hf in range(2):
                    for kk in range(4):
                        mm(banks[3 + hf][:, :], gT[:, kk, c * 128:(c + 1) * 128], wglu[:, kk, hf * 512:(hf + 1) * 512],
                           kk == 0, kk == 3, ["gT", "wglu"], ["b%d" % (3 + hf)])
                    tt(gab[:, hf * 512:(hf + 1) * 512], banks[3 + hf][:, :], bgl[:, hf * 512:(hf + 1) * 512], ALU.add,
                       ["b%d" % (3 + hf), "bgl"], ["gab%d" % hf])
                act(gab[:, 512:1024], gab[:, 512:1024], AF.Sigmoid, ["gab1"], ["gab1"])
                tt(yv, gab[:, 0:512], gab[:, 512:1024], ALU.mult, ["gab0", "gab1"], ["yv"])
                rstd, rn = rms_scale(yv, "yv", 512, rs, junk, "s5n")
                stt(yv, yv, rstd, nwr, ALU.mult, ALU.mult, ["yv", rn, "nwr"], ["yv"])
                tt(ybf, yv, sz, ALU.mult, ["yv", "sz"], ["ybf"])
                finish(l, c, ybf, "ybf", 4, wout, "wout", yT, "s5")
            P.fence()

    AR.reset()
    fnw = AR.alloc([128, D], F32)
    junk = AR.alloc([128, D], BF16)
    ob = [AR.alloc([128, D], F32) for _ in range(2)]
    rs = [AR.alloc([128, 4], F32) for _ in range(2)]
    dma("sp", fnw, fnw_d[0, :].partition_broadcast(128), [], ["fnw"])
    outnames = []
    for c in range(NCH):
        s_ = c % 2
        rstd, rn = rms_scale(x_sb[:, c, :], "x%d" % c, D, rs[s_], junk, "fn%d" % s_)
        stt(ob[s_], x_sb[:, c, :], rstd, fnw, ALU.mult, ALU.mult, ["x%d" % c, rn, "fnw"], ["ob%d" % s_])
        dma("sp", out_d[c * 128:(c + 1) * 128, :], ob[s_], ["ob%d" % s_], ["out%d" % c])
        outnames.append("out%d" % c)
    P.add("sp", lambda e: e.nop(), outnames, ["done"])
    n = P.finalize(es)
    return nc, es, n


def _host_layout(inp, nlw=NL):
    f = np.float32
    g = {k: np.asarray(v) for k, v in inp.items()}
    w_in = g["w_in"]
    offs = np.cumsum([0, 512, 1024, 8, 512, 512, 512, 512, 512, 4, 4, 512, 512, 512, 256, 256, 512])
    (s_z, s_xbc, s_dt, m_z, m_q, m_k, m_v, m_o, m_i, m_f, c_z, c_u, r_z, r_q, r_k, r_v) = [
        (int(offs[i]), int(offs[i + 1])) for i in range(16)]
    W = np.zeros((NL, D, NCOLS), f)

    def put(name, off, src_lo, n):
        o = SEG[name][0] + off
        W[:, :, o:o + n] = w_in[:, :, src_lo:src_lo + n]
    put("ssd_z", 0, s_z[0], 512)
    put("ssd_xbc", 0, s_xbc[0], 1024)
    for h in range(2):
        nm = "ml%d" % h
        for j, sg in enumerate((m_z, m_q, m_k, m_v, m_o)):
            put(nm, j * 256, sg[0] + h * 256, 256)
        nm = "ret%d" % h
        put(nm, 0, r_z[0] + h * 256, 256)
        for hh in range(2):
            put(nm, 256 + hh * 128, r_q[0] + (2 * h + hh) * 64, 64)
            put(nm, 512 + hh * 128, r_k[0] + (2 * h + hh) * 64, 64)
        put(nm, 768, r_v[0] + h * 256, 256)
    put("s5_z", 0, c_z[0], 512)
    put("s5_u", 0, c_u[0], 512)
    put("misc", 0, s_dt[0], 8)
    put("misc", 8, m_i[0], 4)
    put("misc", 12, m_f[0], 4)
    rows = np.zeros((NL, NROW), f)

    def prow(name, arr):
        o, w = ROW[name]
        rows[:, o:o + w] = arr.reshape(NL, w)
    prow("norm_w", g["norm_w"]); prow("b_ada", g["b_ada"]); prow("dt_bias", g["ssd_dt_bias"])
    prow("a_log", g["ssd_a_log"]); prow("ssd_d", g["ssd_d"]); prow("ssd_nw", g["ssd_norm_w"])
    prow("ml_ib", g["ml_i_bias"]); prow("ml_fb", g["ml_f_bias"]); prow("ml_nw", g["ml_norm_w"])
    prow("b_glu", g["s5_b_glu"]); prow("s5_nw", g["s5_norm_w"]); prow("ret_nw", g["ret_norm_w"])
    prow("lam_re", g["s5_lambda_re"]); prow("lam_im", g["s5_lambda_im"])
    prow("lstep", np.repeat(g["s5_log_step"], 64, axis=1))
    cs = np.concatenate([g["ssd_conv_w"], g["ssd_conv_b"][:, None, :]], axis=1)
    convs = cs.reshape(NL, 5, 8, 128).transpose(0, 3, 2, 1).reshape(NL, 128, 40)
    cm = np.concatenate([g["ml_conv_w"], g["ml_conv_b"][:, None, :]], axis=1)
    cm = cm.reshape(NL, 5, 8, 128)
    order = [0, 1, 4, 5, 2, 3, 6, 7]
    convm = cm[:, :, order, :].transpose(0, 3, 2, 1).reshape(NL, 128, 40)
    s5col = np.zeros((NL, 128, 52), f)
    s5col[:, :, 0:4] = g["s5_d"].reshape(NL, 4, 128).transpose(0, 2, 1)
    s5col[:, :, 4:20] = g["s5_lambda_re"].reshape(NL, 16, 128).transpose(0, 2, 1)
    s5col[:, :, 20:36] = g["s5_lambda_im"].reshape(NL, 16, 128).transpose(0, 2, 1)
    s5col[:, :, 36:52] = np.repeat(g["s5_log_step"], 64, axis=1).reshape(NL, 16, 128).transpose(0, 2, 1)
    s5b = np.zeros((NL, 2, 4, 128, 512), f)
    s5c = np.zeros((NL, 2, 16, 128, 128), f)
    for ri, (bsrc, csrc) in enumerate(((g["s5_b_re"], g["s5_c_re"]), (g["s5_b_im"], g["s5_c_im"]))):
        for gi in range(32):
            k, gl = gi // 8, gi % 8
            s5b[:, ri, k, gl * 16:(gl + 1) * 16, gl * 64:(gl + 1) * 64] = bsrc[:, gi].transpose(0, 2, 1)
            jt, g2 = gi // 2, gi % 2
            s5c[:, ri, jt, g2 * 64:(g2 + 1) * 64, gl * 16:(gl + 1) * 16] = csrc[:, gi].transpose(0, 2, 1)
    cst = np.zeros((128, 128 * 5 + 32 + 512 + 16), f)
    ii = np.arange(128)
    cst[:, 0:128] = np.eye(128)
    cst[:, 128:256] = (ii[:, None] <= ii[None, :])
    cst[:, 256:384] = 1.0
    cst[:, 384:512] = np.where(ii[:, None] <= ii[None, :], 0.0, -30000.0)
    cst[:, 512:640] = ii[None, :]
    cst[:, 1192] = ii
    cst[:, 1193] = -ii
    cst[:, 640:672] = np.exp(-math.log(10000.0) * np.arange(32, dtype=np.float64) / 32)[None, :]
    for h in range(4):
        lg = math.log1p(-2.0 ** -(5 + h))
        rel = ii[None, :] - ii[:, None]
        cst[:, 672 + h * 128:672 + (h + 1) * 128] = np.where(rel >= 0, np.exp(np.maximum(rel, 0) * lg), 0.0) * 64 ** -0.5
        cst[:, 1184 + h] = np.exp((ii + 1.0) * lg)
        cst[:, 1188 + h] = np.exp((127.0 - ii) * lg) * 64 ** -0.5
    shared = dict(w_in=W, w_out=np.ascontiguousarray(g["w_out"], f), w_ada=np.ascontiguousarray(g["w_ada"], f),
                  w_glu=np.ascontiguousarray(g["s5_w_glu"], f), rows=rows,
                  fnw=np.ascontiguousarray(g["final_norm_w"].reshape(1, D), f),
                  convs=np.ascontiguousarray(convs, f), convm=np.ascontiguousarray(convm, f), s5col=s5col,
                  s5b=s5b, s5c=s5c, cst=cst)
    if nlw < NL:
        shared = {k: (np.ascontiguousarray(v[:nlw]) if k not in ("fnw", "cst") else v) for k, v in shared.items()}
    maps = []
    for b in range(8):
        m = dict(shared)
        m["x"] = np.ascontiguousarray(g["x"][b], f)
        m["cT"] = np.ascontiguousarray(g["c"][b].reshape(8, 128).T, f)
        m["posT"] = np.ascontiguousarray(g["positions"][b].reshape(NCH, 128).T.astype(np.int32))
        maps.append(m)
    return maps


_CACHE = {}


def kernel(_nl=NL, _mix=(0, 1, 2, 3), **inputs):
    key = (_nl, tuple(_mix))
    if key not in _CACHE:
        nc, es, n = build(_nl, _mix)
        _CACHE[key] = nc
    nc = _CACHE[key]
    maps = _host_layout(inputs, max(_nl, 1))
    res = run_bass_kernel_spmd(nc, maps, core_ids=list(range(8)))
    out = np.stack([np.asarray(r["out"]).reshape(S, D) for r in res.results], axis=0)
    return out.astype(np.float32)
```
